# Optimizing a Trainium2 kernel written in Bass

```python
import math
import jax, jax.numpy as jnp
from jax import lax
import numpy as np

D_MODEL = 1024
BATCH = 4
SEQ = 8192
DEPTH = 1

CHUNK = 64
Q_BLOCK = 128
CONV_WIDTH = 3
D_CONV = D_MODEL // 2
N_HEADS_DIFF = 4
DH_DIFF = 64
DV_DIFF = 2 * DH_DIFF
D_ATTN = N_HEADS_DIFF * DV_DIFF
D_FF = ((8 * D_MODEL // 3 + 255) // 256) * 256
EPS = 1e-6
NEG_INF = -1e30
COLS_CONV = 3 * D_CONV
COLS_ATTN = 3 * D_ATTN
COLS_GATES = 2 * D_MODEL
D_IN_TOTAL = COLS_CONV + COLS_ATTN + COLS_GATES

kernel_name = "hybrid_shortconv_diffattn_gated_block"


def rms_norm(x, g):
    xf = x.astype(jnp.float32)
    y = xf * lax.rsqrt(jnp.mean(xf * xf, axis=-1, keepdims=True) + EPS)
    return y.astype(x.dtype) * g


def alibi_slopes(n_heads):
    return jnp.asarray([2.0 ** (-8.0 * (i + 1) / n_heads) for i in range(n_heads)], dtype=jnp.float32)


def lambda_init_for(layer_idx):
    return 0.8 - 0.6 * math.exp(-0.3 * layer_idx)


def causal_depthwise_conv(u, w):
    S = u.shape[1]
    k = w.shape[0]
    up = jnp.pad(u, ((0, 0), (k - 1, 0), (0, 0)))
    out = up[:, 0:S] * w[0]
    for i in range(1, k):
        out = out + up[:, i:i + S] * w[i]
    return out


def short_conv_branch(u, gb, gc, conv_w, w_out_a):
    z = causal_depthwise_conv(gc * u, conv_w)
    return (gb * z) @ w_out_a


def diff_attention_branch(q, k, v, lam, subln_g, lambda_init, w_out_b):
    Bsz, S = q.shape[0], q.shape[1]
    nblk = S // Q_BLOCK
    scale = 1.0 / math.sqrt(DH_DIFF)
    slopes = alibi_slopes(N_HEADS_DIFF)
    kpos = jnp.arange(S, dtype=jnp.int32)
    kchunk = kpos // CHUNK
    q_blocks = jnp.moveaxis(q.reshape(Bsz, nblk, Q_BLOCK, N_HEADS_DIFF, 2, DH_DIFF), 1, 0)
    qpos_blocks = kpos.reshape(nblk, Q_BLOCK)

    def one_block(args):
        qb, qpos = args
        s = jnp.einsum('bqhcd,bkhcd->bhcqk', qb.astype(jnp.float32), k.astype(jnp.float32)) * scale
        dist = jnp.abs(qpos[:, None] - kpos[None, :]).astype(jnp.float32)
        bias = -slopes[:, None, None] * dist[None]
        allowed = kchunk[None, :] <= (qpos // CHUNK)[:, None]
        s = jnp.where(allowed[None, None, None], s + bias[None, :, None], NEG_INF)
        p = jax.nn.softmax(s, axis=-1)
        a = (p[:, :, 0] - lam * p[:, :, 1]).astype(v.dtype)
        return jnp.einsum('bhqk,bkhe->bqhe', a, v)

    o = lax.map(one_block, (q_blocks, qpos_blocks))
    o = jnp.moveaxis(o, 0, 1).reshape(Bsz, S, N_HEADS_DIFF, DV_DIFF)
    o = rms_norm(o, subln_g) * (1.0 - lambda_init)
    return o.reshape(Bsz, S, D_ATTN) @ w_out_b


def swiglu(h, w_gate, w_up, w_down):
    return (jax.nn.silu(h @ w_gate) * (h @ w_up)) @ w_down


def setup_inputs(seed: int = 0) -> dict:
    key = jax.random.key(seed)
    ks = jax.random.split(key, 24)
    n = jax.random.normal
    L, D = DEPTH, D_MODEL
    return {
        "x": n(ks[0], (BATCH, SEQ, D), jnp.float32),
        "c": n(ks[1], (BATCH, D), jnp.float32),
        "w_ada": n(ks[2], (L, D, 6 * D), jnp.float32) * (0.2 * D ** -0.5),
        "b_ada": n(ks[3], (L, 6 * D), jnp.float32) * 0.02,
        "g_mix": 1.0 + 0.05 * n(ks[4], (L, D), jnp.float32),
        "w_in": n(ks[5], (L, D, D_IN_TOTAL), jnp.float32) * D ** -0.5,
        "conv_w": n(ks[6], (L, CONV_WIDTH, D_CONV), jnp.float32) * CONV_WIDTH ** -0.5,
        "w_out_a": n(ks[7], (L, D_CONV, D), jnp.float32) * D_CONV ** -0.5,
        "lambda_q1": n(ks[8], (L, DH_DIFF), jnp.float32) * 0.1,
        "lambda_k1": n(ks[9], (L, DH_DIFF), jnp.float32) * 0.1,
        "lambda_q2": n(ks[10], (L, DH_DIFF), jnp.float32) * 0.1,
        "lambda_k2": n(ks[11], (L, DH_DIFF), jnp.float32) * 0.1,
        "subln_g": 1.0 + 0.05 * n(ks[12], (L, DV_DIFF), jnp.float32),
        "w_out_b": n(ks[13], (L, D_ATTN, D), jnp.float32) * D_ATTN ** -0.5,
        "w_out": n(ks[14], (L, D, D), jnp.float32) * D ** -0.5,
        "g_ffn": 1.0 + 0.05 * n(ks[15], (L, D), jnp.float32),
        "w_gate": n(ks[16], (L, D, D_FF), jnp.float32) * D ** -0.5,
        "w_up": n(ks[17], (L, D, D_FF), jnp.float32) * D ** -0.5,
        "w_down": n(ks[18], (L, D_FF, D), jnp.float32) * D_FF ** -0.5,
        "g_final": 1.0 + 0.05 * n(ks[19], (D,), jnp.float32),
    }


def reference(x, c, w_ada, b_ada, g_mix, w_in, conv_w, w_out_a, lambda_q1, lambda_k1,
              lambda_q2, lambda_k2, subln_g, w_out_b, w_out, g_ffn, w_gate, w_up, w_down,
              g_final):
    Bsz, S, D = x.shape
    for l in range(DEPTH):
        lambda_init = lambda_init_for(l + 1)
        mod = jax.nn.silu(c) @ w_ada[l] + b_ada[l]
        sh_m, sc_m, gt_m, sh_f, sc_f, gt_f = [m[:, None, :] for m in jnp.split(mod, 6, axis=-1)]

        h = rms_norm(x, g_mix[l]) * (1.0 + sc_m) + sh_m
        proj = h @ w_in[l]
        p_conv = proj[..., :COLS_CONV]
        p_attn = proj[..., COLS_CONV:COLS_CONV + COLS_ATTN]
        p_gate = proj[..., COLS_CONV + COLS_ATTN:]

        u, gb, gc = jnp.split(p_conv, 3, axis=-1)
        y_a = short_conv_branch(u, gb, gc, conv_w[l], w_out_a[l])

        q, k, v = jnp.split(p_attn, 3, axis=-1)
        q = q.reshape(Bsz, S, N_HEADS_DIFF, 2, DH_DIFF)
        k = k.reshape(Bsz, S, N_HEADS_DIFF, 2, DH_DIFF)
        v = v.reshape(Bsz, S, N_HEADS_DIFF, DV_DIFF)
        lam = (jnp.exp(jnp.sum(lambda_q1[l].astype(jnp.float32) * lambda_k1[l].astype(jnp.float32)))
               - jnp.exp(jnp.sum(lambda_q2[l].astype(jnp.float32) * lambda_k2[l].astype(jnp.float32)))
               + lambda_init)
        y_b = diff_attention_branch(q, k, v, lam, subln_g[l], lambda_init, w_out_b[l])

        g_a, g_b = jnp.split(p_gate, 2, axis=-1)
        merged = jax.nn.sigmoid(g_a) * y_a + jax.nn.sigmoid(g_b) * y_b
        x = x + gt_m * (merged @ w_out[l])

        h2 = rms_norm(x, g_ffn[l]) * (1.0 + sc_f) + sh_f
        x = x + gt_f * swiglu(h2, w_gate[l], w_up[l], w_down[l])
    return rms_norm(x, g_final)
```

```python
import math
from contextlib import ExitStack

import numpy as np
import ml_dtypes

import concourse.bass as bass
import concourse.mybir as mybir
from concourse.bass_utils import run_bass_kernel_spmd

F32 = mybir.dt.float32
BF16 = mybir.dt.bfloat16
I32 = mybir.dt.int32
AF = mybir.ActivationFunctionType
ALU = mybir.AluOpType

D = 1024
S = 8192
NB = 4
DFF = 2816
EPS = 1e-6
LAMBDA_INIT = 0.8 - 0.6 * math.exp(-0.3 * 1)
SLOPES = [2.0 ** (-8.0 * (i + 1) / 4) for i in range(4)]
NSLOT = 8
TQ = 512
MASKV = -float(2 ** 20)


class DSem:
    def __init__(self, sem):
        self.sem = sem
        self.count = 0


class Op:
    __slots__ = ("eng", "fn", "deps", "sig", "sigidx", "dsem", "dcount")

    def __init__(self, eng, fn):
        self.eng = eng
        self.fn = fn
        self.deps = []
        self.sig = False
        self.sigidx = 0
        self.dsem = None
        self.dcount = 0


class Buf:
    def __init__(self, name, t):
        self.name = name
        self.t = t
        self.w = {}
        self.r = {}

    def __getitem__(self, k):
        return self.t[k]


class Prog:
    ENGS = ("pe", "act", "dve", "pool", "sp")
    SYNC_SELF = ("act", "dve", "pool")

    def __init__(self, nc, es):
        self.nc = nc
        self.es = es
        self.q = {e: [] for e in self.ENGS}
        self.esem = {e: es.enter_context(nc.semaphore("es_" + e)) for e in self.ENGS}
        self.bar = {}
        self.nbuf = 0

    def sb(self, name, shape, dtype, es=None):
        t = (es or self.es).enter_context(self.nc.sbuf_tensor("s_" + name, list(shape), dtype))
        return Buf(name, t)

    def ps(self, name, shape, dtype=F32, es=None):
        t = (es or self.es).enter_context(self.nc.psum_tensor("p_" + name, list(shape), dtype))
        return Buf(name, t)

    def dsem(self, name, es=None):
        return DSem((es or self.es).enter_context(self.nc.semaphore(name)))

    def dram(self, name, t):
        return Buf(name, t)

    def add(self, eng, fn, reads=(), writes=(), dsem=None, exclude=()):
        op = Op(eng, fn)
        self.q[eng].append(op)
        if dsem is not None:
            dsem.count += 16
            op.dsem = dsem
            op.dcount = dsem.count
            key = ("d", id(dsem))
        else:
            key = eng
        deps = {}
        for b in reads:
            for d in b.w.values():
                deps[id(d)] = d
        for b in writes:
            for d in b.r.values():
                deps[id(d)] = d
            for d in b.w.values():
                deps[id(d)] = d
        if eng in self.bar:
            for d in self.bar.pop(eng):
                deps[id(d)] = d
        for x_ in exclude:
            deps.pop(id(x_), None)
        op.deps = list(deps.values())
        for b in reads:
            b.r[key] = op
        for b in writes:
            if b.r:
                b.w = {key: op}
                b.r = {}
            else:
                b.w[key] = op
        return op

    def barrier(self, extra=()):
        last = []
        for e in self.ENGS:
            for op in reversed(self.q[e]):
                if op.dsem is None:
                    last.append(op)
                    break
        seen = {}
        for e in self.ENGS:
            for op in self.q[e]:
                if op.dsem is not None:
                    seen[id(op.dsem)] = op
        last += list(seen.values())
        for e in self.ENGS:
            self.bar[e] = list(last)

    def emit(self, block):
        for e in self.ENGS:
            for op in self.q[e]:
                for d in op.deps:
                    if d.dsem is None and (d.eng != op.eng or op.eng in self.SYNC_SELF):
                        d.sig = True
        for e in self.ENGS:
            n = 0
            for op in self.q[e]:
                if op.sig:
                    n += 1
                    op.sigidx = n

        def runner(ename):
            def f(eng):
                waited = {}
                for op in self.q[ename]:
                    need = {}
                    for d in op.deps:
                        if d.dsem is not None:
                            k, v, s = ("d", id(d.dsem)), d.dcount, d.dsem.sem
                        elif d.eng == ename and ename not in self.SYNC_SELF:
                            continue
                        else:
                            k, v, s = d.eng, d.sigidx, self.esem[d.eng]
                        if waited.get(k, 0) < v and need.get(k, (0, None))[0] < v:
                            need[k] = (v, s)
                    for k, (v, s) in need.items():
                        eng.wait_ge(s, v)
                        waited[k] = v
                    if op.fn is None:
                        continue
                    inst = op.fn(eng)
                    if op.dsem is not None:
                        inst.then_inc(op.dsem.sem, 16)
                    elif op.sig:
                        inst.then_inc(self.esem[ename], 1)
            return f

        block.tensor(runner("pe"))
        block.scalar(runner("act"))
        block.vector(runner("dve"))
        block.gpsimd(runner("pool"))
        block.sync(runner("sp"))


def _build(debug=None):
    nc = bass.Bass("TRN2", target_bir_lowering=False)
    es = ExitStack()
    P = Prog(nc, es)

    def din(name, shape, dt=F32):
        return nc.dram_tensor(name, list(shape), dt, kind="ExternalInput")

    xs_d = din("xs", [S, D])
    xo_d = din("xo", [NSLOT * TQ, D])
    xh_d = din("xh", [NSLOT, 128, D])
    hm_d = din("hm", [128, NSLOT])
    cc_d = din("ccol", [128, 8])
    wada_d = din("w_ada", [D, 6 * D])
    bcol_d = din("bcol", [128, 48])
    brow_d = din("brow", [2, D])
    gmix_d = din("gmix", [128, 8])
    gffn_d = din("gffn", [128, 8])
    gfin_d = din("gfin", [1, D])
    win_d = din("w_in", [D, 5120])
    cw_d = din("cw", [128, 12])
    woa_d = din("w_out_a", [512, D])
    wob_d = din("w_out_b", [512, D])
    wo_d = din("w_out", [D, D])
    wg_d = din("w_gate", [D, DFF])
    wu_d = din("w_up", [D, DFF])
    wd_d = din("w_down", [DFF, D])
    lam_d = din("lam", [4, 64])
    subg_d = din("subg", [128, 1])
    ident_d = din("ident", [128, 128], BF16)
    diag_d = din("diag", [4, 128, 128], BF16)
    tmask_d = din("tmask", [8, 128, 512], BF16)
    NCB = sum(16 * (2 * j + 2) for j in range(NSLOT))
    cb_d = din("cbias", [128, NCB])
    y_d = nc.dram_tensor("y", [NSLOT * TQ, D], F32, kind="ExternalOutput")

    wb_in_d = nc.dram_tensor("wb_in", [D, 5120], BF16)
    wb_oa_d = nc.dram_tensor("wb_oa", [512, D], BF16)
    wb_ob_d = nc.dram_tensor("wb_ob", [512, D], BF16)
    wb_o_d = nc.dram_tensor("wb_o", [D, D], BF16)
    wb_g_d = nc.dram_tensor("wb_g", [D, DFF], BF16)
    wb_u_d = nc.dram_tensor("wb_u", [D, DFF], BF16)
    wb_d_d = nc.dram_tensor("wb_d", [DFF, D], BF16)

    dbg = {}

    def dout(name, shape, dt=F32):
        t = nc.dram_tensor(name, list(shape), dt, kind="ExternalOutput")
        dbg[name] = t
        return t

    def dma(q, out, in_, reads=(), writes=(), dsem=None, exclude=(), **kw):
        return P.add(q, lambda e: e.dma_start(out=out, in_=in_, **kw), reads, writes, dsem, exclude)

    def rsqrt(eng, out_b, out_ap, x_b, x_ap, tmp_b, tmp_ap):
        P.add(eng, lambda e: e.tensor_scalar(out=out_ap.bitcast(I32), in0=x_ap.bitcast(I32),
                                             scalar1=-0.5, scalar2=1597463007.0,
                                             op0=ALU.mult, op1=ALU.add),
              [x_b], [out_b])
        for _ in range(3):
            P.add(eng, lambda e: e.tensor_tensor(out=tmp_ap, in0=out_ap, in1=out_ap, op=ALU.mult),
                  [out_b], [tmp_b])
            P.add(eng, lambda e: e.tensor_tensor(out=tmp_ap, in0=tmp_ap, in1=x_ap, op=ALU.mult),
                  [tmp_b, x_b], [tmp_b])
            P.add(eng, lambda e: e.tensor_scalar(out=tmp_ap, in0=tmp_ap, scalar1=-0.5, scalar2=1.5,
                                                 op0=ALU.mult, op1=ALU.add),
                  [tmp_b], [tmp_b])
            P.add(eng, lambda e: e.tensor_tensor(out=out_ap, in0=out_ap, in1=tmp_ap, op=ALU.mult),
                  [out_b, tmp_b], [out_b])

    ident = P.sb("ident", [128, 128], BF16)
    ones_bf = P.sb("ones_bf", [128, 128], BF16)
    ones_f = P.sb("ones_f", [128, 128], F32)
    ccol = P.sb("ccol", [128, 8], F32)
    scol = P.sb("scol", [128, 8], F32)
    bcol = P.sb("bcol", [128, 48], F32)
    gmix = P.sb("gmix", [128, 8], F32)
    gffn = P.sb("gffn", [128, 8], F32)
    modc = P.sb("modc", [128, 32], F32)
    A1 = P.sb("A1", [128, 8], F32)
    A2 = P.sb("A2", [128, 8], F32)
    gtm = P.sb("gtm", [128, D], F32)
    gtf = P.sb("gtf", [128, D], F32)
    gfin = P.sb("gfin", [128, D], F32)
    cw = P.sb("cw", [128, 12], F32)
    hm = P.sb("hm", [128, NSLOT], F32)
    subg = P.sb("subg", [128, 1], F32)
    nlam = P.sb("nlam", [128, 1], F32)
    cbias = P.sb("cbias", [128, NCB], F32)

    cs = P.dsem("cs")
    cops = []
    for (b, d_ap) in ((ident, ident_d.ap()), (ccol, cc_d.ap()), (bcol, bcol_d.ap()),
                      (gmix, gmix_d.ap()), (gffn, gffn_d.ap()), (cw, cw_d.ap()),
                      (hm, hm_d.ap()), (subg, subg_d.ap()), (cbias, cb_d.ap())):
        cops.append(dma("sp", b[:], d_ap, writes=[b], dsem=cs))
    cops.append(dma("sp", gfin[:], gfin_d.ap().partition_broadcast(128), writes=[gfin], dsem=cs))
    cops.append(dma("sp", gtm[:], brow_d.ap()[0:1, :].partition_broadcast(128), writes=[gtm], dsem=cs))
    cops.append(dma("sp", gtf[:], brow_d.ap()[1:2, :].partition_broadcast(128), writes=[gtf], dsem=cs))
    lamt = P.sb("lamt", [128, 256], F32)
    cops.append(dma("sp", lamt[:], lam_d.ap().rearrange("(o a) b -> o (a b)", o=1).partition_broadcast(128),
                    writes=[lamt], dsem=cs))
    for o_ in cops:
        o_.dcount = cs.count

    wcs = P.dsem("wcs")
    wbufs = {}
    wops = []
    for (src, dst, pat, kw) in (
            (win_d, wb_in_d, "k (a n) -> (k a) n", dict(n=1024)),
            (woa_d, wb_oa_d, None, None), (wob_d, wb_ob_d, None, None), (wo_d, wb_o_d, None, None),
            (wg_d, wb_g_d, "k (a n) -> (k a) n", dict(n=1408)),
            (wu_d, wb_u_d, "k (a n) -> (k a) n", dict(n=1408)),
            (wd_d, wb_d_d, None, None)):
        sa, da = src.ap(), dst.ap()
        if pat is not None:
            sa, da = sa.rearrange(pat, **kw), da.rearrange(pat, **kw)
        wbufs[dst.name] = Buf("wscr_" + dst.name, None)
        wops.append(dma("pool", da, sa, writes=[wbufs[dst.name]], dsem=wcs))
    for o_ in wops:
        o_.dcount = wcs.count

    P.add("pool", lambda e: e.memset(ones_bf[:], 1.0), [], [ones_bf])
    P.add("pool", lambda e: e.memset(ones_f[:], 1.0), [], [ones_f])

    with ExitStack() as es0:
        srep = P.sb("srep", [128, 8, 128], F32, es0)
        lamp = P.sb("lamp", [128, 128], F32, es0)
        lams = P.sb("lams", [128, 2], F32, es0)
        wblk = [P.sb("wblk%d" % i, [128, 8, 512], F32, es0) for i in range(3)]
        wsem = [P.dsem("wsem%d" % i, es0) for i in range(3)]
        pcol = P.sb("pcol", [128, 32], F32, es0)
        ctmp = P.sb("ctmp", [128, 128], F32, es0)
        identf = P.sb("identf", [128, 128], F32, es0)
        prow = [P.ps("prow%d" % i, [128, 512], F32, es0) for i in range(2)]


        P.add("act", lambda e: e.activation(out=scol[:], in_=ccol[:], func=AF.Tanh, scale=0.5),
              [ccol], [scol])
        P.add("dve", lambda e: e.tensor_scalar(out=scol[:], in0=scol[:], scalar1=1.0, scalar2=0.5,
                                               op0=ALU.add, op1=ALU.mult), [scol], [scol])
        P.add("dve", lambda e: e.tensor_tensor(out=scol[:], in0=scol[:], in1=ccol[:], op=ALU.mult),
              [scol, ccol], [scol])
        for kc in range(8):
            P.add("dve", lambda e, kc=kc: e.tensor_copy(
                out=srep[:, kc, :], in_=scol[:, kc:kc + 1].to_broadcast([128, 128])),
                [scol], [srep])

        P.add("dve", lambda e: e.tensor_tensor(out=lamp[:, 0:64], in0=lamt[:, 0:64], in1=lamt[:, 64:128],
                                               op=ALU.mult), [lamt], [lamp])
        P.add("dve", lambda e: e.tensor_tensor(out=lamp[:, 64:128], in0=lamt[:, 128:192],
                                               in1=lamt[:, 192:256], op=ALU.mult), [lamt], [lamp])
        P.add("dve", lambda e: e.tensor_reduce(out=lams[:], in_=lamp[:].rearrange("p (a b) -> p a b", a=2),
                                               axis=mybir.AxisListType.X, op=ALU.add), [lamp], [lams])
        P.add("act", lambda e: e.activation(out=lams[:], in_=lams[:], func=AF.Exp), [lams], [lams])
        P.add("dve", lambda e: e.tensor_tensor(out=nlam[:], in0=lams[:, 1:2], in1=lams[:, 0:1],
                                               op=ALU.subtract), [lams], [nlam])
        P.add("dve", lambda e: e.tensor_scalar(out=nlam[:], in0=nlam[:], scalar1=-LAMBDA_INIT,
                                               scalar2=None, op0=ALU.add), [nlam], [nlam])
        P.add("dve", lambda e: e.tensor_scalar(out=subg[:], in0=subg[:], scalar1=1.0 - LAMBDA_INIT,
                                               scalar2=None, op0=ALU.mult), [subg], [subg])

        wv = wada_d.ap().rearrange("(kc p) n -> p kc n", p=128)
        colmap = {0: 0, 1: 0, 2: 1, 3: 1, 6: 2, 7: 2, 8: 3, 9: 3}
        rowmap = {4: (gtm, 0), 5: (gtm, 1), 10: (gtf, 0), 11: (gtf, 1)}
        order = [0, 1, 2, 3, 6, 7, 8, 9, 4, 5, 10, 11]
        P.add("dve", lambda e: e.tensor_copy(out=identf[:], in_=ident[:]), [ident], [identf])
        for i, blk in enumerate(order):
            wb = wblk[i % 3]
            dma("sp", wb[:], wv[:, :, blk * 512:(blk + 1) * 512], writes=[wb], dsem=wsem[i % 3])
            pr = prow[i % 2]
            for kc in range(8):
                P.add("pe", lambda e, wb=wb, kc=kc, pr=pr: e.matmul(
                    pr[:], lhsT=srep[:, kc, :], rhs=wb[:, kc, :], start=(kc == 0), stop=(kc == 7)),
                    [wb, srep], [pr])
            if blk in colmap:
                v = colmap[blk]
                for dc4 in range(4):
                    col = v * 8 + (blk % 2) * 4 + dc4
                    P.add("dve", lambda e, pr=pr, dc4=dc4: e.tensor_tensor(
                        out=ctmp[:], in0=pr[:, dc4 * 128:(dc4 + 1) * 128], in1=identf[:], op=ALU.mult),
                        [pr, identf], [ctmp])
                    P.add("dve", lambda e, col=col: e.tensor_reduce(
                        out=pcol[:, col:col + 1], in_=ctmp[:], axis=mybir.AxisListType.X, op=ALU.add),
                        [ctmp], [pcol])
            else:
                tgt, half = rowmap[blk]
                P.add("dve", lambda e, tgt=tgt, half=half, pr=pr: e.tensor_tensor(
                    out=tgt[:, half * 512:(half + 1) * 512], in0=pr[:],
                    in1=tgt[:, half * 512:(half + 1) * 512], op=ALU.add), [pr, tgt], [tgt])
                P.add("dve", lambda e, tgt=tgt, half=half: e.tensor_scalar(
                    out=tgt[:, half * 512:(half + 1) * 512], in0=tgt[:, half * 512:(half + 1) * 512],
                    scalar1=0.5, scalar2=None, op0=ALU.mult), [tgt], [tgt])
            if i == 7:
                P.add("dve", lambda e: e.tensor_tensor(out=modc[:, 0:16], in0=pcol[:, 0:16],
                                                       in1=bcol[:, 0:16], op=ALU.add),
                      [pcol, bcol], [modc])
                P.add("dve", lambda e: e.tensor_tensor(out=modc[:, 16:32], in0=pcol[:, 16:32],
                                                       in1=bcol[:, 24:40], op=ALU.add),
                      [pcol, bcol], [modc])
                P.add("dve", lambda e: e.scalar_tensor_tensor(out=A1[:], in0=modc[:, 8:16], scalar=1.0,
                                                              in1=gmix[:], op0=ALU.add, op1=ALU.mult),
                      [modc, gmix], [A1])
                P.add("dve", lambda e: e.scalar_tensor_tensor(out=A2[:], in0=modc[:, 24:32], scalar=1.0,
                                                              in1=gffn[:], op0=ALU.add, op1=ALU.mult),
                      [modc, gffn], [A2])

    P.barrier()

    if debug == "p0":
        o1 = dout("d_modc", [128, 32])
        o2 = dout("d_gtm", [128, D])
        o3 = dout("d_A1", [128, 8])
        o4 = dout("d_nlam", [128, 1])
        ds = P.dsem("dbgs")
        fin = []
        for (o, b) in ((o1, modc), (o2, gtm), (o3, A1), (o4, nlam)):
            fin.append(dma("sp", o.ap(), b[:], reads=[b], dsem=ds))
        op = P.add("sp", None)
        op.deps = fin
        with nc.Block() as block:
            P.emit(block)
        es.close()
        return nc, dbg

    on_t = es.enter_context(nc.sbuf_tensor("s_on", [128, NSLOT, 4, 512], BF16))
    on = [Buf("on%d" % j, on_t) for j in range(NSLOT)]
    selA = P.sb("selA", [64, 128], F32)
    selB = P.sb("selB", [64, 128], F32)
    P.add("pool", lambda e: e.memset(selA[0:32, :], 1.0 / 32), [], [selA])
    P.add("pool", lambda e: e.memset(selA[32:64, :], 0.0), [], [selA])
    P.add("pool", lambda e: e.memset(selB[0:32, :], 0.0), [], [selB])
    P.add("pool", lambda e: e.memset(selB[32:64, :], 1.0 / 32), [], [selB])

    def norm_tile(xt, rot, Acol, Bcol, psT_b, psT_ap, hdst_b, hdst_fn, st, sqj, xsb, ncols=128,
                  bcol_buf=None):
        ssq, sx, sy, stt = st
        P.add("act", lambda e: e.activation(out=sqj[:], in_=xt[:], func=AF.Square, accum_out=ssq[:]),
              [xt], [sqj, ssq])
        P.add("dve", lambda e: e.tensor_scalar(out=sx[:], in0=ssq[:], scalar1=1.0 / D, scalar2=EPS,
                                               op0=ALU.mult, op1=ALU.add), [ssq], [sx])
        rsqrt("dve", sy, sy[:], sx, sx[:], stt, stt[:])
        P.add("pool", lambda e: e.tensor_scalar(out=xsb[:], in0=xt[:], scalar1=sy[:, 0:1], scalar2=None,
                                                op0=ALU.mult), [xt, sy], [xsb])
        for kc in range(8):
            P.add("pe", lambda e, kc=kc: e.transpose(out=psT_ap[:, kc * 128:(kc + 1) * 128],
                                                     in_=xsb[:, kc * 128:(kc + 1) * 128],
                                                     identity=ident[:]), [xsb, ident], psT_b)
        bdep = bcol_buf if bcol_buf is not None else Bcol
        for kc in range(8):
            P.add("dve", lambda e, kc=kc: e.tensor_scalar(
                out=hdst_fn(kc), in0=psT_ap[:, kc * 128:kc * 128 + ncols],
                scalar1=Acol[:, kc:kc + 1], scalar2=Bcol[:, kc:kc + 1], op0=ALU.mult, op1=ALU.add),
                list(psT_b) + [Acol, bdep], [hdst_b])

    esA = ExitStack()
    KT = [P.sb("KT%d" % i, [128, S], BF16, esA) for i in range(2)]
    V = P.sb("V", [128, S // 128, 256], BF16, esA)
    wk = P.sb("wk", [128, 8, 256], BF16, esA)
    wvv = P.sb("wv", [128, 8, 256], BF16, esA)
    wq = P.sb("wq", [128, 8, 256], BF16, esA)
    wsemA = [P.dsem("wsA%d" % i, esA) for i in range(3)]
    xt = [P.sb("xt%d" % i, [128, D], F32, esA) for i in range(3)]
    xsem = [P.dsem("xsem%d" % i, esA) for i in range(3)]
    sqj = P.sb("sqj", [128, D], BF16, esA)
    xsb = [P.sb("xsb%d" % i, [128, D], BF16, esA) for i in range(2)]
    hT_t = [esA.enter_context(nc.sbuf_tensor("s_hT%d" % i, [128, 8, 512], BF16)) for i in range(2)]
    hT = [[Buf("hT%d_%d" % (i, r), hT_t[i]) for r in range(4)] for i in range(2)]
    stats = [tuple(P.sb("st%d_%d" % (i, k), [128, 1], F32, esA) for k in range(4)) for i in range(4)]
    QT = [P.sb("QT%d" % i, [128, 2, 512], BF16, esA) for i in range(2)]
    Qpad = [P.sb("Qpad%d" % c, [128, 2, 512], BF16, esA) for c in range(2)]
    for c in range(2):
        P.add("pool", lambda e, c=c: e.memset(Qpad[c][:], 0.0), [], [Qpad[c]])
    Pp = [P.sb("Pp%d" % i, [128, 1024], BF16, esA) for i in range(3)]
    tmask = P.sb("tmask", [128, 8, 512], BF16, esA)
    diag = P.sb("diag", [128, 4, 128], BF16, esA)
    O1s = P.sb("O1s", [128, 512], F32, esA)
    O2s = P.sb("O2s", [128, 512], F32, esA)
    dens = P.sb("dens", [64, 512], F32, esA)
    rden = P.sb("rden", [64, 512], F32, esA)
    t1 = P.sb("t1", [128, 512], F32, esA)
    rx = P.sb("rx", [128, 512], F32, esA)
    ry = P.sb("ry", [128, 512], F32, esA)
    rt = P.sb("rt", [128, 512], F32, esA)
    pp_t = [esA.enter_context(nc.psum_tensor("p_pp%d" % i, [128, 1024], F32)) for i in range(4)]
    bk = [[Buf("bk%d_%d" % (i, h), pp_t[i]) for h in range(2)] for i in range(4)]

    tsem = P.dsem("tsem", esA)
    o_a = dma("sp", tmask[:], tmask_d.ap().rearrange("t p q -> p t q"), writes=[tmask], dsem=tsem)
    o_b = dma("sp", diag[:], diag_d.ap().rearrange("h p c -> p h c"), writes=[diag], dsem=tsem)
    o_a.dcount = tsem.count
    o_b.dcount = tsem.count

    win_v = win_d.ap().rearrange("(kc p) n -> p kc n", p=128)
    xs_v = xs_d.ap()
    xo_v = xo_d.ap()
    nkts = [4 * (2 * j + 2) for j in range(NSLOT)]
    cb_base = [sum(4 * n for n in nkts[:j]) for j in range(NSLOT)]

    for hg in range(2):
        for (wb_, c0, si) in ((wk, 2048 + hg * 256, 0), (wvv, 2560 + hg * 256, 1), (wq, 1536 + hg * 256, 2)):
            dma("pool", wb_[:], win_v[:, :, c0:c0 + 256], writes=[wb_], dsem=wsemA[si])

        for i in range(S // 128):
            g, r = i // 4, i % 4
            x_b = xt[i % 3]
            dma("sp", x_b[:], xs_v[i * 128:(i + 1) * 128, :], writes=[x_b], dsem=xsem[i % 3])
            hb = hT[g % 2]
            ht = hT_t[g % 2]
            psT_b = bk[i % 2]
            psT_ap = pp_t[i % 2][:].bitcast(BF16)
            norm_tile(x_b, i, A1, modc, psT_b, psT_ap, hb[r],
                      lambda kc, ht=ht, r=r: ht[:, kc, r * 128:(r + 1) * 128],
                      stats[i % 4], sqj, xsb[i % 2])
            pv_b = bk[3][i % 2]
            pv_ap = pp_t[3][:, (i % 2) * 512:(i % 2) * 512 + 256]
            for kc in range(8):
                P.add("pe", lambda e, kc=kc, ht=ht, r=r, pv_ap=pv_ap: e.matmul(
                    pv_ap, lhsT=ht[:, kc, r * 128:(r + 1) * 128], rhs=wvv[:, kc, :],
                    start=(kc == 0), stop=(kc == 7)), [hb[r], wvv], [pv_b])
            P.add("act", lambda e, i=i, pv_ap=pv_ap: e.activation(out=V[:, i, :], in_=pv_ap, func=AF.Copy),
                  [pv_b], [V])
            if r == 3:
                for hl in range(2):
                    pk_b = bk[2][hl]
                    pk_ap = pp_t[2][:, hl * 512:(hl + 1) * 512]
                    for kc in range(8):
                        P.add("pe", lambda e, kc=kc, ht=ht, hl=hl, pk_ap=pk_ap: e.matmul(
                            pk_ap, lhsT=wk[:, kc, hl * 128:(hl + 1) * 128], rhs=ht[:, kc, :],
                            start=(kc == 0), stop=(kc == 7)), hb + [wk], [pk_b])
                    P.add("dve", lambda e, hl=hl, g=g, pk_ap=pk_ap: e.tensor_copy(
                        out=KT[hl][:, g * 512:(g + 1) * 512], in_=pk_ap), [pk_b], [KT[hl]])

        for j in range(NSLOT):
            nkt = nkts[j]
            hb = hT[0]
            ht = hT_t[0]
            for r in range(4):
                ii = j * 4 + r
                x_b = xt[ii % 3]
                dma("sp", x_b[:], xo_v[ii * 128:(ii + 1) * 128, :], writes=[x_b], dsem=xsem[ii % 3])
                norm_tile(x_b, ii, A1, modc, [bk[3][1]], pp_t[3][:, 512:1024].bitcast(BF16), hb[r],
                          lambda kc, r=r: ht[:, kc, r * 128:(r + 1) * 128],
                          stats[ii % 4], sqj, xsb[ii % 2])
            qt = QT[j % 2]
            for hl in range(2):
                F_ap = pp_t[3][:, 512:1024]
                for kc in range(8):
                    P.add("pe", lambda e, kc=kc, hl=hl, F_ap=F_ap: e.matmul(
                        F_ap, lhsT=wq[:, kc, hl * 128:(hl + 1) * 128], rhs=ht[:, kc, :],
                        start=(kc == 0), stop=(kc == 7)), hb + [wq], [bk[3][1]])
                P.add("dve", lambda e, hl=hl, qt=qt, F_ap=F_ap: e.tensor_scalar(
                    out=qt[:, hl, :], in0=F_ap, scalar1=0.125, scalar2=None, op0=ALU.mult),
                    [bk[3][1]], [qt])
                for c in range(2):
                    P.add("dve", lambda e, hl=hl, c=c, F_ap=F_ap: e.tensor_scalar(
                        out=Qpad[c][c * 64:(c + 1) * 64, hl, :], in0=F_ap[c * 64:(c + 1) * 64, :],
                        scalar1=0.125, scalar2=None, op0=ALU.mult), [bk[3][1]], [Qpad[c]])

            for hl in range(2):
                h = 2 * hg + hl
                O1_b, O2_b, den_b, F_b = bk[2][0], bk[2][1], bk[3][0], bk[3][1]
                O1_ap, O2_ap = pp_t[2][:, 0:512], pp_t[2][:, 512:1024]
                den_ap, F_ap = pp_t[3][:, 0:512], pp_t[3][:, 512:1024]

                def qk(kt, hl=hl, h=h, qt=qt, nkt=nkt):
                    Sb = bk[kt % 2]
                    St = pp_t[kt % 2]
                    tadd = kt >= nkt - 8
                    for c in range(2):
                        if tadd:
                            P.add("pe", lambda e, c=c: e.matmul(
                                St[:, c * 512:(c + 1) * 512], lhsT=KT[hl][:, kt * 128:(kt + 1) * 128],
                                rhs=Qpad[c][:, hl, :], start=True, stop=False),
                                [KT[hl], Qpad[c]], [Sb[c]])
                        else:
                            P.add("pe", lambda e, c=c: e.matmul(
                                St[:, c * 512:(c + 1) * 512],
                                lhsT=KT[hl][c * 64:(c + 1) * 64, kt * 128:(kt + 1) * 128],
                                rhs=qt[c * 64:(c + 1) * 64, hl, :], start=True, stop=True),
                                [KT[hl], qt], [Sb[c]])
                    if tadd:
                        tk = kt - (nkt - 8)
                        for c in range(2):
                            P.add("pe", lambda e, c=c: e.matmul(
                                St[:, c * 512:(c + 1) * 512], lhsT=diag[:, h, :], rhs=tmask[:, tk, :],
                                start=False, stop=True), [diag, tmask], [Sb[c]])

                def ex(kt, hl=hl, h=h, j=j, nkt=nkt):
                    Sb = bk[kt % 2]
                    St = pp_t[kt % 2]
                    pb = Pp[kt % 3]
                    ci = cb_base[j] + h * nkt + kt
                    P.add("act", lambda e: e.activation(out=pb[:], in_=St[:], func=AF.Exp,
                                                        bias=cbias[:, ci:ci + 1], scale=1.0),
                          [Sb[0], Sb[1], cbias], [pb])

                def av(kt, hl=hl, nkt=nkt):
                    pb = Pp[kt % 3]
                    first, last = kt == 0, kt == nkt - 1
                    P.add("pe", lambda e: e.matmul(O1_ap, lhsT=V[:, kt, hl * 128:(hl + 1) * 128],
                                                   rhs=pb[:, 0:512], start=first, stop=last),
                          [V, pb], [O1_b])
                    P.add("pe", lambda e: e.matmul(O2_ap, lhsT=V[:, kt, hl * 128:(hl + 1) * 128],
                                                   rhs=pb[:, 512:1024], start=first, stop=last),
                          [V, pb], [O2_b])
                    P.add("pe", lambda e: e.matmul(den_ap[0:32, :], lhsT=ones_bf[:, 0:32],
                                                   rhs=pb[:, 0:512], start=first, stop=last),
                          [ones_bf, pb], [den_b])
                    P.add("pe", lambda e: e.matmul(den_ap[32:64, :], lhsT=ones_bf[:, 0:32],
                                                   rhs=pb[:, 512:1024], start=first, stop=last),
                          [ones_bf, pb], [den_b])

                qk(0)
                qk(1)
                for kt in range(nkt):
                    ex(kt)
                    av(kt)
                    if kt + 2 < nkt:
                        qk(kt + 2)

                P.add("dve", lambda e: e.tensor_copy(out=O1s[:], in_=O1_ap), [O1_b], [O1s])
                P.add("dve", lambda e: e.tensor_copy(out=O2s[:], in_=O2_ap), [O2_b], [O2s])
                P.add("dve", lambda e: e.tensor_copy(out=dens[:], in_=den_ap[0:64, :]), [den_b], [dens])
                P.add("dve", lambda e: e.reciprocal(out=rden[:], in_=dens[:]), [dens], [rden])
                P.add("pe", lambda e: e.matmul(F_ap, lhsT=selA[:], rhs=rden[:], start=True, stop=True),
                      [selA, rden], [F_b])
                P.add("dve", lambda e: e.tensor_tensor(out=t1[:], in0=O1s[:], in1=F_ap, op=ALU.mult),
                      [O1s, F_b], [t1])
                P.add("pe", lambda e: e.matmul(F_ap, lhsT=selB[:], rhs=rden[:], start=True, stop=True),
                      [selB, rden], [F_b])
                P.add("dve", lambda e: e.tensor_tensor(out=O2s[:], in0=O2s[:], in1=F_ap, op=ALU.mult),
                      [O2s, F_b], [O2s])
                P.add("dve", lambda e: e.scalar_tensor_tensor(out=O1s[:], in0=O2s[:], scalar=nlam[:, 0:1],
                                                              in1=t1[:], op0=ALU.mult, op1=ALU.add),
                      [O2s, nlam, t1], [O1s])
                P.add("dve", lambda e: e.tensor_tensor(out=O2s[:], in0=O1s[:], in1=O1s[:], op=ALU.mult),
                      [O1s], [O2s])
                P.add("pe", lambda e: e.matmul(F_ap, lhsT=ones_f[:], rhs=O2s[:], start=True, stop=True),
                      [ones_f, O2s], [F_b])
                P.add("dve", lambda e: e.tensor_scalar(out=rx[:], in0=F_ap, scalar1=1.0 / 128, scalar2=EPS,
                                                       op0=ALU.mult, op1=ALU.add), [F_b], [rx])
                rsqrt("dve", ry, ry[:], rx, rx[:], rt, rt[:])
                P.add("dve", lambda e, h=h, j=j: e.scalar_tensor_tensor(
                    out=on_t[:, j, h, :], in0=O1s[:], scalar=subg[:, 0:1], in1=ry[:],
                    op0=ALU.mult, op1=ALU.mult), [O1s, subg, ry], [on[j]])

    if debug == "p2":
        o1 = dout("d_on", [128, NSLOT * 4 * 512], BF16)
        o2 = dout("d_KT", [128, S], BF16)
        o3 = dout("d_V", [128, (S // 128) * 256], BF16)
        ds = P.dsem("dbgs")
        fin = [dma("sp", o1.ap(), on_t[:].rearrange("p a b c -> p (a b c)"), reads=on, dsem=ds),
               dma("sp", o2.ap(), KT[0][:], reads=[KT[0]], dsem=ds),
               dma("sp", o3.ap(), V[:].rearrange("p a b -> p (a b)"), reads=[V], dsem=ds)]
        op = P.add("sp", None)
        op.deps = fin
        with nc.Block() as block:
            P.emit(block)
        esA.close()
        es.close()
        return nc, dbg

    P.barrier()
    esA.close()

    es3 = ExitStack()
    NR = 5
    ring = [P.sb("ring%d" % i, [128, 4096], BF16, es3) for i in range(NR)]
    rsem = [P.dsem("rsem%d" % i, es3) for i in range(NR)]
    ring_n = [0]

    def wload(src_ap, shape_str, **kw):
        i = ring_n[0] % NR
        ring_n[0] += 1
        rb = ring[i]
        n = 1
        for v_ in src_ap.shape[1:]:
            n *= v_
        dst = rb[:, 0:n].rearrange(shape_str, **kw)
        dma("sp", dst, src_ap, reads=list(wbufs.values()), writes=[rb], dsem=rsem[i])
        return rb, dst

    x3 = [P.sb("x3_%d" % i, [128, D], F32, es3) for i in range(4)]
    x3sem = [P.dsem("x3sem%d" % i, es3) for i in range(4)]
    xhb = P.sb("xhb", [128, D], F32, es3)
    xhsem = P.dsem("xhsem", es3)
    sqj3 = P.sb("sqj3", [128, D], BF16, es3)
    xsb3 = [P.sb("xsb3_%d" % i, [128, D], BF16, es3) for i in range(2)]
    stats3 = [tuple(P.sb("st3_%d_%d" % (i, k_), [128, 1], F32, es3) for k_ in range(4)) for i in range(4)]
    h3_t = es3.enter_context(nc.sbuf_tensor("s_h3", [128, 8, 514], BF16))
    h3 = [Buf("h3_%d" % r, h3_t) for r in range(5)]
    usb = P.sb("usb", [128, 514], F32, es3)
    vsb = P.sb("vsb", [128, 514], F32, es3)
    zsb = P.sb("zsb", [128, 512], F32, es3)
    aT = P.sb("aT", [128, 4, 512], BF16, es3)
    tha = [P.sb("tha%d" % i, [128, 512], F32, es3) for i in range(2)]
    m1 = [P.sb("m1_%d" % i, [128, 512], F32, es3) for i in range(2)]
    mT = P.sb("mT", [128, 8, 512], BF16, es3)
    AT = P.sb("AT", [128, 22, 512], BF16, es3)
    tmpr = [P.sb("tmpr%d" % i, [128, 512], F32, es3) for i in range(2)]
    osem = [P.dsem("osem%d" % i, es3) for i in range(4)]
    pb_t = [es3.enter_context(nc.psum_tensor("p_b%d" % i, [128, 512], F32)) for i in range(8)]
    pb = [Buf("pb%d" % i, pb_t[i]) for i in range(8)]
    bank_n = [0]

    def nbank():
        i = bank_n[0] % 8
        bank_n[0] += 1
        return pb[i], pb_t[i]

    win_b = wb_in_d.ap().rearrange("(kc p) n -> p kc n", p=128)
    woa_b = wb_oa_d.ap().rearrange("(cc p) n -> p cc n", p=128)
    wob_b = wb_ob_d.ap().rearrange("(cc p) n -> p cc n", p=128)
    wo_b = wb_o_d.ap().rearrange("(kc p) n -> p kc n", p=128)
    wg_b = wb_g_d.ap().rearrange("(kc p) n -> p kc n", p=128)
    wu_b = wb_u_d.ap().rearrange("(kc p) n -> p kc n", p=128)
    wd_b = wb_d_d.ap().rearrange("(fc p) n -> p fc n", p=128)
    y_v = y_d.ap()
    out_ops = []

    for j in range(NSLOT):
        for r in range(5):
            if r < 4:
                x_b = x3[r]
                dma("pool", x_b[:], xo_v[(j * 4 + r) * 128:(j * 4 + r + 1) * 128, :], writes=[x_b],
                    dsem=x3sem[r])
            else:
                x_b = xhb
                dma("pool", x_b[:], xh_d.ap()[j], writes=[x_b], dsem=xhsem)
            pbk, pbt = nbank()
            if r < 4:
                hfn = lambda kc, r=r: h3_t[:, kc, 2 + r * 128:2 + (r + 1) * 128]
                norm_tile(x_b, r, A1, modc, [pbk], pbt[:].bitcast(BF16), h3[r], hfn,
                          stats3[r % 4], sqj3, xsb3[r % 2])
            else:
                norm_tile(x_b, r, A1, modc, [pbk], pbt[:].bitcast(BF16), h3[4],
                          lambda kc: h3_t[:, kc, 0:2], stats3[0], sqj3, xsb3[0], ncols=2)

        wu_, wu_v = wload(win_b[:, :, 0:512], "p (k n) -> p k n", k=8)
        wgb_, wgb_v = wload(win_b[:, :, 512:1024], "p (k n) -> p k n", k=8)
        wgc_, wgc_v = wload(win_b[:, :, 1024:1536], "p (k n) -> p k n", k=8)
        for cc in range(4):
            bu, tu = nbank()
            bgc, tgc = nbank()
            bgb, tgb = nbank()
            bh, th = nbank()
            for (wb_, wv_, bb, tt, hcol) in ((wu_, wu_v, bu, tu, 0), (wgc_, wgc_v, bgc, tgc, 2),
                                            (wgb_, wgb_v, bgb, tgb, None)):
                for kc in range(8):
                    P.add("pe", lambda e, kc=kc, wv_=wv_, tt=tt, cc=cc: e.matmul(
                        tt[:], lhsT=wv_[:, kc, cc * 128:(cc + 1) * 128], rhs=h3_t[:, kc, 2:514],
                        start=(kc == 0), stop=(kc == 7)), [wb_] + h3[0:4], [bb])
                if hcol is not None:
                    for kc in range(8):
                        P.add("pe", lambda e, kc=kc, wv_=wv_, th=th, cc=cc, hcol=hcol: e.matmul(
                            th[:, hcol:hcol + 2], lhsT=wv_[:, kc, cc * 128:(cc + 1) * 128],
                            rhs=h3_t[:, kc, 0:2], start=(kc == 0), stop=(kc == 7)), [wb_, h3[4]], [bh])
            P.add("act", lambda e, tu=tu: e.activation(out=usb[:, 2:514], in_=tu[:], func=AF.Copy),
                  [bu], [usb])
            P.add("act", lambda e, th=th: e.activation(out=usb[:, 0:2], in_=th[:, 0:2], func=AF.Copy),
                  [bh], [usb])
            P.add("dve", lambda e, tgc=tgc: e.tensor_tensor(out=vsb[:, 2:514], in0=tgc[:], in1=usb[:, 2:514],
                                                            op=ALU.mult), [bgc, usb], [vsb])
            P.add("dve", lambda e, th=th, j=j: e.scalar_tensor_tensor(
                out=vsb[:, 0:2], in0=th[:, 2:4], scalar=hm[:, j:j + 1], in1=usb[:, 0:2],
                op0=ALU.mult, op1=ALU.mult), [bh, hm, usb], [vsb])
            P.add("dve", lambda e, cc=cc: e.tensor_scalar(out=zsb[:], in0=vsb[:, 0:512],
                                                          scalar1=cw[:, cc * 3:cc * 3 + 1], scalar2=None,
                                                          op0=ALU.mult), [vsb, cw], [zsb])
            for i_ in (1, 2):
                P.add("dve", lambda e, cc=cc, i_=i_: e.scalar_tensor_tensor(
                    out=zsb[:], in0=vsb[:, i_:i_ + 512], scalar=cw[:, cc * 3 + i_:cc * 3 + i_ + 1],
                    in1=zsb[:], op0=ALU.mult, op1=ALU.add), [vsb, cw, zsb], [zsb])
            P.add("dve", lambda e, cc=cc, tgb=tgb: e.tensor_tensor(out=aT[:, cc, :], in0=tgb[:], in1=zsb[:],
                                                                   op=ALU.mult), [bgb, zsb], [aT])

        for dh_ in range(2):
            i_ab = ring_n[0] % NR
            ring_n[0] += 1
            wab_ = ring[i_ab]
            wab_v = wab_[:, 0:4096].rearrange("p (a c n) -> p a c n", a=2, c=4)
            oa1 = dma("sp", wab_v[:, 0], woa_b[:, :, dh_ * 512:(dh_ + 1) * 512], reads=list(wbufs.values()),
                      writes=[wab_], dsem=rsem[i_ab])
            oa2 = dma("sp", wab_v[:, 1], wob_b[:, :, dh_ * 512:(dh_ + 1) * 512], reads=list(wbufs.values()),
                      writes=[wab_], dsem=rsem[i_ab], exclude=[oa1])
            oa1.dcount = oa2.dcount
            woa_ = wob_ = wab_
            wga_, wga_v = wload(win_b[:, :, 3072 + dh_ * 512:3072 + (dh_ + 1) * 512], "p (k n) -> p k n", k=8)
            wgb2_, wgb2_v = wload(win_b[:, :, 4096 + dh_ * 512:4096 + (dh_ + 1) * 512], "p (k n) -> p k n", k=8)
            for dc4 in range(4):
                dc = dh_ * 4 + dc4
                bga, tga = nbank()
                bya, tya = nbank()
                bgb_, tgb_ = nbank()
                byb, tyb = nbank()
                for (wb_, wv_, bb, tt) in ((wga_, wga_v, bga, tga), (wgb2_, wgb2_v, bgb_, tgb_)):
                    for kc in range(8):
                        P.add("pe", lambda e, kc=kc, wv_=wv_, tt=tt, dc4=dc4: e.matmul(
                            tt[:], lhsT=wv_[:, kc, dc4 * 128:(dc4 + 1) * 128], rhs=h3_t[:, kc, 2:514],
                            start=(kc == 0), stop=(kc == 7)), [wb_] + h3[0:4], [bb])
                for c4 in range(4):
                    P.add("pe", lambda e, c4=c4, dc4=dc4, tya=tya, wab_v=wab_v: e.matmul(
                        tya[:], lhsT=wab_v[:, 0, c4, dc4 * 128:(dc4 + 1) * 128], rhs=aT[:, c4, :],
                        start=(c4 == 0), stop=(c4 == 3)), [woa_, aT], [bya])
                for c4 in range(4):
                    P.add("pe", lambda e, c4=c4, dc4=dc4, tyb=tyb, j=j, wab_v=wab_v: e.matmul(
                        tyb[:], lhsT=wab_v[:, 1, c4, dc4 * 128:(dc4 + 1) * 128], rhs=on_t[:, j, c4, :],
                        start=(c4 == 0), stop=(c4 == 3)), [wob_, on[j]], [byb])
                P.add("act", lambda e, tga=tga: e.activation(out=tha[0][:], in_=tga[:], func=AF.Tanh, scale=0.5),
                      [bga], [tha[0]])
                P.add("act", lambda e, tgb_=tgb_: e.activation(out=tha[1][:], in_=tgb_[:], func=AF.Tanh, scale=0.5),
                      [bgb_], [tha[1]])
                P.add("dve", lambda e, tya=tya: e.scalar_tensor_tensor(
                    out=m1[0][:], in0=tha[0][:], scalar=1.0, in1=tya[:], op0=ALU.add, op1=ALU.mult),
                    [tha[0], bya], [m1[0]])
                P.add("dve", lambda e, tyb=tyb: e.scalar_tensor_tensor(
                    out=m1[1][:], in0=tha[1][:], scalar=1.0, in1=tyb[:], op0=ALU.add, op1=ALU.mult),
                    [tha[1], byb], [m1[1]])
                P.add("dve", lambda e, dc=dc: e.tensor_tensor(out=mT[:, dc, :], in0=m1[0][:], in1=m1[1][:],
                                                              op=ALU.add), [m1[0], m1[1]], [mT])

        for dh_ in range(2):
            wo_, wo_v = wload(wo_b[:, :, dh_ * 512:(dh_ + 1) * 512], "p (k n) -> p k n", k=8)
            for ts in range(4):
                bo, to = nbank()
                for kc in range(8):
                    P.add("pe", lambda e, kc=kc, ts=ts, to=to, wo_v=wo_v: e.matmul(
                        to[:], lhsT=mT[:, kc, ts * 128:(ts + 1) * 128], rhs=wo_v[:, kc, :],
                        start=(kc == 0), stop=(kc == 7)), [mT, wo_], [bo])
                tr = tmpr[ts % 2]
                P.add("dve", lambda e, to=to, tr=tr, dh_=dh_: e.tensor_tensor(
                    out=tr[:], in0=to[:], in1=gtm[:, dh_ * 512:(dh_ + 1) * 512], op=ALU.mult),
                    [bo, gtm], [tr])
                P.add("pool", lambda e, ts=ts, tr=tr, dh_=dh_: e.tensor_tensor(
                    out=x3[ts][:, dh_ * 512:(dh_ + 1) * 512], in0=x3[ts][:, dh_ * 512:(dh_ + 1) * 512],
                    in1=tr[:], op=ALU.add), [x3[ts], tr], [x3[ts]])

        for r in range(4):
            pbk, pbt = nbank()
            norm_tile(x3[r], r, A2, modc[:, 16:24], [pbk], pbt[:].bitcast(BF16), h3[r],
                      lambda kc, r=r: h3_t[:, kc, 2 + r * 128:2 + (r + 1) * 128],
                      stats3[r % 4], sqj3, xsb3[r % 2], bcol_buf=modc)

        for t6 in range(6):
            nf = 4 if t6 < 5 else 2
            wg_, wg_v = wload(wg_b[:, :, t6 * 512:t6 * 512 + nf * 128], "p (k n) -> p k n", k=8)
            wu2_, wu2_v = wload(wu_b[:, :, t6 * 512:t6 * 512 + nf * 128], "p (k n) -> p k n", k=8)
            for f4 in range(nf):
                fc = t6 * 4 + f4
                bg_, tg_ = nbank()
                bu_, tu_ = nbank()
                for (wb_, wv_, bb, tt) in ((wg_, wg_v, bg_, tg_), (wu2_, wu2_v, bu_, tu_)):
                    for kc in range(8):
                        P.add("pe", lambda e, kc=kc, wv_=wv_, tt=tt, f4=f4: e.matmul(
                            tt[:], lhsT=wv_[:, kc, f4 * 128:(f4 + 1) * 128], rhs=h3_t[:, kc, 2:514],
                            start=(kc == 0), stop=(kc == 7)), [wb_] + h3[0:4], [bb])
                th_ = tha[fc % 2]
                s1_ = m1[fc % 2]
                P.add("act", lambda e, tg_=tg_, th_=th_: e.activation(out=th_[:], in_=tg_[:], func=AF.Tanh,
                                                                     scale=0.5), [bg_], [th_])
                P.add("dve", lambda e, tg_=tg_, th_=th_, s1_=s1_: e.scalar_tensor_tensor(
                    out=s1_[:], in0=th_[:], scalar=1.0, in1=tg_[:], op0=ALU.add, op1=ALU.mult),
                    [th_, bg_], [s1_])
                P.add("dve", lambda e, tu_=tu_, s1_=s1_, fc=fc: e.tensor_tensor(
                    out=AT[:, fc, :], in0=s1_[:], in1=tu_[:], op=ALU.mult), [s1_, bu_], [AT])

        for dh_ in range(2):
            banks = [nbank() for _ in range(4)]
            for t3 in range(3):
                f0 = t3 * 8
                nf = 8 if t3 < 2 else 6
                wd_, wd_v = wload(wd_b[:, f0:f0 + nf, dh_ * 512:(dh_ + 1) * 512], "p (f n) -> p f n", f=nf)
                for ts in range(4):
                    bo, to = banks[ts]
                    for f_ in range(nf):
                        fc = f0 + f_
                        P.add("pe", lambda e, f_=f_, fc=fc, ts=ts, to=to, wd_v=wd_v: e.matmul(
                            to[:], lhsT=AT[:, fc, ts * 128:(ts + 1) * 128], rhs=wd_v[:, f_, :],
                            start=(fc == 0), stop=(fc == 21)), [AT, wd_], [bo])
            for ts in range(4):
                bo, to = banks[ts]
                tr = tmpr[ts % 2]
                P.add("dve", lambda e, to=to, tr=tr, dh_=dh_: e.tensor_tensor(
                    out=tr[:], in0=to[:], in1=gtf[:, dh_ * 512:(dh_ + 1) * 512], op=ALU.mult),
                    [bo, gtf], [tr])
                P.add("pool", lambda e, ts=ts, tr=tr, dh_=dh_: e.tensor_tensor(
                    out=x3[ts][:, dh_ * 512:(dh_ + 1) * 512], in0=x3[ts][:, dh_ * 512:(dh_ + 1) * 512],
                    in1=tr[:], op=ALU.add), [x3[ts], tr], [x3[ts]])
        for ts in range(4):
            ssq, sx, sy, stt_ = stats3[ts]
            P.add("act", lambda e, ts=ts, ssq=ssq: e.activation(out=sqj3[:], in_=x3[ts][:], func=AF.Square,
                                                                accum_out=ssq[:]), [x3[ts]], [sqj3, ssq])
            P.add("dve", lambda e, ssq=ssq, sx=sx: e.tensor_scalar(out=sx[:], in0=ssq[:], scalar1=1.0 / D,
                                                                   scalar2=EPS, op0=ALU.mult, op1=ALU.add),
                  [ssq], [sx])
            rsqrt("dve", sy, sy[:], sx, sx[:], stt_, stt_[:])
            P.add("dve", lambda e, ts=ts, sy=sy: e.scalar_tensor_tensor(
                out=x3[ts][:], in0=x3[ts][:], scalar=sy[:, 0:1], in1=gfin[:], op0=ALU.mult, op1=ALU.mult),
                [x3[ts], sy, gfin], [x3[ts]])
            out_ops.append(dma("pool", y_v[(j * 4 + ts) * 128:(j * 4 + ts + 1) * 128, :], x3[ts][:],
                               reads=[x3[ts]], dsem=x3sem[ts]))

    opf = P.add("sp", None)
    opf.deps = list(out_ops)
    opf2 = P.add("pool", None)
    opf2.deps = list(out_ops)
    with nc.Block() as block:
        P.emit(block)
    es3.close()
    es.close()
    return nc, dbg


def _prep_inputs(inp):
    f32 = np.float32
    x = np.asarray(inp["x"], f32)
    c = np.asarray(inp["c"], f32)
    w_ada = np.ascontiguousarray(np.asarray(inp["w_ada"], f32)[0])
    b_ada = np.asarray(inp["b_ada"], f32)[0]
    shared = {
        "w_ada": w_ada,
        "bcol": np.ascontiguousarray(b_ada.reshape(48, 128).T),
        "brow": np.ascontiguousarray(np.stack([b_ada[2048:3072], b_ada[5120:6144]])),
        "gmix": np.ascontiguousarray(np.asarray(inp["g_mix"], f32)[0].reshape(8, 128).T),
        "gffn": np.ascontiguousarray(np.asarray(inp["g_ffn"], f32)[0].reshape(8, 128).T),
        "gfin": np.ascontiguousarray(np.asarray(inp["g_final"], f32).reshape(1, D)),
        "w_in": np.ascontiguousarray(np.asarray(inp["w_in"], f32)[0]),
        "cw": np.ascontiguousarray(
            np.asarray(inp["conv_w"], f32)[0].reshape(3, 4, 128).transpose(2, 1, 0).reshape(128, 12)),
        "w_out_a": np.ascontiguousarray(np.asarray(inp["w_out_a"], f32)[0]),
        "w_out_b": np.ascontiguousarray(np.asarray(inp["w_out_b"], f32)[0]),
        "w_out": np.ascontiguousarray(np.asarray(inp["w_out"], f32)[0]),
        "w_gate": np.ascontiguousarray(np.asarray(inp["w_gate"], f32)[0]),
        "w_up": np.ascontiguousarray(np.asarray(inp["w_up"], f32)[0]),
        "w_down": np.ascontiguousarray(np.asarray(inp["w_down"], f32)[0]),
        "lam": np.ascontiguousarray(np.stack([np.asarray(inp[k], f32)[0] for k in
                                              ("lambda_q1", "lambda_k1", "lambda_q2", "lambda_k2")])),
        "subg": np.ascontiguousarray(np.asarray(inp["subln_g"], f32)[0].reshape(128, 1)),
        "ident": np.eye(128, dtype=f32).astype(ml_dtypes.bfloat16),
        "diag": np.stack([np.eye(128, dtype=f32) * s for s in SLOPES]).astype(ml_dtypes.bfloat16),
    }
    maps = []
    for core in range(8):
        b, par = core // 2, core % 2
        tiles = [2 * j + 1 for j in range(NSLOT)] if par == 0 else [2 * j for j in range(NSLOT)]
        xb = x[b]
        xo = np.concatenate([xb[t * TQ:(t + 1) * TQ] for t in tiles], axis=0)
        xh = np.zeros((NSLOT, 128, D), f32)
        hmask = np.zeros((128, NSLOT), f32)
        for j, t in enumerate(tiles):
            if t > 0:
                xh[j, 0:2] = xb[t * TQ - 2:t * TQ]
                hmask[:, j] = 1.0
        qoff = 512 if par == 0 else 0
        kk = np.arange(1024)[:, None]
        qq = (np.arange(512) + qoff)[None, :]
        allowed = (kk // 64) <= (qq // 64)
        tm = np.where(allowed, np.where(kk > qq, -2.0 * (kk - qq), 0.0), MASKV).astype(f32)
        tm = tm.reshape(8, 128, 512)
        cb = np.zeros((128, sum(16 * (2 * jj + 2) for jj in range(NSLOT))), f32)
        idx = 0
        for j, t in enumerate(tiles):
            ref = t * TQ + 255
            nkt = 4 * (2 * j + 2)
            for h in range(4):
                for kt in range(nkt):
                    kpos = kt * 128 + np.arange(128)
                    cb[:, idx] = SLOPES[h] * (kpos - ref)
                    idx += 1
        assert idx == cb.shape[1]
        m = dict(shared)
        m.update({
            "xs": np.ascontiguousarray(xb[:S]), "xo": np.ascontiguousarray(xo), "xh": xh, "hm": hmask,
            "ccol": np.ascontiguousarray(c[b].reshape(8, 128).T),
            "tmask": tm.astype(ml_dtypes.bfloat16), "cbias": cb,
        })
        maps.append((m, tiles))
    return maps


def kernel(**inputs):
    maps = _prep_inputs(inputs)
    nc, _ = _build()
    res = run_bass_kernel_spmd(nc, [m for m, _ in maps], core_ids=list(range(8)))
    out = np.zeros((NB, S, D), np.float32)
    for core in range(8):
        y = res.results[core]["y"]
        b = core // 2
        for j, t in enumerate(maps[core][1]):
            out[b, t * TQ:(t + 1) * TQ] = y[j * TQ:(j + 1) * TQ]
    return out
```

```python
import math
from contextlib import ExitStack

import numpy as np
import ml_dtypes

import concourse.bass as bass
import concourse.mybir as mybir
from concourse.bass_utils import run_bass_kernel_spmd

F32 = mybir.dt.float32
BF16 = mybir.dt.bfloat16
I32 = mybir.dt.int32
AF = mybir.ActivationFunctionType
ALU = mybir.AluOpType

D = 1024
S = 8192
NB = 4
DFF = 2816
EPS = 1e-6
LAMBDA_INIT = 0.8 - 0.6 * math.exp(-0.3 * 1)
SLOPES = [2.0 ** (-8.0 * (i + 1) / 4) for i in range(4)]
NSLOT = 8
TQ = 512
MASKV = -float(2 ** 20)


class DSem:
    def __init__(self, sem):
        self.sem = sem
        self.count = 0


class Op:
    __slots__ = ("eng", "fn", "deps", "sig", "sigidx", "dsem", "dcount")

    def __init__(self, eng, fn):
        self.eng = eng
        self.fn = fn
        self.deps = []
        self.sig = False
        self.sigidx = 0
        self.dsem = None
        self.dcount = 0


class Buf:
    def __init__(self, name, t):
        self.name = name
        self.t = t
        self.w = {}
        self.r = {}

    def __getitem__(self, k):
        return self.t[k]


class Prog:
    ENGS = ("pe", "act", "dve", "pool", "sp")
    SYNC_SELF = ("act", "dve", "pool")

    def __init__(self, nc, es):
        self.nc = nc
        self.es = es
        self.q = {e: [] for e in self.ENGS}
        self.esem = {e: es.enter_context(nc.semaphore("es_" + e)) for e in self.ENGS}
        self.bar = {}
        self.nbuf = 0

    def sb(self, name, shape, dtype, es=None):
        t = (es or self.es).enter_context(self.nc.sbuf_tensor("s_" + name, list(shape), dtype))
        return Buf(name, t)

    def ps(self, name, shape, dtype=F32, es=None):
        t = (es or self.es).enter_context(self.nc.psum_tensor("p_" + name, list(shape), dtype))
        return Buf(name, t)

    def dsem(self, name, es=None):
        return DSem((es or self.es).enter_context(self.nc.semaphore(name)))

    def dram(self, name, t):
        return Buf(name, t)

    def add(self, eng, fn, reads=(), writes=(), dsem=None, exclude=()):
        op = Op(eng, fn)
        self.q[eng].append(op)
        if dsem is not None:
            dsem.count += 16
            op.dsem = dsem
            op.dcount = dsem.count
            key = ("d", id(dsem))
        else:
            key = eng
        deps = {}
        for b in reads:
            for d in b.w.values():
                deps[id(d)] = d
        for b in writes:
            for d in b.r.values():
                deps[id(d)] = d
            for d in b.w.values():
                deps[id(d)] = d
        if eng in self.bar:
            for d in self.bar.pop(eng):
                deps[id(d)] = d
        for x_ in exclude:
            deps.pop(id(x_), None)
        op.deps = list(deps.values())
        for b in reads:
            b.r[key] = op
        for b in writes:
            if b.r:
                b.w = {key: op}
                b.r = {}
            else:
                b.w[key] = op
        return op

    def barrier(self, extra=()):
        last = []
        for e in self.ENGS:
            for op in reversed(self.q[e]):
                if op.dsem is None:
                    last.append(op)
                    break
        seen = {}
        for e in self.ENGS:
            for op in self.q[e]:
                if op.dsem is not None:
                    seen[id(op.dsem)] = op
        last += list(seen.values())
        for e in self.ENGS:
            self.bar[e] = list(last)

    def emit(self, block):
        for e in self.ENGS:
            for op in self.q[e]:
                for d in op.deps:
                    if d.dsem is None and (d.eng != op.eng or op.eng in self.SYNC_SELF):
                        d.sig = True
        for e in self.ENGS:
            n = 0
            for op in self.q[e]:
                if op.sig:
                    n += 1
                    op.sigidx = n

        def runner(ename):
            def f(eng):
                waited = {}
                for op in self.q[ename]:
                    need = {}
                    for d in op.deps:
                        if d.dsem is not None:
                            k, v, s = ("d", id(d.dsem)), d.dcount, d.dsem.sem
                        elif d.eng == ename and ename not in self.SYNC_SELF:
                            continue
                        else:
                            k, v, s = d.eng, d.sigidx, self.esem[d.eng]
                        if waited.get(k, 0) < v and need.get(k, (0, None))[0] < v:
                            need[k] = (v, s)
                    for k, (v, s) in need.items():
                        eng.wait_ge(s, v)
                        waited[k] = v
                    if op.fn is None:
                        continue
                    inst = op.fn(eng)
                    if op.dsem is not None:
                        inst.then_inc(op.dsem.sem, 16)
                    elif op.sig:
                        inst.then_inc(self.esem[ename], 1)
            return f

        block.tensor(runner("pe"))
        block.scalar(runner("act"))
        block.vector(runner("dve"))
        block.gpsimd(runner("pool"))
        block.sync(runner("sp"))


def _build(debug=None):
    nc = bass.Bass("TRN2", target_bir_lowering=False)
    es = ExitStack()
    P = Prog(nc, es)

    def din(name, shape, dt=F32):
        return nc.dram_tensor(name, list(shape), dt, kind="ExternalInput")

    xs_d = din("xs", [S, D])
    xo_d = din("xo", [NSLOT * TQ, D])
    xh_d = din("xh", [NSLOT, 128, D])
    hm_d = din("hm", [128, NSLOT])
    cc_d = din("ccol", [128, 8])
    wada_d = din("w_ada", [D, 6 * D])
    bcol_d = din("bcol", [128, 48])
    brow_d = din("brow", [2, D])
    gmix_d = din("gmix", [128, 8])
    gffn_d = din("gffn", [128, 8])
    gfin_d = din("gfin", [1, D])
    win_d = din("w_in", [D, 5120])
    cw_d = din("cw", [128, 12])
    woa_d = din("w_out_a", [512, D])
    wob_d = din("w_out_b", [512, D])
    wo_d = din("w_out", [D, D])
    wg_d = din("w_gate", [D, DFF])
    wu_d = din("w_up", [D, DFF])
    wd_d = din("w_down", [DFF, D])
    lam_d = din("lam", [4, 64])
    subg_d = din("subg", [128, 1])
    ident_d = din("ident", [128, 128], BF16)
    diag_d = din("diag", [4, 128, 128], BF16)
    tmask_d = din("tmask", [8, 128, 512], BF16)
    NCB = sum(16 * (2 * j + 2) for j in range(NSLOT))
    cb_d = din("cbias", [128, NCB])
    y_d = nc.dram_tensor("y", [NSLOT * TQ, D], F32, kind="ExternalOutput")

    wb_in_d = nc.dram_tensor("wb_in", [D, 5120], BF16)
    wb_oa_d = nc.dram_tensor("wb_oa", [512, D], BF16)
    wb_ob_d = nc.dram_tensor("wb_ob", [512, D], BF16)
    wb_o_d = nc.dram_tensor("wb_o", [D, D], BF16)
    wb_g_d = nc.dram_tensor("wb_g", [D, DFF], BF16)
    wb_u_d = nc.dram_tensor("wb_u", [D, DFF], BF16)
    wb_d_d = nc.dram_tensor("wb_d", [DFF, D], BF16)

    dbg = {}

    def dout(name, shape, dt=F32):
        t = nc.dram_tensor(name, list(shape), dt, kind="ExternalOutput")
        dbg[name] = t
        return t

    def dma(q, out, in_, reads=(), writes=(), dsem=None, exclude=(), **kw):
        return P.add(q, lambda e: e.dma_start(out=out, in_=in_, **kw), reads, writes, dsem, exclude)

    def rsqrt(eng, out_b, out_ap, x_b, x_ap, tmp_b, tmp_ap):
        P.add(eng, lambda e: e.tensor_scalar(out=out_ap.bitcast(I32), in0=x_ap.bitcast(I32),
                                             scalar1=-0.5, scalar2=1597463007.0,
                                             op0=ALU.mult, op1=ALU.add),
              [x_b], [out_b])
        for _ in range(3):
            P.add(eng, lambda e: e.tensor_tensor(out=tmp_ap, in0=out_ap, in1=out_ap, op=ALU.mult),
                  [out_b], [tmp_b])
            P.add(eng, lambda e: e.tensor_tensor(out=tmp_ap, in0=tmp_ap, in1=x_ap, op=ALU.mult),
                  [tmp_b, x_b], [tmp_b])
            P.add(eng, lambda e: e.tensor_scalar(out=tmp_ap, in0=tmp_ap, scalar1=-0.5, scalar2=1.5,
                                                 op0=ALU.mult, op1=ALU.add),
                  [tmp_b], [tmp_b])
            P.add(eng, lambda e: e.tensor_tensor(out=out_ap, in0=out_ap, in1=tmp_ap, op=ALU.mult),
                  [out_b, tmp_b], [out_b])

    ident = P.sb("ident", [128, 128], BF16)
    ones_bf = P.sb("ones_bf", [128, 128], BF16)
    ones_f = P.sb("ones_f", [128, 128], F32)
    ccol = P.sb("ccol", [128, 8], F32)
    scol = P.sb("scol", [128, 8], F32)
    bcol = P.sb("bcol", [128, 48], F32)
    gmix = P.sb("gmix", [128, 8], F32)
    gffn = P.sb("gffn", [128, 8], F32)
    modc = P.sb("modc", [128, 32], F32)
    A1 = P.sb("A1", [128, 8], F32)
    A2 = P.sb("A2", [128, 8], F32)
    gtm = P.sb("gtm", [128, D], F32)
    gtf = P.sb("gtf", [128, D], F32)
    gfin = P.sb("gfin", [128, D], F32)
    cw = P.sb("cw", [128, 12], F32)
    hm = P.sb("hm", [128, NSLOT], F32)
    subg = P.sb("subg", [128, 1], F32)
    nlam = P.sb("nlam", [128, 1], F32)
    cbias = P.sb("cbias", [128, NCB], F32)

    cs = P.dsem("cs")
    cops = []
    for (b, d_ap) in ((ident, ident_d.ap()), (ccol, cc_d.ap()), (bcol, bcol_d.ap()),
                      (gmix, gmix_d.ap()), (gffn, gffn_d.ap()), (cw, cw_d.ap()),
                      (hm, hm_d.ap()), (subg, subg_d.ap()), (cbias, cb_d.ap())):
        cops.append(dma("sp", b[:], d_ap, writes=[b], dsem=cs))
    cops.append(dma("sp", gfin[:], gfin_d.ap().partition_broadcast(128), writes=[gfin], dsem=cs))
    cops.append(dma("sp", gtm[:], brow_d.ap()[0:1, :].partition_broadcast(128), writes=[gtm], dsem=cs))
    cops.append(dma("sp", gtf[:], brow_d.ap()[1:2, :].partition_broadcast(128), writes=[gtf], dsem=cs))
    lamt = P.sb("lamt", [128, 256], F32)
    cops.append(dma("sp", lamt[:], lam_d.ap().rearrange("(o a) b -> o (a b)", o=1).partition_broadcast(128),
                    writes=[lamt], dsem=cs))
    for o_ in cops:
        o_.dcount = cs.count

    wcs = P.dsem("wcs")
    wbufs = {}
    wops = []
    for (src, dst, pat, kw) in (
            (win_d, wb_in_d, "k (a n) -> (k a) n", dict(n=1024)),
            (woa_d, wb_oa_d, None, None), (wob_d, wb_ob_d, None, None), (wo_d, wb_o_d, None, None),
            (wg_d, wb_g_d, "k (a n) -> (k a) n", dict(n=1408)),
            (wu_d, wb_u_d, "k (a n) -> (k a) n", dict(n=1408)),
            (wd_d, wb_d_d, None, None)):
        sa, da = src.ap(), dst.ap()
        if pat is not None:
            sa, da = sa.rearrange(pat, **kw), da.rearrange(pat, **kw)
        wbufs[dst.name] = Buf("wscr_" + dst.name, None)
        wops.append(dma("pool", da, sa, writes=[wbufs[dst.name]], dsem=wcs))
    for o_ in wops:
        o_.dcount = wcs.count

    P.add("pool", lambda e: e.memset(ones_bf[:], 1.0), [], [ones_bf])
    P.add("pool", lambda e: e.memset(ones_f[:], 1.0), [], [ones_f])

    with ExitStack() as es0:
        srep = P.sb("srep", [128, 8, 128], F32, es0)
        lamp = P.sb("lamp", [128, 128], F32, es0)
        lams = P.sb("lams", [128, 2], F32, es0)
        wblk = [P.sb("wblk%d" % i, [128, 8, 512], F32, es0) for i in range(3)]
        wsem = [P.dsem("wsem%d" % i, es0) for i in range(3)]
        pcol = P.sb("pcol", [128, 32], F32, es0)
        ctmp = P.sb("ctmp", [128, 128], F32, es0)
        identf = P.sb("identf", [128, 128], F32, es0)
        prow = [P.ps("prow%d" % i, [128, 512], F32, es0) for i in range(2)]


        P.add("act", lambda e: e.activation(out=scol[:], in_=ccol[:], func=AF.Tanh, scale=0.5),
              [ccol], [scol])
        P.add("dve", lambda e: e.tensor_scalar(out=scol[:], in0=scol[:], scalar1=1.0, scalar2=0.5,
                                               op0=ALU.add, op1=ALU.mult), [scol], [scol])
        P.add("dve", lambda e: e.tensor_tensor(out=scol[:], in0=scol[:], in1=ccol[:], op=ALU.mult),
              [scol, ccol], [scol])
        for kc in range(8):
            P.add("dve", lambda e, kc=kc: e.tensor_copy(
                out=srep[:, kc, :], in_=scol[:, kc:kc + 1].to_broadcast([128, 128])),
                [scol], [srep])

        P.add("dve", lambda e: e.tensor_tensor(out=lamp[:, 0:64], in0=lamt[:, 0:64], in1=lamt[:, 64:128],
                                               op=ALU.mult), [lamt], [lamp])
        P.add("dve", lambda e: e.tensor_tensor(out=lamp[:, 64:128], in0=lamt[:, 128:192],
                                               in1=lamt[:, 192:256], op=ALU.mult), [lamt], [lamp])
        P.add("dve", lambda e: e.tensor_reduce(out=lams[:], in_=lamp[:].rearrange("p (a b) -> p a b", a=2),
                                               axis=mybir.AxisListType.X, op=ALU.add), [lamp], [lams])
        P.add("act", lambda e: e.activation(out=lams[:], in_=lams[:], func=AF.Exp), [lams], [lams])
        P.add("dve", lambda e: e.tensor_tensor(out=nlam[:], in0=lams[:, 1:2], in1=lams[:, 0:1],
                                               op=ALU.subtract), [lams], [nlam])
        P.add("dve", lambda e: e.tensor_scalar(out=nlam[:], in0=nlam[:], scalar1=-LAMBDA_INIT,
                                               scalar2=None, op0=ALU.add), [nlam], [nlam])
        P.add("dve", lambda e: e.tensor_scalar(out=subg[:], in0=subg[:], scalar1=1.0 - LAMBDA_INIT,
                                               scalar2=None, op0=ALU.mult), [subg], [subg])

        wv = wada_d.ap().rearrange("(kc p) n -> p kc n", p=128)
        colmap = {0: 0, 1: 0, 2: 1, 3: 1, 6: 2, 7: 2, 8: 3, 9: 3}
        rowmap = {4: (gtm, 0), 5: (gtm, 1), 10: (gtf, 0), 11: (gtf, 1)}
        order = [0, 1, 2, 3, 6, 7, 8, 9, 4, 5, 10, 11]
        P.add("dve", lambda e: e.tensor_copy(out=identf[:], in_=ident[:]), [ident], [identf])
        for i, blk in enumerate(order):
            wb = wblk[i % 3]
            dma("sp", wb[:], wv[:, :, blk * 512:(blk + 1) * 512], writes=[wb], dsem=wsem[i % 3])
            pr = prow[i % 2]
            for kc in range(8):
                P.add("pe", lambda e, wb=wb, kc=kc, pr=pr: e.matmul(
                    pr[:], lhsT=srep[:, kc, :], rhs=wb[:, kc, :], start=(kc == 0), stop=(kc == 7)),
                    [wb, srep], [pr])
            if blk in colmap:
                v = colmap[blk]
                for dc4 in range(4):
                    col = v * 8 + (blk % 2) * 4 + dc4
                    P.add("dve", lambda e, pr=pr, dc4=dc4: e.tensor_tensor(
                        out=ctmp[:], in0=pr[:, dc4 * 128:(dc4 + 1) * 128], in1=identf[:], op=ALU.mult),
                        [pr, identf], [ctmp])
                    P.add("dve", lambda e, col=col: e.tensor_reduce(
                        out=pcol[:, col:col + 1], in_=ctmp[:], axis=mybir.AxisListType.X, op=ALU.add),
                        [ctmp], [pcol])
            else:
                tgt, half = rowmap[blk]
                P.add("dve", lambda e, tgt=tgt, half=half, pr=pr: e.tensor_tensor(
                    out=tgt[:, half * 512:(half + 1) * 512], in0=pr[:],
                    in1=tgt[:, half * 512:(half + 1) * 512], op=ALU.add), [pr, tgt], [tgt])
                P.add("dve", lambda e, tgt=tgt, half=half: e.tensor_scalar(
                    out=tgt[:, half * 512:(half + 1) * 512], in0=tgt[:, half * 512:(half + 1) * 512],
                    scalar1=0.5, scalar2=None, op0=ALU.mult), [tgt], [tgt])
            if i == 7:
                P.add("dve", lambda e: e.tensor_tensor(out=modc[:, 0:16], in0=pcol[:, 0:16],
                                                       in1=bcol[:, 0:16], op=ALU.add),
                      [pcol, bcol], [modc])
                P.add("dve", lambda e: e.tensor_tensor(out=modc[:, 16:32], in0=pcol[:, 16:32],
                                                       in1=bcol[:, 24:40], op=ALU.add),
                      [pcol, bcol], [modc])
                P.add("dve", lambda e: e.scalar_tensor_tensor(out=A1[:], in0=modc[:, 8:16], scalar=1.0,
                                                              in1=gmix[:], op0=ALU.add, op1=ALU.mult),
                      [modc, gmix], [A1])
                P.add("dve", lambda e: e.scalar_tensor_tensor(out=A2[:], in0=modc[:, 24:32], scalar=1.0,
                                                              in1=gffn[:], op0=ALU.add, op1=ALU.mult),
                      [modc, gffn], [A2])

    P.barrier()

    if debug == "p0":
        o1 = dout("d_modc", [128, 32])
        o2 = dout("d_gtm", [128, D])
        o3 = dout("d_A1", [128, 8])
        o4 = dout("d_nlam", [128, 1])
        ds = P.dsem("dbgs")
        fin = []
        for (o, b) in ((o1, modc), (o2, gtm), (o3, A1), (o4, nlam)):
            fin.append(dma("sp", o.ap(), b[:], reads=[b], dsem=ds))
        op = P.add("sp", None)
        op.deps = fin
        with nc.Block() as block:
            P.emit(block)
        es.close()
        return nc, dbg

    on_t = es.enter_context(nc.sbuf_tensor("s_on", [128, NSLOT, 4, 512], BF16))
    on = [Buf("on%d" % j, on_t) for j in range(NSLOT)]
    def norm_tile(xt, rot, Acol, Bcol, psT_b, psT_ap, hdst_b, hdst_fn, st, sqj, xsb, ncols=128,
                  bcol_buf=None):
        ssq, sx, sy, stt = st
        P.add("act", lambda e: e.activation(out=sqj[:], in_=xt[:], func=AF.Square, accum_out=ssq[:]),
              [xt], [sqj, ssq])
        P.add("dve", lambda e: e.tensor_scalar(out=sx[:], in0=ssq[:], scalar1=1.0 / D, scalar2=EPS,
                                               op0=ALU.mult, op1=ALU.add), [ssq], [sx])
        rsqrt("dve", sy, sy[:], sx, sx[:], stt, stt[:])
        P.add("act", lambda e: e.activation(out=xsb[:], in_=xt[:], func=AF.Copy, scale=sy[:, 0:1]),
              [xt, sy], [xsb])
        for kc in range(8):
            P.add("pe", lambda e, kc=kc: e.transpose(out=psT_ap[:, kc * 128:(kc + 1) * 128],
                                                     in_=xsb[:, kc * 128:(kc + 1) * 128],
                                                     identity=ident[:]), [xsb, ident], psT_b)
        bdep = bcol_buf if bcol_buf is not None else Bcol
        for kc in range(8):
            P.add("dve", lambda e, kc=kc: e.tensor_scalar(
                out=hdst_fn(kc), in0=psT_ap[:, kc * 128:kc * 128 + ncols],
                scalar1=Acol[:, kc:kc + 1], scalar2=Bcol[:, kc:kc + 1], op0=ALU.mult, op1=ALU.add),
                list(psT_b) + [Acol, bdep], [hdst_b])

    esA = ExitStack()
    KT = [P.sb("KT%d" % i, [128, S], BF16, esA) for i in range(2)]
    V = P.sb("V", [128, S // 128, 256], BF16, esA)
    wk = P.sb("wk", [128, 8, 256], BF16, esA)
    wvv = P.sb("wv", [128, 8, 256], BF16, esA)
    wq = P.sb("wq", [128, 8, 256], BF16, esA)
    wsemA = [P.dsem("wsA%d" % i, esA) for i in range(3)]
    xt = [P.sb("xt%d" % i, [128, D], F32, esA) for i in range(3)]
    xsem = [P.dsem("xsem%d" % i, esA) for i in range(3)]
    sqj = P.sb("sqj", [128, D], BF16, esA)
    xsb = [P.sb("xsb%d" % i, [128, D], BF16, esA) for i in range(2)]
    hT_t = [esA.enter_context(nc.sbuf_tensor("s_hT%d" % i, [128, 8, 512], BF16)) for i in range(2)]
    hT = [[Buf("hT%d_%d" % (i, r), hT_t[i]) for r in range(4)] for i in range(2)]
    stats = [tuple(P.sb("st%d_%d" % (i, k), [128, 1], F32, esA) for k in range(4)) for i in range(4)]
    QT = [P.sb("QT%d" % i, [128, 2, 512], BF16, esA) for i in range(2)]
    Qpad = [P.sb("Qpad%d" % c, [128, 2, 512], BF16, esA) for c in range(2)]
    for c in range(2):
        P.add("pool", lambda e, c=c: e.memset(Qpad[c][:], 0.0), [], [Qpad[c]])
    Pp = [P.sb("Pp%d" % i, [128, 1024], BF16, esA) for i in range(3)]
    tmask = P.sb("tmask", [128, 8, 512], BF16, esA)
    diag = P.sb("diag", [128, 4, 128], BF16, esA)
    O1s = P.sb("O1s", [128, 512], F32, esA)
    O2s = P.sb("O2s", [128, 512], F32, esA)
    t1 = P.sb("t1", [128, 512], F32, esA)
    rx = P.sb("rx", [128, 512], F32, esA)
    ry = P.sb("ry", [128, 512], F32, esA)
    rt = P.sb("rt", [128, 512], F32, esA)
    pp_t = [esA.enter_context(nc.psum_tensor("p_pp%d" % i, [128, 1024], F32)) for i in range(4)]
    bk = [[Buf("bk%d_%d" % (i, h), pp_t[i]) for h in range(2)] for i in range(4)]

    tsem = P.dsem("tsem", esA)
    o_a = dma("sp", tmask[:], tmask_d.ap().rearrange("t p q -> p t q"), writes=[tmask], dsem=tsem)
    o_b = dma("sp", diag[:], diag_d.ap().rearrange("h p c -> p h c"), writes=[diag], dsem=tsem)
    o_a.dcount = tsem.count
    o_b.dcount = tsem.count

    win_v = win_d.ap().rearrange("(kc p) n -> p kc n", p=128)
    xs_v = xs_d.ap()
    xo_v = xo_d.ap()
    nkts = [4 * (2 * j + 2) for j in range(NSLOT)]
    cb_base = [sum(4 * n for n in nkts[:j]) for j in range(NSLOT)]

    for hg in range(2):
        for (wb_, c0, si) in ((wk, 2048 + hg * 256, 0), (wvv, 2560 + hg * 256, 1), (wq, 1536 + hg * 256, 2)):
            dma("pool", wb_[:], win_v[:, :, c0:c0 + 256], writes=[wb_], dsem=wsemA[si])

        for i in range(S // 128):
            g, r = i // 4, i % 4
            x_b = xt[i % 3]
            dma("sp", x_b[:], xs_v[i * 128:(i + 1) * 128, :], writes=[x_b], dsem=xsem[i % 3])
            hb = hT[g % 2]
            ht = hT_t[g % 2]
            psT_b = bk[i % 2]
            psT_ap = pp_t[i % 2][:].bitcast(BF16)
            norm_tile(x_b, i, A1, modc, psT_b, psT_ap, hb[r],
                      lambda kc, ht=ht, r=r: ht[:, kc, r * 128:(r + 1) * 128],
                      stats[i % 4], sqj, xsb[i % 2])
            pv_b = bk[3][i % 2]
            pv_ap = pp_t[3][:, (i % 2) * 512:(i % 2) * 512 + 256]
            for kc in range(8):
                P.add("pe", lambda e, kc=kc, ht=ht, r=r, pv_ap=pv_ap: e.matmul(
                    pv_ap, lhsT=ht[:, kc, r * 128:(r + 1) * 128], rhs=wvv[:, kc, :],
                    start=(kc == 0), stop=(kc == 7)), [hb[r], wvv], [pv_b])
            P.add("act", lambda e, i=i, pv_ap=pv_ap: e.activation(out=V[:, i, :], in_=pv_ap, func=AF.Copy),
                  [pv_b], [V])
            if r == 3:
                for hl in range(2):
                    pk_b = bk[2][hl]
                    pk_ap = pp_t[2][:, hl * 512:(hl + 1) * 512]
                    for kc in range(8):
                        P.add("pe", lambda e, kc=kc, ht=ht, hl=hl, pk_ap=pk_ap: e.matmul(
                            pk_ap, lhsT=wk[:, kc, hl * 128:(hl + 1) * 128], rhs=ht[:, kc, :],
                            start=(kc == 0), stop=(kc == 7)), hb + [wk], [pk_b])
                    P.add("dve", lambda e, hl=hl, g=g, pk_ap=pk_ap: e.tensor_copy(
                        out=KT[hl][:, g * 512:(g + 1) * 512], in_=pk_ap), [pk_b], [KT[hl]])

        for j in range(NSLOT):
            nkt = nkts[j]
            hb = hT[0]
            ht = hT_t[0]
            for r in range(4):
                ii = j * 4 + r
                x_b = xt[ii % 3]
                dma("sp", x_b[:], xo_v[ii * 128:(ii + 1) * 128, :], writes=[x_b], dsem=xsem[ii % 3])
                norm_tile(x_b, ii, A1, modc, [bk[3][1]], pp_t[3][:, 512:1024].bitcast(BF16), hb[r],
                          lambda kc, r=r: ht[:, kc, r * 128:(r + 1) * 128],
                          stats[ii % 4], sqj, xsb[ii % 2])
            qt = QT[j % 2]
            for hl in range(2):
                F_ap = pp_t[3][:, 512:1024]
                for kc in range(8):
                    P.add("pe", lambda e, kc=kc, hl=hl, F_ap=F_ap: e.matmul(
                        F_ap, lhsT=wq[:, kc, hl * 128:(hl + 1) * 128], rhs=ht[:, kc, :],
                        start=(kc == 0), stop=(kc == 7)), hb + [wq], [bk[3][1]])
                P.add("dve", lambda e, hl=hl, qt=qt, F_ap=F_ap: e.tensor_scalar(
                    out=qt[:, hl, :], in0=F_ap, scalar1=0.125, scalar2=None, op0=ALU.mult),
                    [bk[3][1]], [qt])
                for c in range(2):
                    P.add("dve", lambda e, hl=hl, c=c, F_ap=F_ap: e.tensor_scalar(
                        out=Qpad[c][c * 64:(c + 1) * 64, hl, :], in0=F_ap[c * 64:(c + 1) * 64, :],
                        scalar1=0.125, scalar2=None, op0=ALU.mult), [bk[3][1]], [Qpad[c]])

            for hl in range(2):
                h = 2 * hg + hl
                O1_b, O2_b, d1_b, d2_b = bk[2][0], bk[2][1], bk[3][0], bk[3][1]
                O1_ap, O2_ap = pp_t[2][:, 0:512], pp_t[2][:, 512:1024]
                d1_ap, d2_ap = pp_t[3][:, 0:512], pp_t[3][:, 512:1024]
                F_b, F_ap = d2_b, d2_ap

                def qk(kt, hl=hl, h=h, qt=qt, nkt=nkt):
                    Sb = bk[kt % 2]
                    St = pp_t[kt % 2]
                    tadd = kt >= nkt - 8
                    for c in range(2):
                        P.add("pe", lambda e, c=c: e.matmul(
                            St[:, c * 512:(c + 1) * 512], lhsT=KT[hl][:, kt * 128:(kt + 1) * 128],
                            rhs=Qpad[c][:, hl, :], start=True, stop=not tadd),
                            [KT[hl], Qpad[c]], [Sb[c]])
                    if tadd:
                        tk = kt - (nkt - 8)
                        for c in range(2):
                            P.add("pe", lambda e, c=c: e.matmul(
                                St[:, c * 512:(c + 1) * 512], lhsT=diag[:, h, :], rhs=tmask[:, tk, :],
                                start=False, stop=True), [diag, tmask], [Sb[c]])

                def ex(kt, hl=hl, h=h, j=j, nkt=nkt):
                    Sb = bk[kt % 2]
                    St = pp_t[kt % 2]
                    pb = Pp[kt % 3]
                    ci = cb_base[j] + h * nkt + kt
                    P.add("act", lambda e: e.activation(out=pb[:], in_=St[:], func=AF.Exp,
                                                        bias=cbias[:, ci:ci + 1], scale=1.0),
                          [Sb[0], Sb[1], cbias], [pb])

                def av(kt, hl=hl, nkt=nkt):
                    pb = Pp[kt % 3]
                    first, last = kt == 0, kt == nkt - 1
                    P.add("pe", lambda e: e.matmul(O1_ap, lhsT=V[:, kt, hl * 128:(hl + 1) * 128],
                                                   rhs=pb[:, 0:512], start=first, stop=last),
                          [V, pb], [O1_b])
                    P.add("pe", lambda e: e.matmul(O2_ap, lhsT=V[:, kt, hl * 128:(hl + 1) * 128],
                                                   rhs=pb[:, 512:1024], start=first, stop=last),
                          [V, pb], [O2_b])
                    P.add("pe", lambda e: e.matmul(d1_ap, lhsT=ones_bf[:], rhs=pb[:, 0:512],
                                                   start=first, stop=last), [ones_bf, pb], [d1_b])
                    P.add("pe", lambda e: e.matmul(d2_ap, lhsT=ones_bf[:], rhs=pb[:, 512:1024],
                                                   start=first, stop=last), [ones_bf, pb], [d2_b])

                qk(0)
                qk(1)
                for kt in range(nkt):
                    ex(kt)
                    av(kt)
                    if kt + 2 < nkt:
                        qk(kt + 2)

                P.add("dve", lambda e: e.tensor_copy(out=rx[:], in_=d1_ap), [d1_b], [rx])
                P.add("dve", lambda e: e.tensor_copy(out=rt[:], in_=d2_ap), [d2_b], [rt])
                P.add("dve", lambda e: e.reciprocal(out=rx[:], in_=rx[:]), [rx], [rx])
                P.add("dve", lambda e: e.reciprocal(out=rt[:], in_=rt[:]), [rt], [rt])
                P.add("dve", lambda e: e.tensor_tensor(out=t1[:], in0=rx[:], in1=O1_ap, op=ALU.mult),
                      [rx, O1_b], [t1])
                P.add("dve", lambda e: e.tensor_tensor(out=O2s[:], in0=rt[:], in1=O2_ap, op=ALU.mult),
                      [rt, O2_b], [O2s])
                P.add("dve", lambda e: e.scalar_tensor_tensor(out=O1s[:], in0=O2s[:], scalar=nlam[:, 0:1],
                                                              in1=t1[:], op0=ALU.mult, op1=ALU.add),
                      [O2s, nlam, t1], [O1s])
                P.add("dve", lambda e: e.tensor_tensor(out=O2s[:], in0=O1s[:], in1=O1s[:], op=ALU.mult),
                      [O1s], [O2s])
                P.add("pe", lambda e: e.matmul(F_ap, lhsT=ones_f[:], rhs=O2s[:], start=True, stop=True),
                      [ones_f, O2s], [F_b])
                P.add("dve", lambda e: e.tensor_scalar(out=rx[:], in0=F_ap, scalar1=1.0 / 128, scalar2=EPS,
                                                       op0=ALU.mult, op1=ALU.add), [F_b], [rx])
                rsqrt("dve", ry, ry[:], rx, rx[:], rt, rt[:])
                P.add("dve", lambda e, h=h, j=j: e.scalar_tensor_tensor(
                    out=on_t[:, j, h, :], in0=O1s[:], scalar=subg[:, 0:1], in1=ry[:],
                    op0=ALU.mult, op1=ALU.mult), [O1s, subg, ry], [on[j]])

    if debug == "p2":
        o1 = dout("d_on", [128, NSLOT * 4 * 512], BF16)
        o2 = dout("d_KT", [128, S], BF16)
        o3 = dout("d_V", [128, (S // 128) * 256], BF16)
        ds = P.dsem("dbgs")
        fin = [dma("sp", o1.ap(), on_t[:].rearrange("p a b c -> p (a b c)"), reads=on, dsem=ds),
               dma("sp", o2.ap(), KT[0][:], reads=[KT[0]], dsem=ds),
               dma("sp", o3.ap(), V[:].rearrange("p a b -> p (a b)"), reads=[V], dsem=ds)]
        op = P.add("sp", None)
        op.deps = fin
        with nc.Block() as block:
            P.emit(block)
        esA.close()
        es.close()
        return nc, dbg

    P.barrier()
    esA.close()

    es3 = ExitStack()
    NR = 5
    ring = [P.sb("ring%d" % i, [128, 4096], BF16, es3) for i in range(NR)]
    rsem = [P.dsem("rsem%d" % i, es3) for i in range(NR)]
    ring_n = [0]

    def wload(src_ap, shape_str, **kw):
        i = ring_n[0] % NR
        ring_n[0] += 1
        rb = ring[i]
        n = 1
        for v_ in src_ap.shape[1:]:
            n *= v_
        dst = rb[:, 0:n].rearrange(shape_str, **kw)
        dma("sp", dst, src_ap, reads=list(wbufs.values()), writes=[rb], dsem=rsem[i])
        return rb, dst

    x3 = [P.sb("x3_%d" % i, [128, D], F32, es3) for i in range(4)]
    x3sem = [P.dsem("x3sem%d" % i, es3) for i in range(4)]
    xhb = P.sb("xhb", [128, D], F32, es3)
    xhsem = P.dsem("xhsem", es3)
    sqj3 = P.sb("sqj3", [128, D], BF16, es3)
    xsb3 = [P.sb("xsb3_%d" % i, [128, D], BF16, es3) for i in range(2)]
    stats3 = [tuple(P.sb("st3_%d_%d" % (i, k_), [128, 1], F32, es3) for k_ in range(4)) for i in range(4)]
    h3_t = es3.enter_context(nc.sbuf_tensor("s_h3", [128, 8, 514], BF16))
    h3 = [Buf("h3_%d" % r, h3_t) for r in range(5)]
    usb = P.sb("usb", [128, 514], F32, es3)
    vsb = P.sb("vsb", [128, 514], F32, es3)
    zsb = P.sb("zsb", [128, 512], F32, es3)
    aT = P.sb("aT", [128, 4, 512], BF16, es3)
    tha = [P.sb("tha%d" % i, [128, 512], F32, es3) for i in range(2)]
    m1 = [P.sb("m1_%d" % i, [128, 512], F32, es3) for i in range(2)]
    mT = P.sb("mT", [128, 8, 512], BF16, es3)
    AT = P.sb("AT", [128, 22, 512], BF16, es3)
    tmpr = [P.sb("tmpr%d" % i, [128, 512], F32, es3) for i in range(2)]
    osem = [P.dsem("osem%d" % i, es3) for i in range(4)]
    pb_t = [es3.enter_context(nc.psum_tensor("p_b%d" % i, [128, 512], F32)) for i in range(8)]
    pb = [Buf("pb%d" % i, pb_t[i]) for i in range(8)]
    bank_n = [0]

    def nbank():
        i = bank_n[0] % 8
        bank_n[0] += 1
        return pb[i], pb_t[i]

    win_b = wb_in_d.ap().rearrange("(kc p) n -> p kc n", p=128)
    woa_b = wb_oa_d.ap().rearrange("(cc p) n -> p cc n", p=128)
    wob_b = wb_ob_d.ap().rearrange("(cc p) n -> p cc n", p=128)
    wo_b = wb_o_d.ap().rearrange("(kc p) n -> p kc n", p=128)
    wg_b = wb_g_d.ap().rearrange("(kc p) n -> p kc n", p=128)
    wu_b = wb_u_d.ap().rearrange("(kc p) n -> p kc n", p=128)
    wd_b = wb_d_d.ap().rearrange("(fc p) n -> p fc n", p=128)
    y_v = y_d.ap()
    out_ops = []

    for j in range(NSLOT):
        for r in range(5):
            if r < 4:
                x_b = x3[r]
                dma("pool", x_b[:], xo_v[(j * 4 + r) * 128:(j * 4 + r + 1) * 128, :], writes=[x_b],
                    dsem=x3sem[r])
            else:
                x_b = xhb
                dma("pool", x_b[:], xh_d.ap()[j], writes=[x_b], dsem=xhsem)
            pbk, pbt = nbank()
            if r < 4:
                hfn = lambda kc, r=r: h3_t[:, kc, 2 + r * 128:2 + (r + 1) * 128]
                norm_tile(x_b, r, A1, modc, [pbk], pbt[:].bitcast(BF16), h3[r], hfn,
                          stats3[r % 4], sqj3, xsb3[r % 2])
            else:
                norm_tile(x_b, r, A1, modc, [pbk], pbt[:].bitcast(BF16), h3[4],
                          lambda kc: h3_t[:, kc, 0:2], stats3[0], sqj3, xsb3[0], ncols=2)

        wu_, wu_v = wload(win_b[:, :, 0:512], "p (k n) -> p k n", k=8)
        wgb_, wgb_v = wload(win_b[:, :, 512:1024], "p (k n) -> p k n", k=8)
        wgc_, wgc_v = wload(win_b[:, :, 1024:1536], "p (k n) -> p k n", k=8)
        for cc in range(4):
            bu, tu = nbank()
            bgc, tgc = nbank()
            bgb, tgb = nbank()
            bh, th = nbank()
            for (wb_, wv_, bb, tt, hcol) in ((wu_, wu_v, bu, tu, 0), (wgc_, wgc_v, bgc, tgc, 2),
                                            (wgb_, wgb_v, bgb, tgb, None)):
                for kc in range(8):
                    P.add("pe", lambda e, kc=kc, wv_=wv_, tt=tt, cc=cc: e.matmul(
                        tt[:], lhsT=wv_[:, kc, cc * 128:(cc + 1) * 128], rhs=h3_t[:, kc, 2:514],
                        start=(kc == 0), stop=(kc == 7)), [wb_] + h3[0:4], [bb])
                if hcol is not None:
                    for kc in range(8):
                        P.add("pe", lambda e, kc=kc, wv_=wv_, th=th, cc=cc, hcol=hcol: e.matmul(
                            th[:, hcol:hcol + 2], lhsT=wv_[:, kc, cc * 128:(cc + 1) * 128],
                            rhs=h3_t[:, kc, 0:2], start=(kc == 0), stop=(kc == 7)), [wb_, h3[4]], [bh])
            P.add("act", lambda e, tu=tu: e.activation(out=usb[:, 2:514], in_=tu[:], func=AF.Copy),
                  [bu], [usb])
            P.add("act", lambda e, th=th: e.activation(out=usb[:, 0:2], in_=th[:, 0:2], func=AF.Copy),
                  [bh], [usb])
            P.add("dve", lambda e, tgc=tgc: e.tensor_tensor(out=vsb[:, 2:514], in0=tgc[:], in1=usb[:, 2:514],
                                                            op=ALU.mult), [bgc, usb], [vsb])
            P.add("dve", lambda e, th=th, j=j: e.scalar_tensor_tensor(
                out=vsb[:, 0:2], in0=th[:, 2:4], scalar=hm[:, j:j + 1], in1=usb[:, 0:2],
                op0=ALU.mult, op1=ALU.mult), [bh, hm, usb], [vsb])
            P.add("dve", lambda e, cc=cc: e.tensor_scalar(out=zsb[:], in0=vsb[:, 0:512],
                                                          scalar1=cw[:, cc * 3:cc * 3 + 1], scalar2=None,
                                                          op0=ALU.mult), [vsb, cw], [zsb])
            for i_ in (1, 2):
                P.add("dve", lambda e, cc=cc, i_=i_: e.scalar_tensor_tensor(
                    out=zsb[:], in0=vsb[:, i_:i_ + 512], scalar=cw[:, cc * 3 + i_:cc * 3 + i_ + 1],
                    in1=zsb[:], op0=ALU.mult, op1=ALU.add), [vsb, cw, zsb], [zsb])
            P.add("dve", lambda e, cc=cc, tgb=tgb: e.tensor_tensor(out=aT[:, cc, :], in0=tgb[:], in1=zsb[:],
                                                                   op=ALU.mult), [bgb, zsb], [aT])

        for dh_ in range(2):
            i_ab = ring_n[0] % NR
            ring_n[0] += 1
            wab_ = ring[i_ab]
            wab_v = wab_[:, 0:4096].rearrange("p (a c n) -> p a c n", a=2, c=4)
            oa1 = dma("sp", wab_v[:, 0], woa_b[:, :, dh_ * 512:(dh_ + 1) * 512], reads=list(wbufs.values()),
                      writes=[wab_], dsem=rsem[i_ab])
            oa2 = dma("sp", wab_v[:, 1], wob_b[:, :, dh_ * 512:(dh_ + 1) * 512], reads=list(wbufs.values()),
                      writes=[wab_], dsem=rsem[i_ab], exclude=[oa1])
            oa1.dcount = oa2.dcount
            woa_ = wob_ = wab_
            wga_, wga_v = wload(win_b[:, :, 3072 + dh_ * 512:3072 + (dh_ + 1) * 512], "p (k n) -> p k n", k=8)
            wgb2_, wgb2_v = wload(win_b[:, :, 4096 + dh_ * 512:4096 + (dh_ + 1) * 512], "p (k n) -> p k n", k=8)
            for dc4 in range(4):
                dc = dh_ * 4 + dc4
                bga, tga = nbank()
                bya, tya = nbank()
                bgb_, tgb_ = nbank()
                byb, tyb = nbank()
                for (wb_, wv_, bb, tt) in ((wga_, wga_v, bga, tga), (wgb2_, wgb2_v, bgb_, tgb_)):
                    for kc in range(8):
                        P.add("pe", lambda e, kc=kc, wv_=wv_, tt=tt, dc4=dc4: e.matmul(
                            tt[:], lhsT=wv_[:, kc, dc4 * 128:(dc4 + 1) * 128], rhs=h3_t[:, kc, 2:514],
                            start=(kc == 0), stop=(kc == 7)), [wb_] + h3[0:4], [bb])
                for c4 in range(4):
                    P.add("pe", lambda e, c4=c4, dc4=dc4, tya=tya, wab_v=wab_v: e.matmul(
                        tya[:], lhsT=wab_v[:, 0, c4, dc4 * 128:(dc4 + 1) * 128], rhs=aT[:, c4, :],
                        start=(c4 == 0), stop=(c4 == 3)), [woa_, aT], [bya])
                for c4 in range(4):
                    P.add("pe", lambda e, c4=c4, dc4=dc4, tyb=tyb, j=j, wab_v=wab_v: e.matmul(
                        tyb[:], lhsT=wab_v[:, 1, c4, dc4 * 128:(dc4 + 1) * 128], rhs=on_t[:, j, c4, :],
                        start=(c4 == 0), stop=(c4 == 3)), [wob_, on[j]], [byb])
                P.add("act", lambda e, tga=tga: e.activation(out=tha[0][:], in_=tga[:], func=AF.Tanh, scale=0.5),
                      [bga], [tha[0]])
                P.add("act", lambda e, tgb_=tgb_: e.activation(out=tha[1][:], in_=tgb_[:], func=AF.Tanh, scale=0.5),
                      [bgb_], [tha[1]])
                P.add("dve", lambda e, tya=tya: e.scalar_tensor_tensor(
                    out=m1[0][:], in0=tha[0][:], scalar=1.0, in1=tya[:], op0=ALU.add, op1=ALU.mult),
                    [tha[0], bya], [m1[0]])
                P.add("dve", lambda e, tyb=tyb: e.scalar_tensor_tensor(
                    out=m1[1][:], in0=tha[1][:], scalar=1.0, in1=tyb[:], op0=ALU.add, op1=ALU.mult),
                    [tha[1], byb], [m1[1]])
                P.add("dve", lambda e, dc=dc: e.tensor_tensor(out=mT[:, dc, :], in0=m1[0][:], in1=m1[1][:],
                                                              op=ALU.add), [m1[0], m1[1]], [mT])

        for dh_ in range(2):
            wo_, wo_v = wload(wo_b[:, :, dh_ * 512:(dh_ + 1) * 512], "p (k n) -> p k n", k=8)
            for ts in range(4):
                bo, to = nbank()
                for kc in range(8):
                    P.add("pe", lambda e, kc=kc, ts=ts, to=to, wo_v=wo_v: e.matmul(
                        to[:], lhsT=mT[:, kc, ts * 128:(ts + 1) * 128], rhs=wo_v[:, kc, :],
                        start=(kc == 0), stop=(kc == 7)), [mT, wo_], [bo])
                tr = tmpr[ts % 2]
                P.add("dve", lambda e, to=to, tr=tr, dh_=dh_: e.tensor_tensor(
                    out=tr[:], in0=to[:], in1=gtm[:, dh_ * 512:(dh_ + 1) * 512], op=ALU.mult),
                    [bo, gtm], [tr])
                P.add("dve", lambda e, ts=ts, tr=tr, dh_=dh_: e.tensor_tensor(
                    out=x3[ts][:, dh_ * 512:(dh_ + 1) * 512], in0=x3[ts][:, dh_ * 512:(dh_ + 1) * 512],
                    in1=tr[:], op=ALU.add), [x3[ts], tr], [x3[ts]])

        for r in range(4):
            pbk, pbt = nbank()
            norm_tile(x3[r], r, A2, modc[:, 16:24], [pbk], pbt[:].bitcast(BF16), h3[r],
                      lambda kc, r=r: h3_t[:, kc, 2 + r * 128:2 + (r + 1) * 128],
                      stats3[r % 4], sqj3, xsb3[r % 2], bcol_buf=modc)

        for t6 in range(6):
            nf = 4 if t6 < 5 else 2
            wg_, wg_v = wload(wg_b[:, :, t6 * 512:t6 * 512 + nf * 128], "p (k n) -> p k n", k=8)
            wu2_, wu2_v = wload(wu_b[:, :, t6 * 512:t6 * 512 + nf * 128], "p (k n) -> p k n", k=8)
            for f4 in range(nf):
                fc = t6 * 4 + f4
                bg_, tg_ = nbank()
                bu_, tu_ = nbank()
                for (wb_, wv_, bb, tt) in ((wg_, wg_v, bg_, tg_), (wu2_, wu2_v, bu_, tu_)):
                    for kc in range(8):
                        P.add("pe", lambda e, kc=kc, wv_=wv_, tt=tt, f4=f4: e.matmul(
                            tt[:], lhsT=wv_[:, kc, f4 * 128:(f4 + 1) * 128], rhs=h3_t[:, kc, 2:514],
                            start=(kc == 0), stop=(kc == 7)), [wb_] + h3[0:4], [bb])
                th_ = tha[fc % 2]
                s1_ = m1[fc % 2]
                P.add("act", lambda e, tg_=tg_, th_=th_: e.activation(out=th_[:], in_=tg_[:], func=AF.Tanh,
                                                                     scale=0.5), [bg_], [th_])
                P.add("dve", lambda e, tg_=tg_, th_=th_, s1_=s1_: e.scalar_tensor_tensor(
                    out=s1_[:], in0=th_[:], scalar=1.0, in1=tg_[:], op0=ALU.add, op1=ALU.mult),
                    [th_, bg_], [s1_])
                P.add("dve", lambda e, tu_=tu_, s1_=s1_, fc=fc: e.tensor_tensor(
                    out=AT[:, fc, :], in0=s1_[:], in1=tu_[:], op=ALU.mult), [s1_, bu_], [AT])

        for dh_ in range(2):
            banks = [nbank() for _ in range(4)]
            for t3 in range(3):
                f0 = t3 * 8
                nf = 8 if t3 < 2 else 6
                wd_, wd_v = wload(wd_b[:, f0:f0 + nf, dh_ * 512:(dh_ + 1) * 512], "p (f n) -> p f n", f=nf)
                for ts in range(4):
                    bo, to = banks[ts]
                    for f_ in range(nf):
                        fc = f0 + f_
                        P.add("pe", lambda e, f_=f_, fc=fc, ts=ts, to=to, wd_v=wd_v: e.matmul(
                            to[:], lhsT=AT[:, fc, ts * 128:(ts + 1) * 128], rhs=wd_v[:, f_, :],
                            start=(fc == 0), stop=(fc == 21)), [AT, wd_], [bo])
            for ts in range(4):
                bo, to = banks[ts]
                tr = tmpr[ts % 2]
                P.add("dve", lambda e, to=to, tr=tr, dh_=dh_: e.tensor_tensor(
                    out=tr[:], in0=to[:], in1=gtf[:, dh_ * 512:(dh_ + 1) * 512], op=ALU.mult),
                    [bo, gtf], [tr])
                P.add("dve", lambda e, ts=ts, tr=tr, dh_=dh_: e.tensor_tensor(
                    out=x3[ts][:, dh_ * 512:(dh_ + 1) * 512], in0=x3[ts][:, dh_ * 512:(dh_ + 1) * 512],
                    in1=tr[:], op=ALU.add), [x3[ts], tr], [x3[ts]])
        for ts in range(4):
            ssq, sx, sy, stt_ = stats3[ts]
            P.add("act", lambda e, ts=ts, ssq=ssq: e.activation(out=sqj3[:], in_=x3[ts][:], func=AF.Square,
                                                                accum_out=ssq[:]), [x3[ts]], [sqj3, ssq])
            P.add("dve", lambda e, ssq=ssq, sx=sx: e.tensor_scalar(out=sx[:], in0=ssq[:], scalar1=1.0 / D,
                                                                   scalar2=EPS, op0=ALU.mult, op1=ALU.add),
                  [ssq], [sx])
            rsqrt("dve", sy, sy[:], sx, sx[:], stt_, stt_[:])
            P.add("dve", lambda e, ts=ts, sy=sy: e.scalar_tensor_tensor(
                out=x3[ts][:], in0=x3[ts][:], scalar=sy[:, 0:1], in1=gfin[:], op0=ALU.mult, op1=ALU.mult),
                [x3[ts], sy, gfin], [x3[ts]])
            out_ops.append(dma("pool", y_v[(j * 4 + ts) * 128:(j * 4 + ts + 1) * 128, :], x3[ts][:],
                               reads=[x3[ts]], dsem=x3sem[ts]))

    opf = P.add("sp", None)
    opf.deps = list(out_ops)
    opf2 = P.add("pool", None)
    opf2.deps = list(out_ops)
    with nc.Block() as block:
        P.emit(block)
    es3.close()
    es.close()
    return nc, dbg


def _prep_inputs(inp):
    f32 = np.float32
    x = np.asarray(inp["x"], f32)
    c = np.asarray(inp["c"], f32)
    w_ada = np.ascontiguousarray(np.asarray(inp["w_ada"], f32)[0])
    b_ada = np.asarray(inp["b_ada"], f32)[0]
    shared = {
        "w_ada": w_ada,
        "bcol": np.ascontiguousarray(b_ada.reshape(48, 128).T),
        "brow": np.ascontiguousarray(np.stack([b_ada[2048:3072], b_ada[5120:6144]])),
        "gmix": np.ascontiguousarray(np.asarray(inp["g_mix"], f32)[0].reshape(8, 128).T),
        "gffn": np.ascontiguousarray(np.asarray(inp["g_ffn"], f32)[0].reshape(8, 128).T),
        "gfin": np.ascontiguousarray(np.asarray(inp["g_final"], f32).reshape(1, D)),
        "w_in": np.ascontiguousarray(np.asarray(inp["w_in"], f32)[0]),
        "cw": np.ascontiguousarray(
            np.asarray(inp["conv_w"], f32)[0].reshape(3, 4, 128).transpose(2, 1, 0).reshape(128, 12)),
        "w_out_a": np.ascontiguousarray(np.asarray(inp["w_out_a"], f32)[0]),
        "w_out_b": np.ascontiguousarray(np.asarray(inp["w_out_b"], f32)[0]),
        "w_out": np.ascontiguousarray(np.asarray(inp["w_out"], f32)[0]),
        "w_gate": np.ascontiguousarray(np.asarray(inp["w_gate"], f32)[0]),
        "w_up": np.ascontiguousarray(np.asarray(inp["w_up"], f32)[0]),
        "w_down": np.ascontiguousarray(np.asarray(inp["w_down"], f32)[0]),
        "lam": np.ascontiguousarray(np.stack([np.asarray(inp[k], f32)[0] for k in
                                              ("lambda_q1", "lambda_k1", "lambda_q2", "lambda_k2")])),
        "subg": np.ascontiguousarray(np.asarray(inp["subln_g"], f32)[0].reshape(128, 1)),
        "ident": np.eye(128, dtype=f32).astype(ml_dtypes.bfloat16),
        "diag": np.stack([np.eye(128, dtype=f32) * s for s in SLOPES]).astype(ml_dtypes.bfloat16),
    }
    maps = []
    for core in range(8):
        b, par = core // 2, core % 2
        tiles = [2 * j + 1 for j in range(NSLOT)] if par == 0 else [2 * j for j in range(NSLOT)]
        xb = x[b]
        xo = np.concatenate([xb[t * TQ:(t + 1) * TQ] for t in tiles], axis=0)
        xh = np.zeros((NSLOT, 128, D), f32)
        hmask = np.zeros((128, NSLOT), f32)
        for j, t in enumerate(tiles):
            if t > 0:
                xh[j, 0:2] = xb[t * TQ - 2:t * TQ]
                hmask[:, j] = 1.0
        qoff = 512 if par == 0 else 0
        kk = np.arange(1024)[:, None]
        qq = (np.arange(512) + qoff)[None, :]
        allowed = (kk // 64) <= (qq // 64)
        tm = np.where(allowed, np.where(kk > qq, -2.0 * (kk - qq), 0.0), MASKV).astype(f32)
        tm = tm.reshape(8, 128, 512)
        cb = np.zeros((128, sum(16 * (2 * jj + 2) for jj in range(NSLOT))), f32)
        idx = 0
        for j, t in enumerate(tiles):
            ref = t * TQ + 255
            nkt = 4 * (2 * j + 2)
            for h in range(4):
                for kt in range(nkt):
                    kpos = kt * 128 + np.arange(128)
                    cb[:, idx] = SLOPES[h] * (kpos - ref)
                    idx += 1
        assert idx == cb.shape[1]
        m = dict(shared)
        m.update({
            "xs": np.ascontiguousarray(xb[:S]), "xo": np.ascontiguousarray(xo), "xh": xh, "hm": hmask,
            "ccol": np.ascontiguousarray(c[b].reshape(8, 128).T),
            "tmask": tm.astype(ml_dtypes.bfloat16), "cbias": cb,
        })
        maps.append((m, tiles))
    return maps


def kernel(**inputs):
    maps = _prep_inputs(inputs)
    nc, _ = _build()
    res = run_bass_kernel_spmd(nc, [m for m, _ in maps], core_ids=list(range(8)))
    out = np.zeros((NB, S, D), np.float32)
    for core in range(8):
        y = res.results[core]["y"]
        b = core // 2
        for j, t in enumerate(maps[core][1]):
            out[b, t * TQ:(t + 1) * TQ] = y[j * TQ:(j + 1) * TQ]
    return out
```

```python
import math
from contextlib import ExitStack

import numpy as np
import ml_dtypes

import concourse.bass as bass
import concourse.mybir as mybir
from concourse.bass_utils import run_bass_kernel_spmd

F32 = mybir.dt.float32
BF16 = mybir.dt.bfloat16
I32 = mybir.dt.int32
AF = mybir.ActivationFunctionType
ALU = mybir.AluOpType

D = 1024
S = 8192
NB = 4
DFF = 2816
EPS = 1e-6
LAMBDA_INIT = 0.8 - 0.6 * math.exp(-0.3 * 1)
SLOPES = [2.0 ** (-8.0 * (i + 1) / 4) for i in range(4)]
NSLOT = 8
TQ = 512
MASKV = -float(2 ** 20)


class DSem:
    def __init__(self, sem):
        self.sem = sem
        self.count = 0


class Op:
    __slots__ = ("eng", "fn", "deps", "sig", "sigidx", "dsem", "dcount")

    def __init__(self, eng, fn):
        self.eng = eng
        self.fn = fn
        self.deps = []
        self.sig = False
        self.sigidx = 0
        self.dsem = None
        self.dcount = 0


class Buf:
    def __init__(self, name, t):
        self.name = name
        self.t = t
        self.w = {}
        self.r = {}

    def __getitem__(self, k):
        return self.t[k]


class Prog:
    ENGS = ("pe", "act", "dve", "pool", "sp")
    SYNC_SELF = ("act", "dve", "pool")

    def __init__(self, nc, es):
        self.nc = nc
        self.es = es
        self.q = {e: [] for e in self.ENGS}
        self.esem = {e: es.enter_context(nc.semaphore("es_" + e)) for e in self.ENGS}
        self.bar = {}
        self.nbuf = 0

    def sb(self, name, shape, dtype, es=None):
        t = (es or self.es).enter_context(self.nc.sbuf_tensor("s_" + name, list(shape), dtype))
        return Buf(name, t)

    def ps(self, name, shape, dtype=F32, es=None):
        t = (es or self.es).enter_context(self.nc.psum_tensor("p_" + name, list(shape), dtype))
        return Buf(name, t)

    def dsem(self, name, es=None):
        return DSem((es or self.es).enter_context(self.nc.semaphore(name)))

    def dram(self, name, t):
        return Buf(name, t)

    def add(self, eng, fn, reads=(), writes=(), dsem=None, exclude=()):
        op = Op(eng, fn)
        self.q[eng].append(op)
        if dsem is not None:
            dsem.count += 16
            op.dsem = dsem
            op.dcount = dsem.count
            key = ("d", id(dsem))
        else:
            key = eng
        deps = {}
        for b in reads:
            for d in b.w.values():
                deps[id(d)] = d
        for b in writes:
            for d in b.r.values():
                deps[id(d)] = d
            for d in b.w.values():
                deps[id(d)] = d
        if eng in self.bar:
            for d in self.bar.pop(eng):
                deps[id(d)] = d
        for x_ in exclude:
            deps.pop(id(x_), None)
        op.deps = list(deps.values())
        for b in reads:
            b.r[key] = op
        for b in writes:
            if b.r:
                b.w = {key: op}
                b.r = {}
            else:
                b.w[key] = op
        return op

    def barrier(self, extra=()):
        last = []
        for e in self.ENGS:
            for op in reversed(self.q[e]):
                if op.dsem is None:
                    last.append(op)
                    break
        seen = {}
        for e in self.ENGS:
            for op in self.q[e]:
                if op.dsem is not None:
                    seen[id(op.dsem)] = op
        last += list(seen.values())
        for e in self.ENGS:
            self.bar[e] = list(last)

    def emit(self, block):
        for e in self.ENGS:
            for op in self.q[e]:
                for d in op.deps:
                    if d.dsem is None and (d.eng != op.eng or op.eng in self.SYNC_SELF):
                        d.sig = True
        for e in self.ENGS:
            n = 0
            for op in self.q[e]:
                if op.sig:
                    n += 1
                    op.sigidx = n

        def runner(ename):
            def f(eng):
                waited = {}
                for op in self.q[ename]:
                    need = {}
                    for d in op.deps:
                        if d.dsem is not None:
                            k, v, s = ("d", id(d.dsem)), d.dcount, d.dsem.sem
                        elif d.eng == ename and ename not in self.SYNC_SELF:
                            continue
                        else:
                            k, v, s = d.eng, d.sigidx, self.esem[d.eng]
                        if waited.get(k, 0) < v and need.get(k, (0, None))[0] < v:
                            need[k] = (v, s)
                    for k, (v, s) in need.items():
                        eng.wait_ge(s, v)
                        waited[k] = v
                    if op.fn is None:
                        continue
                    inst = op.fn(eng)
                    if op.dsem is not None:
                        inst.then_inc(op.dsem.sem, 16)
                    elif op.sig:
                        inst.then_inc(self.esem[ename], 1)
            return f

        block.tensor(runner("pe"))
        block.scalar(runner("act"))
        block.vector(runner("dve"))
        block.gpsimd(runner("pool"))
        block.sync(runner("sp"))


def _build(debug=None):
    nc = bass.Bass("TRN2", target_bir_lowering=False)
    es = ExitStack()
    P = Prog(nc, es)

    def din(name, shape, dt=F32):
        return nc.dram_tensor(name, list(shape), dt, kind="ExternalInput")

    xs_d = din("xs", [S, D])
    xo_d = din("xo", [NSLOT * TQ, D])
    xh_d = din("xh", [NSLOT, 128, D])
    hm_d = din("hm", [128, NSLOT])
    cc_d = din("ccol", [128, 8])
    wada_d = din("w_ada", [D, 6 * D])
    bcol_d = din("bcol", [128, 48])
    brow_d = din("brow", [2, D])
    gmix_d = din("gmix", [128, 8])
    gffn_d = din("gffn", [128, 8])
    gfin_d = din("gfin", [1, D])
    win_d = din("w_in", [D, 5120])
    cw_d = din("cw", [128, 12])
    woa_d = din("w_out_a", [512, D])
    wob_d = din("w_out_b", [512, D])
    wo_d = din("w_out", [D, D])
    wg_d = din("w_gate", [D, DFF])
    wu_d = din("w_up", [D, DFF])
    wd_d = din("w_down", [DFF, D])
    lam_d = din("lam", [4, 64])
    subg_d = din("subg", [128, 1])
    ident_d = din("ident", [128, 128], BF16)
    diag_d = din("diag", [4, 128, 128], BF16)
    tmask_d = din("tmask", [8, 128, 512], BF16)
    NCB = sum(16 * (2 * j + 2) for j in range(NSLOT))
    cb_d = din("cbias", [128, NCB])
    y_d = nc.dram_tensor("y", [NSLOT * TQ, D], F32, kind="ExternalOutput")

    wb_in_d = nc.dram_tensor("wb_in", [D, 5120], BF16)
    wb_oa_d = nc.dram_tensor("wb_oa", [512, D], BF16)
    wb_ob_d = nc.dram_tensor("wb_ob", [512, D], BF16)
    wb_o_d = nc.dram_tensor("wb_o", [D, D], BF16)
    wb_g_d = nc.dram_tensor("wb_g", [D, DFF], BF16)
    wb_u_d = nc.dram_tensor("wb_u", [D, DFF], BF16)
    wb_d_d = nc.dram_tensor("wb_d", [DFF, D], BF16)

    dbg = {}

    def dout(name, shape, dt=F32):
        t = nc.dram_tensor(name, list(shape), dt, kind="ExternalOutput")
        dbg[name] = t
        return t

    def dma(q, out, in_, reads=(), writes=(), dsem=None, exclude=(), **kw):
        return P.add(q, lambda e: e.dma_start(out=out, in_=in_, **kw), reads, writes, dsem, exclude)

    def rsqrt(eng, out_b, out_ap, x_b, x_ap, tmp_b, tmp_ap):
        P.add(eng, lambda e: e.tensor_scalar(out=out_ap.bitcast(I32), in0=x_ap.bitcast(I32),
                                             scalar1=-0.5, scalar2=1597463007.0,
                                             op0=ALU.mult, op1=ALU.add),
              [x_b], [out_b])
        for _ in range(3):
            P.add(eng, lambda e: e.tensor_tensor(out=tmp_ap, in0=out_ap, in1=out_ap, op=ALU.mult),
                  [out_b], [tmp_b])
            P.add(eng, lambda e: e.tensor_tensor(out=tmp_ap, in0=tmp_ap, in1=x_ap, op=ALU.mult),
                  [tmp_b, x_b], [tmp_b])
            P.add(eng, lambda e: e.tensor_scalar(out=tmp_ap, in0=tmp_ap, scalar1=-0.5, scalar2=1.5,
                                                 op0=ALU.mult, op1=ALU.add),
                  [tmp_b], [tmp_b])
            P.add(eng, lambda e: e.tensor_tensor(out=out_ap, in0=out_ap, in1=tmp_ap, op=ALU.mult),
                  [out_b, tmp_b], [out_b])

    ident = P.sb("ident", [128, 128], BF16)
    ones_bf = P.sb("ones_bf", [128, 128], BF16)
    ones_f = P.sb("ones_f", [128, 128], F32)
    ccol = P.sb("ccol", [128, 8], F32)
    scol = P.sb("scol", [128, 8], F32)
    bcol = P.sb("bcol", [128, 48], F32)
    gmix = P.sb("gmix", [128, 8], F32)
    gffn = P.sb("gffn", [128, 8], F32)
    modc = P.sb("modc", [128, 32], F32)
    A1 = P.sb("A1", [128, 8], F32)
    A2 = P.sb("A2", [128, 8], F32)
    gtm = P.sb("gtm", [128, D], F32)
    gtf = P.sb("gtf", [128, D], F32)
    gfin = P.sb("gfin", [128, D], F32)
    cw = P.sb("cw", [128, 12], F32)
    hm = P.sb("hm", [128, NSLOT], F32)
    subg = P.sb("subg", [128, 1], F32)
    nlam = P.sb("nlam", [128, 1], F32)
    cbias = P.sb("cbias", [128, NCB], F32)

    cs = P.dsem("cs")
    cops = []
    for (b, d_ap) in ((ident, ident_d.ap()), (ccol, cc_d.ap()), (bcol, bcol_d.ap()),
                      (gmix, gmix_d.ap()), (gffn, gffn_d.ap()), (cw, cw_d.ap()),
                      (hm, hm_d.ap()), (subg, subg_d.ap()), (cbias, cb_d.ap())):
        cops.append(dma("sp", b[:], d_ap, writes=[b], dsem=cs))
    cops.append(dma("sp", gfin[:], gfin_d.ap().partition_broadcast(128), writes=[gfin], dsem=cs))
    cops.append(dma("sp", gtm[:], brow_d.ap()[0:1, :].partition_broadcast(128), writes=[gtm], dsem=cs))
    cops.append(dma("sp", gtf[:], brow_d.ap()[1:2, :].partition_broadcast(128), writes=[gtf], dsem=cs))
    lamt = P.sb("lamt", [128, 256], F32)
    cops.append(dma("sp", lamt[:], lam_d.ap().rearrange("(o a) b -> o (a b)", o=1).partition_broadcast(128),
                    writes=[lamt], dsem=cs))
    for o_ in cops:
        o_.dcount = cs.count

    wcs = P.dsem("wcs")
    wbufs = {}
    wops = []
    for (src, dst, pat, kw) in (
            (win_d, wb_in_d, "k (a n) -> (k a) n", dict(n=1024)),
            (woa_d, wb_oa_d, None, None), (wob_d, wb_ob_d, None, None), (wo_d, wb_o_d, None, None),
            (wg_d, wb_g_d, "k (a n) -> (k a) n", dict(n=1408)),
            (wu_d, wb_u_d, "k (a n) -> (k a) n", dict(n=1408)),
            (wd_d, wb_d_d, None, None)):
        sa, da = src.ap(), dst.ap()
        if pat is not None:
            sa, da = sa.rearrange(pat, **kw), da.rearrange(pat, **kw)
        wbufs[dst.name] = Buf("wscr_" + dst.name, None)
        wops.append(dma("pool", da, sa, writes=[wbufs[dst.name]], dsem=wcs))
    for o_ in wops:
        o_.dcount = wcs.count

    P.add("pool", lambda e: e.memset(ones_bf[:], 1.0), [], [ones_bf])
    P.add("pool", lambda e: e.memset(ones_f[:], 1.0), [], [ones_f])

    with ExitStack() as es0:
        srep = P.sb("srep", [128, 8, 128], F32, es0)
        lamp = P.sb("lamp", [128, 128], F32, es0)
        lams = P.sb("lams", [128, 2], F32, es0)
        wblk = [P.sb("wblk%d" % i, [128, 8, 512], F32, es0) for i in range(3)]
        wsem = [P.dsem("wsem%d" % i, es0) for i in range(3)]
        pcol = P.sb("pcol", [128, 32], F32, es0)
        ctmp = P.sb("ctmp", [128, 128], F32, es0)
        identf = P.sb("identf", [128, 128], F32, es0)
        prow = [P.ps("prow%d" % i, [128, 512], F32, es0) for i in range(2)]


        P.add("act", lambda e: e.activation(out=scol[:], in_=ccol[:], func=AF.Tanh, scale=0.5),
              [ccol], [scol])
        P.add("dve", lambda e: e.tensor_scalar(out=scol[:], in0=scol[:], scalar1=1.0, scalar2=0.5,
                                               op0=ALU.add, op1=ALU.mult), [scol], [scol])
        P.add("dve", lambda e: e.tensor_tensor(out=scol[:], in0=scol[:], in1=ccol[:], op=ALU.mult),
              [scol, ccol], [scol])
        for kc in range(8):
            P.add("dve", lambda e, kc=kc: e.tensor_copy(
                out=srep[:, kc, :], in_=scol[:, kc:kc + 1].to_broadcast([128, 128])),
                [scol], [srep])

        P.add("dve", lambda e: e.tensor_tensor(out=lamp[:, 0:64], in0=lamt[:, 0:64], in1=lamt[:, 64:128],
                                               op=ALU.mult), [lamt], [lamp])
        P.add("dve", lambda e: e.tensor_tensor(out=lamp[:, 64:128], in0=lamt[:, 128:192],
                                               in1=lamt[:, 192:256], op=ALU.mult), [lamt], [lamp])
        P.add("dve", lambda e: e.tensor_reduce(out=lams[:], in_=lamp[:].rearrange("p (a b) -> p a b", a=2),
                                               axis=mybir.AxisListType.X, op=ALU.add), [lamp], [lams])
        P.add("act", lambda e: e.activation(out=lams[:], in_=lams[:], func=AF.Exp), [lams], [lams])
        P.add("dve", lambda e: e.tensor_tensor(out=nlam[:], in0=lams[:, 1:2], in1=lams[:, 0:1],
                                               op=ALU.subtract), [lams], [nlam])
        P.add("dve", lambda e: e.tensor_scalar(out=nlam[:], in0=nlam[:], scalar1=-LAMBDA_INIT,
                                               scalar2=None, op0=ALU.add), [nlam], [nlam])
        P.add("dve", lambda e: e.tensor_scalar(out=subg[:], in0=subg[:], scalar1=1.0 - LAMBDA_INIT,
                                               scalar2=None, op0=ALU.mult), [subg], [subg])

        wv = wada_d.ap().rearrange("(kc p) n -> p kc n", p=128)
        colmap = {0: 0, 1: 0, 2: 1, 3: 1, 6: 2, 7: 2, 8: 3, 9: 3}
        rowmap = {4: (gtm, 0), 5: (gtm, 1), 10: (gtf, 0), 11: (gtf, 1)}
        order = [0, 1, 2, 3, 6, 7, 8, 9, 4, 5, 10, 11]
        P.add("dve", lambda e: e.tensor_copy(out=identf[:], in_=ident[:]), [ident], [identf])
        for i, blk in enumerate(order):
            wb = wblk[i % 3]
            dma("sp", wb[:], wv[:, :, blk * 512:(blk + 1) * 512], writes=[wb], dsem=wsem[i % 3])
            pr = prow[i % 2]
            for kc in range(8):
                P.add("pe", lambda e, wb=wb, kc=kc, pr=pr: e.matmul(
                    pr[:], lhsT=srep[:, kc, :], rhs=wb[:, kc, :], start=(kc == 0), stop=(kc == 7)),
                    [wb, srep], [pr])
            if blk in colmap:
                v = colmap[blk]
                for dc4 in range(4):
                    col = v * 8 + (blk % 2) * 4 + dc4
                    P.add("dve", lambda e, pr=pr, dc4=dc4: e.tensor_tensor(
                        out=ctmp[:], in0=pr[:, dc4 * 128:(dc4 + 1) * 128], in1=identf[:], op=ALU.mult),
                        [pr, identf], [ctmp])
                    P.add("dve", lambda e, col=col: e.tensor_reduce(
                        out=pcol[:, col:col + 1], in_=ctmp[:], axis=mybir.AxisListType.X, op=ALU.add),
                        [ctmp], [pcol])
            else:
                tgt, half = rowmap[blk]
                P.add("dve", lambda e, tgt=tgt, half=half, pr=pr: e.tensor_tensor(
                    out=tgt[:, half * 512:(half + 1) * 512], in0=pr[:],
                    in1=tgt[:, half * 512:(half + 1) * 512], op=ALU.add), [pr, tgt], [tgt])
                P.add("dve", lambda e, tgt=tgt, half=half: e.tensor_scalar(
                    out=tgt[:, half * 512:(half + 1) * 512], in0=tgt[:, half * 512:(half + 1) * 512],
                    scalar1=0.5, scalar2=None, op0=ALU.mult), [tgt], [tgt])
            if i == 7:
                P.add("dve", lambda e: e.tensor_tensor(out=modc[:, 0:16], in0=pcol[:, 0:16],
                                                       in1=bcol[:, 0:16], op=ALU.add),
                      [pcol, bcol], [modc])
                P.add("dve", lambda e: e.tensor_tensor(out=modc[:, 16:32], in0=pcol[:, 16:32],
                                                       in1=bcol[:, 24:40], op=ALU.add),
                      [pcol, bcol], [modc])
                P.add("dve", lambda e: e.scalar_tensor_tensor(out=A1[:], in0=modc[:, 8:16], scalar=1.0,
                                                              in1=gmix[:], op0=ALU.add, op1=ALU.mult),
                      [modc, gmix], [A1])
                P.add("dve", lambda e: e.scalar_tensor_tensor(out=A2[:], in0=modc[:, 24:32], scalar=1.0,
                                                              in1=gffn[:], op0=ALU.add, op1=ALU.mult),
                      [modc, gffn], [A2])

    P.barrier()

    if debug == "p0":
        o1 = dout("d_modc", [128, 32])
        o2 = dout("d_gtm", [128, D])
        o3 = dout("d_A1", [128, 8])
        o4 = dout("d_nlam", [128, 1])
        ds = P.dsem("dbgs")
        fin = []
        for (o, b) in ((o1, modc), (o2, gtm), (o3, A1), (o4, nlam)):
            fin.append(dma("sp", o.ap(), b[:], reads=[b], dsem=ds))
        op = P.add("sp", None)
        op.deps = fin
        with nc.Block() as block:
            P.emit(block)
        es.close()
        return nc, dbg

    on_t = es.enter_context(nc.sbuf_tensor("s_on", [128, NSLOT, 4, 512], BF16))
    on = [Buf("on%d" % j, on_t) for j in range(NSLOT)]
    def norm_a(xt, st, sqj, xsb):
        ssq, sx, sy, stt = st
        P.add("act", lambda e: e.activation(out=sqj[:], in_=xt[:], func=AF.Square, accum_out=ssq[:]),
              [xt], [sqj, ssq])
        P.add("dve", lambda e: e.tensor_scalar(out=sx[:], in0=ssq[:], scalar1=1.0 / D, scalar2=EPS,
                                               op0=ALU.mult, op1=ALU.add), [ssq], [sx])
        rsqrt("dve", sy, sy[:], sx, sx[:], stt, stt[:])
        P.add("act", lambda e: e.activation(out=xsb[:], in_=xt[:], func=AF.Copy, scale=sy[:, 0:1]),
              [xt, sy], [xsb])

    def norm_b(xsb, Acol, Bcol, psT_b, psT_ap, hdst_b, hdst_fn, ncols=128, bcol_buf=None):
        for kc in range(8):
            P.add("pe", lambda e, kc=kc: e.transpose(out=psT_ap[:, kc * 128:(kc + 1) * 128],
                                                     in_=xsb[:, kc * 128:(kc + 1) * 128],
                                                     identity=ident[:]), [xsb, ident], psT_b)
        bdep = bcol_buf if bcol_buf is not None else Bcol
        for kc in range(8):
            P.add("dve", lambda e, kc=kc: e.tensor_scalar(
                out=hdst_fn(kc), in0=psT_ap[:, kc * 128:kc * 128 + ncols],
                scalar1=Acol[:, kc:kc + 1], scalar2=Bcol[:, kc:kc + 1], op0=ALU.mult, op1=ALU.add),
                list(psT_b) + [Acol, bdep], [hdst_b])

    esA = ExitStack()
    KT = [P.sb("KT%d" % i, [128, S], BF16, esA) for i in range(2)]
    V = P.sb("V", [128, S // 128, 256], BF16, esA)
    wk = P.sb("wk", [128, 8, 256], BF16, esA)
    wvv = P.sb("wv", [128, 8, 256], BF16, esA)
    wq = P.sb("wq", [128, 8, 256], BF16, esA)
    wsemA = [P.dsem("wsA%d" % i, esA) for i in range(3)]
    NXT = 4
    xt = [P.sb("xt%d" % i, [128, D], F32, esA) for i in range(NXT)]
    xsem = [P.dsem("xsem%d" % i, esA) for i in range(NXT)]
    sqj = P.sb("sqj", [128, D], BF16, esA)
    xsb = [P.sb("xsb%d" % i, [128, D], BF16, esA) for i in range(4)]
    hT_t = [esA.enter_context(nc.sbuf_tensor("s_hT%d" % i, [128, 8, 512], BF16)) for i in range(2)]
    hT = [[Buf("hT%d_%d" % (i, r), hT_t[i]) for r in range(4)] for i in range(2)]
    stats = [tuple(P.sb("st%d_%d" % (i, k), [128, 1], F32, esA) for k in range(4)) for i in range(6)]
    QT = [P.sb("QT%d" % i, [128, 2, 512], BF16, esA) for i in range(2)]
    Qpad = [P.sb("Qpad%d" % c, [128, 2, 512], BF16, esA) for c in range(2)]
    for c in range(2):
        P.add("pool", lambda e, c=c: e.memset(Qpad[c][:], 0.0), [], [Qpad[c]])
    Pp = [P.sb("Pp%d" % i, [128, 1024], BF16, esA) for i in range(3)]
    tmask = P.sb("tmask", [128, 8, 512], BF16, esA)
    diag = P.sb("diag", [128, 4, 128], BF16, esA)
    O1s = P.sb("O1s", [128, 512], F32, esA)
    O2s = P.sb("O2s", [128, 512], F32, esA)
    t1 = P.sb("t1", [128, 512], F32, esA)
    rx = P.sb("rx", [128, 512], F32, esA)
    ry = P.sb("ry", [128, 512], F32, esA)
    rt = P.sb("rt", [128, 512], F32, esA)
    pp_t = [esA.enter_context(nc.psum_tensor("p_pp%d" % i, [128, 1024], F32)) for i in range(4)]
    bk = [[Buf("bk%d_%d" % (i, h), pp_t[i]) for h in range(2)] for i in range(4)]

    tsem = P.dsem("tsem", esA)
    o_a = dma("sp", tmask[:], tmask_d.ap().rearrange("t p q -> p t q"), writes=[tmask], dsem=tsem)
    o_b = dma("sp", diag[:], diag_d.ap().rearrange("h p c -> p h c"), writes=[diag], dsem=tsem)
    o_a.dcount = tsem.count
    o_b.dcount = tsem.count

    win_v = win_d.ap().rearrange("(kc p) n -> p kc n", p=128)
    xs_v = xs_d.ap()
    xo_v = xo_d.ap()
    nkts = [4 * (2 * j + 2) for j in range(NSLOT)]
    cb_base = [sum(4 * n for n in nkts[:j]) for j in range(NSLOT)]

    for hg in range(2):
        for (wb_, c0, si) in ((wk, 2048 + hg * 256, 0), (wvv, 2560 + hg * 256, 1), (wq, 1536 + hg * 256, 2)):
            dma("pool", wb_[:], win_v[:, :, c0:c0 + 256], writes=[wb_], dsem=wsemA[si])

        NT1 = S // 128
        LOOK = 2

        def p1_a(i):
            x_b = xt[i % NXT]
            dma("sp", x_b[:], xs_v[i * 128:(i + 1) * 128, :], writes=[x_b], dsem=xsem[i % NXT])
            norm_a(x_b, stats[i % 6], sqj, xsb[i % 4])

        for i0 in range(min(LOOK, NT1)):
            p1_a(i0)
        for i in range(NT1):
            g, r = i // 4, i % 4
            if i + LOOK < NT1:
                p1_a(i + LOOK)
            hb = hT[g % 2]
            ht = hT_t[g % 2]
            psT_b = bk[i % 2]
            psT_ap = pp_t[i % 2][:].bitcast(BF16)
            norm_b(xsb[i % 4], A1, modc, psT_b, psT_ap, hb[r],
                   lambda kc, ht=ht, r=r: ht[:, kc, r * 128:(r + 1) * 128])
            pv_b = bk[3][i % 2]
            pv_ap = pp_t[3][:, (i % 2) * 512:(i % 2) * 512 + 256]
            for kc in range(8):
                P.add("pe", lambda e, kc=kc, ht=ht, r=r, pv_ap=pv_ap: e.matmul(
                    pv_ap, lhsT=ht[:, kc, r * 128:(r + 1) * 128], rhs=wvv[:, kc, :],
                    start=(kc == 0), stop=(kc == 7)), [hb[r], wvv], [pv_b])
            P.add("act", lambda e, i=i, pv_ap=pv_ap: e.activation(out=V[:, i, :], in_=pv_ap, func=AF.Copy),
                  [pv_b], [V])
            if r == 3:
                for hl in range(2):
                    pk_b = bk[2][hl]
                    pk_ap = pp_t[2][:, hl * 512:(hl + 1) * 512]
                    for kc in range(8):
                        P.add("pe", lambda e, kc=kc, ht=ht, hl=hl, pk_ap=pk_ap: e.matmul(
                            pk_ap, lhsT=wk[:, kc, hl * 128:(hl + 1) * 128], rhs=ht[:, kc, :],
                            start=(kc == 0), stop=(kc == 7)), hb + [wk], [pk_b])
                    P.add("dve", lambda e, hl=hl, g=g, pk_ap=pk_ap: e.tensor_copy(
                        out=KT[hl][:, g * 512:(g + 1) * 512], in_=pk_ap), [pk_b], [KT[hl]])

        for j in range(NSLOT):
            nkt = nkts[j]
            hb = hT[0]
            ht = hT_t[0]
            for r in range(4):
                ii = j * 4 + r
                x_b = xt[ii % NXT]
                dma("sp", x_b[:], xo_v[ii * 128:(ii + 1) * 128, :], writes=[x_b], dsem=xsem[ii % NXT])
                norm_a(x_b, stats[ii % 6], sqj, xsb[r])
            for r in range(4):
                norm_b(xsb[r], A1, modc, [bk[3][1]], pp_t[3][:, 512:1024].bitcast(BF16), hb[r],
                       lambda kc, r=r: ht[:, kc, r * 128:(r + 1) * 128])
            qt = QT[j % 2]
            for hl in range(2):
                F_ap = pp_t[3][:, 512:1024]
                for kc in range(8):
                    P.add("pe", lambda e, kc=kc, hl=hl, F_ap=F_ap: e.matmul(
                        F_ap, lhsT=wq[:, kc, hl * 128:(hl + 1) * 128], rhs=ht[:, kc, :],
                        start=(kc == 0), stop=(kc == 7)), hb + [wq], [bk[3][1]])
                P.add("dve", lambda e, hl=hl, qt=qt, F_ap=F_ap: e.tensor_scalar(
                    out=qt[:, hl, :], in0=F_ap, scalar1=0.125, scalar2=None, op0=ALU.mult),
                    [bk[3][1]], [qt])
                for c in range(2):
                    P.add("dve", lambda e, hl=hl, c=c, F_ap=F_ap: e.tensor_scalar(
                        out=Qpad[c][c * 64:(c + 1) * 64, hl, :], in0=F_ap[c * 64:(c + 1) * 64, :],
                        scalar1=0.125, scalar2=None, op0=ALU.mult), [bk[3][1]], [Qpad[c]])

            for hl in range(2):
                h = 2 * hg + hl
                O1_b, O2_b, d1_b, d2_b = bk[2][0], bk[2][1], bk[3][0], bk[3][1]
                O1_ap, O2_ap = pp_t[2][:, 0:512], pp_t[2][:, 512:1024]
                d1_ap, d2_ap = pp_t[3][:, 0:512], pp_t[3][:, 512:1024]
                F_b, F_ap = d2_b, d2_ap

                def qk(kt, hl=hl, h=h, qt=qt, nkt=nkt):
                    Sb = bk[kt % 2]
                    St = pp_t[kt % 2]
                    tadd = kt >= nkt - 8
                    for c in range(2):
                        P.add("pe", lambda e, c=c: e.matmul(
                            St[:, c * 512:(c + 1) * 512], lhsT=KT[hl][:, kt * 128:(kt + 1) * 128],
                            rhs=Qpad[c][:, hl, :], start=True, stop=not tadd),
                            [KT[hl], Qpad[c]], [Sb[c]])
                    if tadd:
                        tk = kt - (nkt - 8)
                        for c in range(2):
                            P.add("pe", lambda e, c=c: e.matmul(
                                St[:, c * 512:(c + 1) * 512], lhsT=diag[:, h, :], rhs=tmask[:, tk, :],
                                start=False, stop=True), [diag, tmask], [Sb[c]])

                def ex(kt, hl=hl, h=h, j=j, nkt=nkt):
                    Sb = bk[kt % 2]
                    St = pp_t[kt % 2]
                    pb = Pp[kt % 3]
                    ci = cb_base[j] + h * nkt + kt
                    P.add("act", lambda e: e.activation(out=pb[:], in_=St[:], func=AF.Exp,
                                                        bias=cbias[:, ci:ci + 1], scale=1.0),
                          [Sb[0], Sb[1], cbias], [pb])

                def av(kt, hl=hl, nkt=nkt):
                    pb = Pp[kt % 3]
                    first, last = kt == 0, kt == nkt - 1
                    P.add("pe", lambda e: e.matmul(O1_ap, lhsT=V[:, kt, hl * 128:(hl + 1) * 128],
                                                   rhs=pb[:, 0:512], start=first, stop=last),
                          [V, pb], [O1_b])
                    P.add("pe", lambda e: e.matmul(O2_ap, lhsT=V[:, kt, hl * 128:(hl + 1) * 128],
                                                   rhs=pb[:, 512:1024], start=first, stop=last),
                          [V, pb], [O2_b])
                    P.add("pe", lambda e: e.matmul(d1_ap, lhsT=ones_bf[:], rhs=pb[:, 0:512],
                                                   start=first, stop=last), [ones_bf, pb], [d1_b])
                    P.add("pe", lambda e: e.matmul(d2_ap, lhsT=ones_bf[:], rhs=pb[:, 512:1024],
                                                   start=first, stop=last), [ones_bf, pb], [d2_b])

                qk(0)
                qk(1)
                for kt in range(nkt):
                    ex(kt)
                    av(kt)
                    if kt + 2 < nkt:
                        qk(kt + 2)

                P.add("dve", lambda e: e.tensor_copy(out=rx[:], in_=d1_ap), [d1_b], [rx])
                P.add("dve", lambda e: e.tensor_copy(out=rt[:], in_=d2_ap), [d2_b], [rt])
                P.add("dve", lambda e: e.reciprocal(out=rx[:], in_=rx[:]), [rx], [rx])
                P.add("dve", lambda e: e.reciprocal(out=rt[:], in_=rt[:]), [rt], [rt])
                P.add("dve", lambda e: e.tensor_tensor(out=t1[:], in0=rx[:], in1=O1_ap, op=ALU.mult),
                      [rx, O1_b], [t1])
                P.add("dve", lambda e: e.tensor_tensor(out=O2s[:], in0=rt[:], in1=O2_ap, op=ALU.mult),
                      [rt, O2_b], [O2s])
                P.add("dve", lambda e: e.scalar_tensor_tensor(out=O1s[:], in0=O2s[:], scalar=nlam[:, 0:1],
                                                              in1=t1[:], op0=ALU.mult, op1=ALU.add),
                      [O2s, nlam, t1], [O1s])
                P.add("dve", lambda e: e.tensor_tensor(out=O2s[:], in0=O1s[:], in1=O1s[:], op=ALU.mult),
                      [O1s], [O2s])
                P.add("pe", lambda e: e.matmul(F_ap, lhsT=ones_f[:], rhs=O2s[:], start=True, stop=True),
                      [ones_f, O2s], [F_b])
                P.add("dve", lambda e: e.tensor_scalar(out=rx[:], in0=F_ap, scalar1=1.0 / 128, scalar2=EPS,
                                                       op0=ALU.mult, op1=ALU.add), [F_b], [rx])
                rsqrt("dve", ry, ry[:], rx, rx[:], rt, rt[:])
                P.add("dve", lambda e, h=h, j=j: e.scalar_tensor_tensor(
                    out=on_t[:, j, h, :], in0=O1s[:], scalar=subg[:, 0:1], in1=ry[:],
                    op0=ALU.mult, op1=ALU.mult), [O1s, subg, ry], [on[j]])

    if debug == "p2":
        o1 = dout("d_on", [128, NSLOT * 4 * 512], BF16)
        o2 = dout("d_KT", [128, S], BF16)
        o3 = dout("d_V", [128, (S // 128) * 256], BF16)
        ds = P.dsem("dbgs")
        fin = [dma("sp", o1.ap(), on_t[:].rearrange("p a b c -> p (a b c)"), reads=on, dsem=ds),
               dma("sp", o2.ap(), KT[0][:], reads=[KT[0]], dsem=ds),
               dma("sp", o3.ap(), V[:].rearrange("p a b -> p (a b)"), reads=[V], dsem=ds)]
        op = P.add("sp", None)
        op.deps = fin
        with nc.Block() as block:
            P.emit(block)
        esA.close()
        es.close()
        return nc, dbg

    P.barrier()
    esA.close()

    es3 = ExitStack()
    NR = 5
    ring = [P.sb("ring%d" % i, [128, 4096], BF16, es3) for i in range(NR)]
    rsem = [P.dsem("rsem%d" % i, es3) for i in range(NR)]
    ring_n = [0]

    def wload(src_ap, shape_str, **kw):
        i = ring_n[0] % NR
        ring_n[0] += 1
        rb = ring[i]
        n = 1
        for v_ in src_ap.shape[1:]:
            n *= v_
        dst = rb[:, 0:n].rearrange(shape_str, **kw)
        dma("sp", dst, src_ap, reads=list(wbufs.values()), writes=[rb], dsem=rsem[i])
        return rb, dst

    x3 = [P.sb("x3_%d" % i, [128, D], F32, es3) for i in range(4)]
    x3sem = [P.dsem("x3sem%d" % i, es3) for i in range(4)]
    xhb = P.sb("xhb", [128, D], F32, es3)
    xhsem = P.dsem("xhsem", es3)
    sqj3 = P.sb("sqj3", [128, D], BF16, es3)
    xsb3 = [P.sb("xsb3_%d" % i, [128, D], BF16, es3) for i in range(5)]
    stats3 = [tuple(P.sb("st3_%d_%d" % (i, k_), [128, 1], F32, es3) for k_ in range(4)) for i in range(5)]
    h3_t = es3.enter_context(nc.sbuf_tensor("s_h3", [128, 8, 514], BF16))
    h3 = [Buf("h3_%d" % r, h3_t) for r in range(5)]
    usb = P.sb("usb", [128, 514], F32, es3)
    vsb = P.sb("vsb", [128, 514], F32, es3)
    zsb = P.sb("zsb", [128, 512], F32, es3)
    aT = P.sb("aT", [128, 4, 512], BF16, es3)
    tha = [P.sb("tha%d" % i, [128, 512], F32, es3) for i in range(2)]
    m1 = [P.sb("m1_%d" % i, [128, 512], F32, es3) for i in range(2)]
    mT = P.sb("mT", [128, 8, 512], BF16, es3)
    AT = P.sb("AT", [128, 22, 512], BF16, es3)
    tmpr = [P.sb("tmpr%d" % i, [128, 512], F32, es3) for i in range(2)]
    osem = [P.dsem("osem%d" % i, es3) for i in range(4)]
    pb_t = [es3.enter_context(nc.psum_tensor("p_b%d" % i, [128, 512], F32)) for i in range(8)]
    pb = [Buf("pb%d" % i, pb_t[i]) for i in range(8)]
    bank_n = [0]

    def nbank():
        i = bank_n[0] % 8
        bank_n[0] += 1
        return pb[i], pb_t[i]

    win_b = wb_in_d.ap().rearrange("(kc p) n -> p kc n", p=128)
    woa_b = wb_oa_d.ap().rearrange("(cc p) n -> p cc n", p=128)
    wob_b = wb_ob_d.ap().rearrange("(cc p) n -> p cc n", p=128)
    wo_b = wb_o_d.ap().rearrange("(kc p) n -> p kc n", p=128)
    wg_b = wb_g_d.ap().rearrange("(kc p) n -> p kc n", p=128)
    wu_b = wb_u_d.ap().rearrange("(kc p) n -> p kc n", p=128)
    wd_b = wb_d_d.ap().rearrange("(fc p) n -> p fc n", p=128)
    y_v = y_d.ap()
    out_ops = []

    for j in range(NSLOT):
        for r in range(5):
            if r < 4:
                x_b = x3[r]
                dma("pool", x_b[:], xo_v[(j * 4 + r) * 128:(j * 4 + r + 1) * 128, :], writes=[x_b],
                    dsem=x3sem[r])
            else:
                x_b = xhb
                dma("pool", x_b[:], xh_d.ap()[j], writes=[x_b], dsem=xhsem)
            norm_a(x_b, stats3[r], sqj3, xsb3[r])
        for r in range(5):
            pbk, pbt = nbank()
            if r < 4:
                norm_b(xsb3[r], A1, modc, [pbk], pbt[:].bitcast(BF16), h3[r],
                       lambda kc, r=r: h3_t[:, kc, 2 + r * 128:2 + (r + 1) * 128])
            else:
                norm_b(xsb3[4], A1, modc, [pbk], pbt[:].bitcast(BF16), h3[4],
                       lambda kc: h3_t[:, kc, 0:2], ncols=2)

        wu_, wu_v = wload(win_b[:, :, 0:512], "p (k n) -> p k n", k=8)
        wgb_, wgb_v = wload(win_b[:, :, 512:1024], "p (k n) -> p k n", k=8)
        wgc_, wgc_v = wload(win_b[:, :, 1024:1536], "p (k n) -> p k n", k=8)
        for cc in range(4):
            bu, tu = nbank()
            bgc, tgc = nbank()
            bgb, tgb = nbank()
            bh, th = nbank()
            for (wb_, wv_, bb, tt, hcol) in ((wu_, wu_v, bu, tu, 0), (wgc_, wgc_v, bgc, tgc, 2),
                                            (wgb_, wgb_v, bgb, tgb, None)):
                for kc in range(8):
                    P.add("pe", lambda e, kc=kc, wv_=wv_, tt=tt, cc=cc: e.matmul(
                        tt[:], lhsT=wv_[:, kc, cc * 128:(cc + 1) * 128], rhs=h3_t[:, kc, 2:514],
                        start=(kc == 0), stop=(kc == 7)), [wb_] + h3[0:4], [bb])
                if hcol is not None:
                    for kc in range(8):
                        P.add("pe", lambda e, kc=kc, wv_=wv_, th=th, cc=cc, hcol=hcol: e.matmul(
                            th[:, hcol:hcol + 2], lhsT=wv_[:, kc, cc * 128:(cc + 1) * 128],
                            rhs=h3_t[:, kc, 0:2], start=(kc == 0), stop=(kc == 7)), [wb_, h3[4]], [bh])
            P.add("act", lambda e, tu=tu: e.activation(out=usb[:, 2:514], in_=tu[:], func=AF.Copy),
                  [bu], [usb])
            P.add("act", lambda e, th=th: e.activation(out=usb[:, 0:2], in_=th[:, 0:2], func=AF.Copy),
                  [bh], [usb])
            P.add("dve", lambda e, tgc=tgc: e.tensor_tensor(out=vsb[:, 2:514], in0=tgc[:], in1=usb[:, 2:514],
                                                            op=ALU.mult), [bgc, usb], [vsb])
            P.add("dve", lambda e, th=th, j=j: e.scalar_tensor_tensor(
                out=vsb[:, 0:2], in0=th[:, 2:4], scalar=hm[:, j:j + 1], in1=usb[:, 0:2],
                op0=ALU.mult, op1=ALU.mult), [bh, hm, usb], [vsb])
            P.add("dve", lambda e, cc=cc: e.tensor_scalar(out=zsb[:], in0=vsb[:, 0:512],
                                                          scalar1=cw[:, cc * 3:cc * 3 + 1], scalar2=None,
                                                          op0=ALU.mult), [vsb, cw], [zsb])
            for i_ in (1, 2):
                P.add("dve", lambda e, cc=cc, i_=i_: e.scalar_tensor_tensor(
                    out=zsb[:], in0=vsb[:, i_:i_ + 512], scalar=cw[:, cc * 3 + i_:cc * 3 + i_ + 1],
                    in1=zsb[:], op0=ALU.mult, op1=ALU.add), [vsb, cw, zsb], [zsb])
            P.add("dve", lambda e, cc=cc, tgb=tgb: e.tensor_tensor(out=aT[:, cc, :], in0=tgb[:], in1=zsb[:],
                                                                   op=ALU.mult), [bgb, zsb], [aT])

        for dh_ in range(2):
            i_ab = ring_n[0] % NR
            ring_n[0] += 1
            wab_ = ring[i_ab]
            wab_v = wab_[:, 0:4096].rearrange("p (a c n) -> p a c n", a=2, c=4)
            oa1 = dma("sp", wab_v[:, 0], woa_b[:, :, dh_ * 512:(dh_ + 1) * 512], reads=list(wbufs.values()),
                      writes=[wab_], dsem=rsem[i_ab])
            oa2 = dma("sp", wab_v[:, 1], wob_b[:, :, dh_ * 512:(dh_ + 1) * 512], reads=list(wbufs.values()),
                      writes=[wab_], dsem=rsem[i_ab], exclude=[oa1])
            oa1.dcount = oa2.dcount
            woa_ = wob_ = wab_
            wga_, wga_v = wload(win_b[:, :, 3072 + dh_ * 512:3072 + (dh_ + 1) * 512], "p (k n) -> p k n", k=8)
            wgb2_, wgb2_v = wload(win_b[:, :, 4096 + dh_ * 512:4096 + (dh_ + 1) * 512], "p (k n) -> p k n", k=8)
            for dc4 in range(4):
                dc = dh_ * 4 + dc4
                bga, tga = nbank()
                bya, tya = nbank()
                bgb_, tgb_ = nbank()
                byb, tyb = nbank()
                for (wb_, wv_, bb, tt) in ((wga_, wga_v, bga, tga), (wgb2_, wgb2_v, bgb_, tgb_)):
                    for kc in range(8):
                        P.add("pe", lambda e, kc=kc, wv_=wv_, tt=tt, dc4=dc4: e.matmul(
                            tt[:], lhsT=wv_[:, kc, dc4 * 128:(dc4 + 1) * 128], rhs=h3_t[:, kc, 2:514],
                            start=(kc == 0), stop=(kc == 7)), [wb_] + h3[0:4], [bb])
                for c4 in range(4):
                    P.add("pe", lambda e, c4=c4, dc4=dc4, tya=tya, wab_v=wab_v: e.matmul(
                        tya[:], lhsT=wab_v[:, 0, c4, dc4 * 128:(dc4 + 1) * 128], rhs=aT[:, c4, :],
                        start=(c4 == 0), stop=(c4 == 3)), [woa_, aT], [bya])
                for c4 in range(4):
                    P.add("pe", lambda e, c4=c4, dc4=dc4, tyb=tyb, j=j, wab_v=wab_v: e.matmul(
                        tyb[:], lhsT=wab_v[:, 1, c4, dc4 * 128:(dc4 + 1) * 128], rhs=on_t[:, j, c4, :],
                        start=(c4 == 0), stop=(c4 == 3)), [wob_, on[j]], [byb])
                P.add("act", lambda e, tga=tga: e.activation(out=tha[0][:], in_=tga[:], func=AF.Tanh, scale=0.5),
                      [bga], [tha[0]])
                P.add("act", lambda e, tgb_=tgb_: e.activation(out=tha[1][:], in_=tgb_[:], func=AF.Tanh, scale=0.5),
                      [bgb_], [tha[1]])
                P.add("dve", lambda e, tya=tya: e.scalar_tensor_tensor(
                    out=m1[0][:], in0=tha[0][:], scalar=1.0, in1=tya[:], op0=ALU.add, op1=ALU.mult),
                    [tha[0], bya], [m1[0]])
                P.add("dve", lambda e, tyb=tyb: e.scalar_tensor_tensor(
                    out=m1[1][:], in0=tha[1][:], scalar=1.0, in1=tyb[:], op0=ALU.add, op1=ALU.mult),
                    [tha[1], byb], [m1[1]])
                P.add("dve", lambda e, dc=dc: e.tensor_tensor(out=mT[:, dc, :], in0=m1[0][:], in1=m1[1][:],
                                                              op=ALU.add), [m1[0], m1[1]], [mT])

        for dh_ in range(2):
            wo_, wo_v = wload(wo_b[:, :, dh_ * 512:(dh_ + 1) * 512], "p (k n) -> p k n", k=8)
            for ts in range(4):
                bo, to = nbank()
                for kc in range(8):
                    P.add("pe", lambda e, kc=kc, ts=ts, to=to, wo_v=wo_v: e.matmul(
                        to[:], lhsT=mT[:, kc, ts * 128:(ts + 1) * 128], rhs=wo_v[:, kc, :],
                        start=(kc == 0), stop=(kc == 7)), [mT, wo_], [bo])
                tr = tmpr[ts % 2]
                P.add("dve", lambda e, to=to, tr=tr, dh_=dh_: e.tensor_tensor(
                    out=tr[:], in0=to[:], in1=gtm[:, dh_ * 512:(dh_ + 1) * 512], op=ALU.mult),
                    [bo, gtm], [tr])
                P.add("dve", lambda e, ts=ts, tr=tr, dh_=dh_: e.tensor_tensor(
                    out=x3[ts][:, dh_ * 512:(dh_ + 1) * 512], in0=x3[ts][:, dh_ * 512:(dh_ + 1) * 512],
                    in1=tr[:], op=ALU.add), [x3[ts], tr], [x3[ts]])

        for r in range(4):
            norm_a(x3[r], stats3[r], sqj3, xsb3[r])
        for r in range(4):
            pbk, pbt = nbank()
            norm_b(xsb3[r], A2, modc[:, 16:24], [pbk], pbt[:].bitcast(BF16), h3[r],
                   lambda kc, r=r: h3_t[:, kc, 2 + r * 128:2 + (r + 1) * 128], bcol_buf=modc)

        for t6 in range(6):
            nf = 4 if t6 < 5 else 2
            wg_, wg_v = wload(wg_b[:, :, t6 * 512:t6 * 512 + nf * 128], "p (k n) -> p k n", k=8)
            wu2_, wu2_v = wload(wu_b[:, :, t6 * 512:t6 * 512 + nf * 128], "p (k n) -> p k n", k=8)
            for f4 in range(nf):
                fc = t6 * 4 + f4
                bg_, tg_ = nbank()
                bu_, tu_ = nbank()
                for (wb_, wv_, bb, tt) in ((wg_, wg_v, bg_, tg_), (wu2_, wu2_v, bu_, tu_)):
                    for kc in range(8):
                        P.add("pe", lambda e, kc=kc, wv_=wv_, tt=tt, f4=f4: e.matmul(
                            tt[:], lhsT=wv_[:, kc, f4 * 128:(f4 + 1) * 128], rhs=h3_t[:, kc, 2:514],
                            start=(kc == 0), stop=(kc == 7)), [wb_] + h3[0:4], [bb])
                th_ = tha[fc % 2]
                s1_ = m1[fc % 2]
                P.add("act", lambda e, tg_=tg_, th_=th_: e.activation(out=th_[:], in_=tg_[:], func=AF.Tanh,
                                                                     scale=0.5), [bg_], [th_])
                P.add("dve", lambda e, tg_=tg_, th_=th_, s1_=s1_: e.scalar_tensor_tensor(
                    out=s1_[:], in0=th_[:], scalar=1.0, in1=tg_[:], op0=ALU.add, op1=ALU.mult),
                    [th_, bg_], [s1_])
                P.add("dve", lambda e, tu_=tu_, s1_=s1_, fc=fc: e.tensor_tensor(
                    out=AT[:, fc, :], in0=s1_[:], in1=tu_[:], op=ALU.mult), [s1_, bu_], [AT])

        for dh_ in range(2):
            banks = [nbank() for _ in range(4)]
            for t3 in range(3):
                f0 = t3 * 8
                nf = 8 if t3 < 2 else 6
                wd_, wd_v = wload(wd_b[:, f0:f0 + nf, dh_ * 512:(dh_ + 1) * 512], "p (f n) -> p f n", f=nf)
                for ts in range(4):
                    bo, to = banks[ts]
                    for f_ in range(nf):
                        fc = f0 + f_
                        P.add("pe", lambda e, f_=f_, fc=fc, ts=ts, to=to, wd_v=wd_v: e.matmul(
                            to[:], lhsT=AT[:, fc, ts * 128:(ts + 1) * 128], rhs=wd_v[:, f_, :],
                            start=(fc == 0), stop=(fc == 21)), [AT, wd_], [bo])
            for ts in range(4):
                bo, to = banks[ts]
                tr = tmpr[ts % 2]
                P.add("dve", lambda e, to=to, tr=tr, dh_=dh_: e.tensor_tensor(
                    out=tr[:], in0=to[:], in1=gtf[:, dh_ * 512:(dh_ + 1) * 512], op=ALU.mult),
                    [bo, gtf], [tr])
                P.add("dve", lambda e, ts=ts, tr=tr, dh_=dh_: e.tensor_tensor(
                    out=x3[ts][:, dh_ * 512:(dh_ + 1) * 512], in0=x3[ts][:, dh_ * 512:(dh_ + 1) * 512],
                    in1=tr[:], op=ALU.add), [x3[ts], tr], [x3[ts]])
        for ts in range(4):
            ssq, sx, sy, stt_ = stats3[ts]
            P.add("act", lambda e, ts=ts, ssq=ssq: e.activation(out=sqj3[:], in_=x3[ts][:], func=AF.Square,
                                                                accum_out=ssq[:]), [x3[ts]], [sqj3, ssq])
            P.add("dve", lambda e, ssq=ssq, sx=sx: e.tensor_scalar(out=sx[:], in0=ssq[:], scalar1=1.0 / D,
                                                                   scalar2=EPS, op0=ALU.mult, op1=ALU.add),
                  [ssq], [sx])
            rsqrt("dve", sy, sy[:], sx, sx[:], stt_, stt_[:])
            P.add("dve", lambda e, ts=ts, sy=sy: e.scalar_tensor_tensor(
                out=x3[ts][:], in0=x3[ts][:], scalar=sy[:, 0:1], in1=gfin[:], op0=ALU.mult, op1=ALU.mult),
                [x3[ts], sy, gfin], [x3[ts]])
            out_ops.append(dma("pool", y_v[(j * 4 + ts) * 128:(j * 4 + ts + 1) * 128, :], x3[ts][:],
                               reads=[x3[ts]], dsem=x3sem[ts]))

    opf = P.add("sp", None)
    opf.deps = list(out_ops)
    opf2 = P.add("pool", None)
    opf2.deps = list(out_ops)
    with nc.Block() as block:
        P.emit(block)
    es3.close()
    es.close()
    return nc, dbg


def _prep_inputs(inp):
    f32 = np.float32
    x = np.asarray(inp["x"], f32)
    c = np.asarray(inp["c"], f32)
    w_ada = np.ascontiguousarray(np.asarray(inp["w_ada"], f32)[0])
    b_ada = np.asarray(inp["b_ada"], f32)[0]
    shared = {
        "w_ada": w_ada,
        "bcol": np.ascontiguousarray(b_ada.reshape(48, 128).T),
        "brow": np.ascontiguousarray(np.stack([b_ada[2048:3072], b_ada[5120:6144]])),
        "gmix": np.ascontiguousarray(np.asarray(inp["g_mix"], f32)[0].reshape(8, 128).T),
        "gffn": np.ascontiguousarray(np.asarray(inp["g_ffn"], f32)[0].reshape(8, 128).T),
        "gfin": np.ascontiguousarray(np.asarray(inp["g_final"], f32).reshape(1, D)),
        "w_in": np.ascontiguousarray(np.asarray(inp["w_in"], f32)[0]),
        "cw": np.ascontiguousarray(
            np.asarray(inp["conv_w"], f32)[0].reshape(3, 4, 128).transpose(2, 1, 0).reshape(128, 12)),
        "w_out_a": np.ascontiguousarray(np.asarray(inp["w_out_a"], f32)[0]),
        "w_out_b": np.ascontiguousarray(np.asarray(inp["w_out_b"], f32)[0]),
        "w_out": np.ascontiguousarray(np.asarray(inp["w_out"], f32)[0]),
        "w_gate": np.ascontiguousarray(np.asarray(inp["w_gate"], f32)[0]),
        "w_up": np.ascontiguousarray(np.asarray(inp["w_up"], f32)[0]),
        "w_down": np.ascontiguousarray(np.asarray(inp["w_down"], f32)[0]),
        "lam": np.ascontiguousarray(np.stack([np.asarray(inp[k], f32)[0] for k in
                                              ("lambda_q1", "lambda_k1", "lambda_q2", "lambda_k2")])),
        "subg": np.ascontiguousarray(np.asarray(inp["subln_g"], f32)[0].reshape(128, 1)),
        "ident": np.eye(128, dtype=f32).astype(ml_dtypes.bfloat16),
        "diag": np.stack([np.eye(128, dtype=f32) * s for s in SLOPES]).astype(ml_dtypes.bfloat16),
    }
    maps = []
    for core in range(8):
        b, par = core // 2, core % 2
        tiles = [2 * j + 1 for j in range(NSLOT)] if par == 0 else [2 * j for j in range(NSLOT)]
        xb = x[b]
        xo = np.concatenate([xb[t * TQ:(t + 1) * TQ] for t in tiles], axis=0)
        xh = np.zeros((NSLOT, 128, D), f32)
        hmask = np.zeros((128, NSLOT), f32)
        for j, t in enumerate(tiles):
            if t > 0:
                xh[j, 0:2] = xb[t * TQ - 2:t * TQ]
                hmask[:, j] = 1.0
        qoff = 512 if par == 0 else 0
        kk = np.arange(1024)[:, None]
        qq = (np.arange(512) + qoff)[None, :]
        allowed = (kk // 64) <= (qq // 64)
        tm = np.where(allowed, np.where(kk > qq, -2.0 * (kk - qq), 0.0), MASKV).astype(f32)
        tm = tm.reshape(8, 128, 512)
        cb = np.zeros((128, sum(16 * (2 * jj + 2) for jj in range(NSLOT))), f32)
        idx = 0
        for j, t in enumerate(tiles):
            ref = t * TQ + 255
            nkt = 4 * (2 * j + 2)
            for h in range(4):
                for kt in range(nkt):
                    kpos = kt * 128 + np.arange(128)
                    cb[:, idx] = SLOPES[h] * (kpos - ref)
                    idx += 1
        assert idx == cb.shape[1]
        m = dict(shared)
        m.update({
            "xs": np.ascontiguousarray(xb[:S]), "xo": np.ascontiguousarray(xo), "xh": xh, "hm": hmask,
            "ccol": np.ascontiguousarray(c[b].reshape(8, 128).T),
            "tmask": tm.astype(ml_dtypes.bfloat16), "cbias": cb,
        })
        maps.append((m, tiles))
    return maps


def kernel(**inputs):
    maps = _prep_inputs(inputs)
    nc, _ = _build()
    res = run_bass_kernel_spmd(nc, [m for m, _ in maps], core_ids=list(range(8)))
    out = np.zeros((NB, S, D), np.float32)
    for core in range(8):
        y = res.results[core]["y"]
        b = core // 2
        for j, t in enumerate(maps[core][1]):
            out[b, t * TQ:(t + 1) * TQ] = y[j * TQ:(j + 1) * TQ]
    return out
```

```python
import math
from contextlib import ExitStack

import numpy as np
import ml_dtypes

import concourse.bass as bass
import concourse.mybir as mybir
from concourse.bass_utils import run_bass_kernel_spmd

F32 = mybir.dt.float32
BF16 = mybir.dt.bfloat16
I32 = mybir.dt.int32
AF = mybir.ActivationFunctionType
ALU = mybir.AluOpType

D = 1024
S = 8192
NB = 4
DFF = 2816
EPS = 1e-6
LAMBDA_INIT = 0.8 - 0.6 * math.exp(-0.3 * 1)
SLOPES = [2.0 ** (-8.0 * (i + 1) / 4) for i in range(4)]
NSLOT = 8
TQ = 512
MASKV = -float(2 ** 20)


class DSem:
    def __init__(self, sem):
        self.sem = sem
        self.count = 0


class Op:
    __slots__ = ("eng", "fn", "deps", "sig", "sigidx", "dsem", "dcount")

    def __init__(self, eng, fn):
        self.eng = eng
        self.fn = fn
        self.deps = []
        self.sig = False
        self.sigidx = 0
        self.dsem = None
        self.dcount = 0


class Buf:
    def __init__(self, name, t):
        self.name = name
        self.t = t
        self.w = {}
        self.r = {}

    def __getitem__(self, k):
        return self.t[k]


class Prog:
    ENGS = ("pe", "act", "dve", "pool", "sp")
    SYNC_SELF = ("act", "dve", "pool")

    def __init__(self, nc, es):
        self.nc = nc
        self.es = es
        self.q = {e: [] for e in self.ENGS}
        self.esem = {e: es.enter_context(nc.semaphore("es_" + e)) for e in self.ENGS}
        self.bar = {}
        self.nbuf = 0

    def sb(self, name, shape, dtype, es=None):
        t = (es or self.es).enter_context(self.nc.sbuf_tensor("s_" + name, list(shape), dtype))
        return Buf(name, t)

    def ps(self, name, shape, dtype=F32, es=None):
        t = (es or self.es).enter_context(self.nc.psum_tensor("p_" + name, list(shape), dtype))
        return Buf(name, t)

    def dsem(self, name, es=None):
        return DSem((es or self.es).enter_context(self.nc.semaphore(name)))

    def dram(self, name, t):
        return Buf(name, t)

    def add(self, eng, fn, reads=(), writes=(), dsem=None, exclude=()):
        op = Op(eng, fn)
        self.q[eng].append(op)
        if dsem is not None:
            dsem.count += 16
            op.dsem = dsem
            op.dcount = dsem.count
            key = ("d", id(dsem))
        else:
            key = eng
        deps = {}
        for b in reads:
            for d in b.w.values():
                deps[id(d)] = d
        for b in writes:
            for d in b.r.values():
                deps[id(d)] = d
            for d in b.w.values():
                deps[id(d)] = d
        if eng in self.bar:
            for d in self.bar.pop(eng):
                deps[id(d)] = d
        for x_ in exclude:
            deps.pop(id(x_), None)
        op.deps = list(deps.values())
        for b in reads:
            b.r[key] = op
        for b in writes:
            if b.r:
                b.w = {key: op}
                b.r = {}
            else:
                b.w[key] = op
        return op

    def barrier(self, extra=()):
        last = []
        for e in self.ENGS:
            for op in reversed(self.q[e]):
                if op.dsem is None:
                    last.append(op)
                    break
        seen = {}
        for e in self.ENGS:
            for op in self.q[e]:
                if op.dsem is not None:
                    seen[id(op.dsem)] = op
        last += list(seen.values())
        for e in self.ENGS:
            self.bar[e] = list(last)

    def emit(self, block):
        for e in self.ENGS:
            for op in self.q[e]:
                for d in op.deps:
                    if d.dsem is None and (d.eng != op.eng or op.eng in self.SYNC_SELF):
                        d.sig = True
        for e in self.ENGS:
            n = 0
            for op in self.q[e]:
                if op.sig:
                    n += 1
                    op.sigidx = n

        def runner(ename):
            def f(eng):
                waited = {}
                for op in self.q[ename]:
                    need = {}
                    for d in op.deps:
                        if d.dsem is not None:
                            k, v, s = ("d", id(d.dsem)), d.dcount, d.dsem.sem
                        elif d.eng == ename and ename not in self.SYNC_SELF:
                            continue
                        else:
                            k, v, s = d.eng, d.sigidx, self.esem[d.eng]
                        if waited.get(k, 0) < v and need.get(k, (0, None))[0] < v:
                            need[k] = (v, s)
                    for k, (v, s) in need.items():
                        eng.wait_ge(s, v)
                        waited[k] = v
                    if op.fn is None:
                        continue
                    inst = op.fn(eng)
                    if op.dsem is not None:
                        inst.then_inc(op.dsem.sem, 16)
                    elif op.sig:
                        inst.then_inc(self.esem[ename], 1)
            return f

        block.tensor(runner("pe"))
        block.scalar(runner("act"))
        block.vector(runner("dve"))
        block.gpsimd(runner("pool"))
        block.sync(runner("sp"))


def _build(debug=None):
    nc = bass.Bass("TRN2", target_bir_lowering=False)
    es = ExitStack()
    P = Prog(nc, es)

    def din(name, shape, dt=F32):
        return nc.dram_tensor(name, list(shape), dt, kind="ExternalInput")

    xs_d = din("xs", [S, D])
    xo_d = din("xo", [NSLOT * TQ, D])
    xh_d = din("xh", [NSLOT, 128, D])
    hm_d = din("hm", [128, NSLOT])
    cc_d = din("ccol", [128, 8])
    wada_d = din("w_ada", [D, 6 * D])
    bcol_d = din("bcol", [128, 48])
    brow_d = din("brow", [2, D])
    gmix_d = din("gmix", [128, 8])
    gffn_d = din("gffn", [128, 8])
    gfin_d = din("gfin", [1, D])
    win_d = din("w_in", [D, 5120])
    cw_d = din("cw", [128, 12])
    woa_d = din("w_out_a", [512, D])
    wob_d = din("w_out_b", [512, D])
    wo_d = din("w_out", [D, D])
    wg_d = din("w_gate", [D, DFF])
    wu_d = din("w_up", [D, DFF])
    wd_d = din("w_down", [DFF, D])
    lam_d = din("lam", [4, 64])
    subg_d = din("subg", [128, 1])
    ident_d = din("ident", [128, 128], BF16)
    diag_d = din("diag", [4, 128, 128], BF16)
    tmask_d = din("tmask", [8, 128, 512], BF16)
    NCB = sum(16 * (2 * j + 2) for j in range(NSLOT))
    cb_d = din("cbias", [128, NCB])
    y_d = nc.dram_tensor("y", [NSLOT * TQ, D], F32, kind="ExternalOutput")

    wb_in_d = nc.dram_tensor("wb_in", [D, 5120], BF16)
    wb_oa_d = nc.dram_tensor("wb_oa", [512, D], BF16)
    wb_ob_d = nc.dram_tensor("wb_ob", [512, D], BF16)
    wb_o_d = nc.dram_tensor("wb_o", [D, D], BF16)
    wb_g_d = nc.dram_tensor("wb_g", [D, DFF], BF16)
    wb_u_d = nc.dram_tensor("wb_u", [D, DFF], BF16)
    wb_d_d = nc.dram_tensor("wb_d", [DFF, D], BF16)

    dbg = {}

    def dout(name, shape, dt=F32):
        t = nc.dram_tensor(name, list(shape), dt, kind="ExternalOutput")
        dbg[name] = t
        return t

    def dma(q, out, in_, reads=(), writes=(), dsem=None, exclude=(), **kw):
        return P.add(q, lambda e: e.dma_start(out=out, in_=in_, **kw), reads, writes, dsem, exclude)

    def rsqrt(eng, out_b, out_ap, x_b, x_ap, tmp_b, tmp_ap):
        P.add(eng, lambda e: e.tensor_scalar(out=out_ap.bitcast(I32), in0=x_ap.bitcast(I32),
                                             scalar1=-0.5, scalar2=1597463007.0,
                                             op0=ALU.mult, op1=ALU.add),
              [x_b], [out_b])
        for _ in range(3):
            P.add(eng, lambda e: e.tensor_tensor(out=tmp_ap, in0=out_ap, in1=out_ap, op=ALU.mult),
                  [out_b], [tmp_b])
            P.add(eng, lambda e: e.tensor_tensor(out=tmp_ap, in0=tmp_ap, in1=x_ap, op=ALU.mult),
                  [tmp_b, x_b], [tmp_b])
            P.add(eng, lambda e: e.tensor_scalar(out=tmp_ap, in0=tmp_ap, scalar1=-0.5, scalar2=1.5,
                                                 op0=ALU.mult, op1=ALU.add),
                  [tmp_b], [tmp_b])
            P.add(eng, lambda e: e.tensor_tensor(out=out_ap, in0=out_ap, in1=tmp_ap, op=ALU.mult),
                  [out_b, tmp_b], [out_b])

    ident = P.sb("ident", [128, 128], BF16)
    ones_bf = P.sb("ones_bf", [128, 128], BF16)
    ones_f = P.sb("ones_f", [128, 128], F32)
    ccol = P.sb("ccol", [128, 8], F32)
    scol = P.sb("scol", [128, 8], F32)
    bcol = P.sb("bcol", [128, 48], F32)
    gmix = P.sb("gmix", [128, 8], F32)
    gffn = P.sb("gffn", [128, 8], F32)
    modc = P.sb("modc", [128, 32], F32)
    A1 = P.sb("A1", [128, 8], F32)
    A2 = P.sb("A2", [128, 8], F32)
    gtm = P.sb("gtm", [128, D], F32)
    gtf = P.sb("gtf", [128, D], F32)
    gfin = P.sb("gfin", [128, D], F32)
    cw = P.sb("cw", [128, 12], F32)
    hm = P.sb("hm", [128, NSLOT], F32)
    subg = P.sb("subg", [128, 1], F32)
    nlam = P.sb("nlam", [128, 1], F32)
    cbias = P.sb("cbias", [128, NCB], F32)

    cs = P.dsem("cs")
    cops = []
    for (b, d_ap) in ((ident, ident_d.ap()), (ccol, cc_d.ap()), (bcol, bcol_d.ap()),
                      (gmix, gmix_d.ap()), (gffn, gffn_d.ap()), (cw, cw_d.ap()),
                      (hm, hm_d.ap()), (subg, subg_d.ap()), (cbias, cb_d.ap())):
        cops.append(dma("sp", b[:], d_ap, writes=[b], dsem=cs))
    cops.append(dma("sp", gfin[:], gfin_d.ap().partition_broadcast(128), writes=[gfin], dsem=cs))
    cops.append(dma("sp", gtm[:], brow_d.ap()[0:1, :].partition_broadcast(128), writes=[gtm], dsem=cs))
    cops.append(dma("sp", gtf[:], brow_d.ap()[1:2, :].partition_broadcast(128), writes=[gtf], dsem=cs))
    lamt = P.sb("lamt", [128, 256], F32)
    cops.append(dma("sp", lamt[:], lam_d.ap().rearrange("(o a) b -> o (a b)", o=1).partition_broadcast(128),
                    writes=[lamt], dsem=cs))
    for o_ in cops:
        o_.dcount = cs.count

    wcs = P.dsem("wcs")
    wbufs = {}
    wops = []
    for (src, dst, pat, kw) in (
            (win_d, wb_in_d, "k (a n) -> (k a) n", dict(n=1024)),
            (woa_d, wb_oa_d, None, None), (wob_d, wb_ob_d, None, None), (wo_d, wb_o_d, None, None),
            (wg_d, wb_g_d, "k (a n) -> (k a) n", dict(n=1408)),
            (wu_d, wb_u_d, "k (a n) -> (k a) n", dict(n=1408)),
            (wd_d, wb_d_d, None, None)):
        sa, da = src.ap(), dst.ap()
        if pat is not None:
            sa, da = sa.rearrange(pat, **kw), da.rearrange(pat, **kw)
        wbufs[dst.name] = Buf("wscr_" + dst.name, None)
        wops.append(dma("pool", da, sa, writes=[wbufs[dst.name]], dsem=wcs))
    for o_ in wops:
        o_.dcount = wcs.count

    P.add("pool", lambda e: e.memset(ones_bf[:], 1.0), [], [ones_bf])
    P.add("pool", lambda e: e.memset(ones_f[:], 1.0), [], [ones_f])

    with ExitStack() as es0:
        srep = P.sb("srep", [128, 8, 128], F32, es0)
        lamp = P.sb("lamp", [128, 128], F32, es0)
        lams = P.sb("lams", [128, 2], F32, es0)
        wblk = [P.sb("wblk%d" % i, [128, 8, 512], F32, es0) for i in range(3)]
        wsem = [P.dsem("wsem%d" % i, es0) for i in range(3)]
        pcol = P.sb("pcol", [128, 32], F32, es0)
        ctmp = P.sb("ctmp", [128, 128], F32, es0)
        identf = P.sb("identf", [128, 128], F32, es0)
        prow = [P.ps("prow%d" % i, [128, 512], F32, es0) for i in range(2)]


        P.add("act", lambda e: e.activation(out=scol[:], in_=ccol[:], func=AF.Tanh, scale=0.5),
              [ccol], [scol])
        P.add("dve", lambda e: e.tensor_scalar(out=scol[:], in0=scol[:], scalar1=1.0, scalar2=0.5,
                                               op0=ALU.add, op1=ALU.mult), [scol], [scol])
        P.add("dve", lambda e: e.tensor_tensor(out=scol[:], in0=scol[:], in1=ccol[:], op=ALU.mult),
              [scol, ccol], [scol])
        for kc in range(8):
            P.add("dve", lambda e, kc=kc: e.tensor_copy(
                out=srep[:, kc, :], in_=scol[:, kc:kc + 1].to_broadcast([128, 128])),
                [scol], [srep])

        P.add("dve", lambda e: e.tensor_tensor(out=lamp[:, 0:64], in0=lamt[:, 0:64], in1=lamt[:, 64:128],
                                               op=ALU.mult), [lamt], [lamp])
        P.add("dve", lambda e: e.tensor_tensor(out=lamp[:, 64:128], in0=lamt[:, 128:192],
                                               in1=lamt[:, 192:256], op=ALU.mult), [lamt], [lamp])
        P.add("dve", lambda e: e.tensor_reduce(out=lams[:], in_=lamp[:].rearrange("p (a b) -> p a b", a=2),
                                               axis=mybir.AxisListType.X, op=ALU.add), [lamp], [lams])
        P.add("act", lambda e: e.activation(out=lams[:], in_=lams[:], func=AF.Exp), [lams], [lams])
        P.add("dve", lambda e: e.tensor_tensor(out=nlam[:], in0=lams[:, 1:2], in1=lams[:, 0:1],
                                               op=ALU.subtract), [lams], [nlam])
        P.add("dve", lambda e: e.tensor_scalar(out=nlam[:], in0=nlam[:], scalar1=-LAMBDA_INIT,
                                               scalar2=None, op0=ALU.add), [nlam], [nlam])
        P.add("dve", lambda e: e.tensor_scalar(out=subg[:], in0=subg[:], scalar1=1.0 - LAMBDA_INIT,
                                               scalar2=None, op0=ALU.mult), [subg], [subg])

        wv = wada_d.ap().rearrange("(kc p) n -> p kc n", p=128)
        colmap = {0: 0, 1: 0, 2: 1, 3: 1, 6: 2, 7: 2, 8: 3, 9: 3}
        rowmap = {4: (gtm, 0), 5: (gtm, 1), 10: (gtf, 0), 11: (gtf, 1)}
        order = [0, 1, 2, 3, 6, 7, 8, 9, 4, 5, 10, 11]
        P.add("dve", lambda e: e.tensor_copy(out=identf[:], in_=ident[:]), [ident], [identf])
        for i, blk in enumerate(order):
            wb = wblk[i % 3]
            dma("sp", wb[:], wv[:, :, blk * 512:(blk + 1) * 512], writes=[wb], dsem=wsem[i % 3])
            pr = prow[i % 2]
            for kc in range(8):
                P.add("pe", lambda e, wb=wb, kc=kc, pr=pr: e.matmul(
                    pr[:], lhsT=srep[:, kc, :], rhs=wb[:, kc, :], start=(kc == 0), stop=(kc == 7)),
                    [wb, srep], [pr])
            if blk in colmap:
                v = colmap[blk]
                for dc4 in range(4):
                    col = v * 8 + (blk % 2) * 4 + dc4
                    P.add("dve", lambda e, pr=pr, dc4=dc4: e.tensor_tensor(
                        out=ctmp[:], in0=pr[:, dc4 * 128:(dc4 + 1) * 128], in1=identf[:], op=ALU.mult),
                        [pr, identf], [ctmp])
                    P.add("dve", lambda e, col=col: e.tensor_reduce(
                        out=pcol[:, col:col + 1], in_=ctmp[:], axis=mybir.AxisListType.X, op=ALU.add),
                        [ctmp], [pcol])
            else:
                tgt, half = rowmap[blk]
                P.add("dve", lambda e, tgt=tgt, half=half, pr=pr: e.tensor_tensor(
                    out=tgt[:, half * 512:(half + 1) * 512], in0=pr[:],
                    in1=tgt[:, half * 512:(half + 1) * 512], op=ALU.add), [pr, tgt], [tgt])
                P.add("dve", lambda e, tgt=tgt, half=half: e.tensor_scalar(
                    out=tgt[:, half * 512:(half + 1) * 512], in0=tgt[:, half * 512:(half + 1) * 512],
                    scalar1=0.5, scalar2=None, op0=ALU.mult), [tgt], [tgt])
            if i == 7:
                P.add("dve", lambda e: e.tensor_tensor(out=modc[:, 0:16], in0=pcol[:, 0:16],
                                                       in1=bcol[:, 0:16], op=ALU.add),
                      [pcol, bcol], [modc])
                P.add("dve", lambda e: e.tensor_tensor(out=modc[:, 16:32], in0=pcol[:, 16:32],
                                                       in1=bcol[:, 24:40], op=ALU.add),
                      [pcol, bcol], [modc])
                P.add("dve", lambda e: e.scalar_tensor_tensor(out=A1[:], in0=modc[:, 8:16], scalar=1.0,
                                                              in1=gmix[:], op0=ALU.add, op1=ALU.mult),
                      [modc, gmix], [A1])
                P.add("dve", lambda e: e.scalar_tensor_tensor(out=A2[:], in0=modc[:, 24:32], scalar=1.0,
                                                              in1=gffn[:], op0=ALU.add, op1=ALU.mult),
                      [modc, gffn], [A2])

    P.barrier()

    if debug == "p0":
        o1 = dout("d_modc", [128, 32])
        o2 = dout("d_gtm", [128, D])
        o3 = dout("d_A1", [128, 8])
        o4 = dout("d_nlam", [128, 1])
        ds = P.dsem("dbgs")
        fin = []
        for (o, b) in ((o1, modc), (o2, gtm), (o3, A1), (o4, nlam)):
            fin.append(dma("sp", o.ap(), b[:], reads=[b], dsem=ds))
        op = P.add("sp", None)
        op.deps = fin
        with nc.Block() as block:
            P.emit(block)
        es.close()
        return nc, dbg

    on_t = es.enter_context(nc.sbuf_tensor("s_on", [128, NSLOT, 4, 512], BF16))
    on = [Buf("on%d" % j, on_t) for j in range(NSLOT)]
    def norm_a1(xts, st, sqj):
        ssq, sx, sy, stt = st
        for r, xt in enumerate(xts):
            P.add("act", lambda e, r=r, xt=xt: e.activation(out=sqj[:], in_=xt[:], func=AF.Square,
                                                           accum_out=ssq[:, r:r + 1]), [xt], [sqj, ssq])

    def norm_a2(xts, st, xsbs):
        ssq, sx, sy, stt = st
        n = len(xts)
        P.add("dve", lambda e: e.tensor_scalar(out=sx[:, 0:n], in0=ssq[:, 0:n], scalar1=1.0 / D, scalar2=EPS,
                                               op0=ALU.mult, op1=ALU.add), [ssq], [sx])
        rsqrt("dve", sy, sy[:, 0:n], sx, sx[:, 0:n], stt, stt[:, 0:n])
        for r, (xt, xsb) in enumerate(zip(xts, xsbs)):
            P.add("act", lambda e, r=r, xt=xt, xsb=xsb: e.activation(out=xsb[:], in_=xt[:], func=AF.Copy,
                                                                    scale=sy[:, r:r + 1]), [xt, sy], [xsb])

    def norm_b(xsb, Acol, Bcol, psT_b, psT_ap, hdst_b, hdst_fn, ncols=128, bcol_buf=None):
        for kc in range(8):
            P.add("pe", lambda e, kc=kc: e.transpose(out=psT_ap[:, kc * 128:(kc + 1) * 128],
                                                     in_=xsb[:, kc * 128:(kc + 1) * 128],
                                                     identity=ident[:]), [xsb, ident], psT_b)
        bdep = bcol_buf if bcol_buf is not None else Bcol
        for kc in range(8):
            P.add("dve", lambda e, kc=kc: e.tensor_scalar(
                out=hdst_fn(kc), in0=psT_ap[:, kc * 128:kc * 128 + ncols],
                scalar1=Acol[:, kc:kc + 1], scalar2=Bcol[:, kc:kc + 1], op0=ALU.mult, op1=ALU.add),
                list(psT_b) + [Acol, bdep], [hdst_b])

    esA = ExitStack()
    KT = [P.sb("KT%d" % i, [128, S], BF16, esA) for i in range(2)]
    V = P.sb("V", [128, S // 128, 256], BF16, esA)
    wk = P.sb("wk", [128, 8, 256], BF16, esA)
    wvv = P.sb("wv", [128, 8, 256], BF16, esA)
    wq = P.sb("wq", [128, 8, 256], BF16, esA)
    wsemA = [P.dsem("wsA%d" % i, esA) for i in range(3)]
    NXT = 4
    xt = [P.sb("xt%d" % i, [128, D], F32, esA) for i in range(NXT)]
    xsem = [P.dsem("xsem%d" % i, esA) for i in range(NXT)]
    sqj = P.sb("sqj", [128, D], BF16, esA)
    xsb = [P.sb("xsb%d" % i, [128, D], BF16, esA) for i in range(6)]
    hT_t = [esA.enter_context(nc.sbuf_tensor("s_hT%d" % i, [128, 8, 512], BF16)) for i in range(2)]
    hT = [[Buf("hT%d_%d" % (i, r), hT_t[i]) for r in range(4)] for i in range(2)]
    stats = [tuple(P.sb("st%d_%d" % (i, k), [128, 4], F32, esA) for k in range(4)) for i in range(2)]
    QT = [P.sb("QT%d" % i, [128, 2, 512], BF16, esA) for i in range(2)]
    Qpad = [P.sb("Qpad%d" % c, [128, 2, 512], BF16, esA) for c in range(2)]
    for c in range(2):
        P.add("pool", lambda e, c=c: e.memset(Qpad[c][:], 0.0), [], [Qpad[c]])
    Pp = [P.sb("Pp%d" % i, [128, 1024], BF16, esA) for i in range(3)]
    tmask = P.sb("tmask", [128, 8, 512], BF16, esA)
    diag = P.sb("diag", [128, 4, 128], BF16, esA)
    O1s = P.sb("O1s", [128, 512], F32, esA)
    O2s = P.sb("O2s", [128, 512], F32, esA)
    rx = P.sb("rx", [128, 512], F32, esA)
    ry = P.sb("ry", [128, 512], F32, esA)
    rt = P.sb("rt", [128, 512], F32, esA)
    pp_t = [esA.enter_context(nc.psum_tensor("p_pp%d" % i, [128, 1024], F32)) for i in range(4)]
    bk = [[Buf("bk%d_%d" % (i, h), pp_t[i]) for h in range(2)] for i in range(4)]

    tsem = P.dsem("tsem", esA)
    o_a = dma("sp", tmask[:], tmask_d.ap().rearrange("t p q -> p t q"), writes=[tmask], dsem=tsem)
    o_b = dma("sp", diag[:], diag_d.ap().rearrange("h p c -> p h c"), writes=[diag], dsem=tsem)
    o_a.dcount = tsem.count
    o_b.dcount = tsem.count

    win_v = win_d.ap().rearrange("(kc p) n -> p kc n", p=128)
    xs_v = xs_d.ap()
    xo_v = xo_d.ap()
    nkts = [4 * (2 * j + 2) for j in range(NSLOT)]
    cb_base = [sum(4 * n for n in nkts[:j]) for j in range(NSLOT)]

    for hg in range(2):
        for (wb_, c0, si) in ((wk, 2048 + hg * 256, 0), (wvv, 2560 + hg * 256, 1), (wq, 1536 + hg * 256, 2)):
            dma("pool", wb_[:], win_v[:, :, c0:c0 + 256], writes=[wb_], dsem=wsemA[si])

        NG1 = S // 512

        def p1_a1(g):
            xts = []
            for r in range(4):
                i = g * 4 + r
                x_b = xt[r]
                dma("sp", x_b[:], xs_v[i * 128:(i + 1) * 128, :], writes=[x_b], dsem=xsem[r])
                xts.append(x_b)
            norm_a1(xts, stats[g % 2], sqj)

        def p1_a2(g):
            norm_a2([xt[r] for r in range(4)], stats[g % 2], [xsb[(4 * g + r) % 6] for r in range(4)])

        def p1_b(g, r):
            i = g * 4 + r
            hb = hT[g % 2]
            ht = hT_t[g % 2]
            psT_b = bk[i % 2]
            psT_ap = pp_t[i % 2][:].bitcast(BF16)
            norm_b(xsb[(4 * g + r) % 6], A1, modc, psT_b, psT_ap, hb[r],
                   lambda kc, ht=ht, r=r: ht[:, kc, r * 128:(r + 1) * 128])
            pv_b = bk[3][i % 2]
            pv_ap = pp_t[3][:, (i % 2) * 512:(i % 2) * 512 + 256]
            for kc in range(8):
                P.add("pe", lambda e, kc=kc, ht=ht, r=r, pv_ap=pv_ap: e.matmul(
                    pv_ap, lhsT=ht[:, kc, r * 128:(r + 1) * 128], rhs=wvv[:, kc, :],
                    start=(kc == 0), stop=(kc == 7)), [hb[r], wvv], [pv_b])
            P.add("act", lambda e, i=i, pv_ap=pv_ap: e.activation(out=V[:, i, :], in_=pv_ap, func=AF.Copy),
                  [pv_b], [V])

        def p1_k(g):
            hb = hT[g % 2]
            ht = hT_t[g % 2]
            for hl in range(2):
                pk_b = bk[2][hl]
                pk_ap = pp_t[2][:, hl * 512:(hl + 1) * 512]
                for kc in range(8):
                    P.add("pe", lambda e, kc=kc, ht=ht, hl=hl, pk_ap=pk_ap: e.matmul(
                        pk_ap, lhsT=wk[:, kc, hl * 128:(hl + 1) * 128], rhs=ht[:, kc, :],
                        start=(kc == 0), stop=(kc == 7)), hb + [wk], [pk_b])
                P.add("dve", lambda e, hl=hl, g=g, pk_ap=pk_ap: e.tensor_copy(
                    out=KT[hl][:, g * 512:(g + 1) * 512], in_=pk_ap), [pk_b], [KT[hl]])

        p1_a1(0)
        p1_a2(0)
        for g in range(NG1):
            if g + 1 < NG1:
                p1_a1(g + 1)
            p1_b(g, 0)
            p1_b(g, 1)
            if g + 1 < NG1:
                p1_a2(g + 1)
            p1_b(g, 2)
            p1_b(g, 3)
            p1_k(g)

        for j in range(NSLOT):
            nkt = nkts[j]
            hb = hT[0]
            ht = hT_t[0]
            xts = []
            for r in range(4):
                ii = j * 4 + r
                x_b = xt[r]
                dma("sp", x_b[:], xo_v[ii * 128:(ii + 1) * 128, :], writes=[x_b], dsem=xsem[r])
                xts.append(x_b)
            norm_a1(xts, stats[j % 2], sqj)
            norm_a2(xts, stats[j % 2], [xsb[r] for r in range(4)])
            for r in range(4):
                norm_b(xsb[r], A1, modc, [bk[3][1]], pp_t[3][:, 512:1024].bitcast(BF16), hb[r],
                       lambda kc, r=r: ht[:, kc, r * 128:(r + 1) * 128])
            qt = QT[j % 2]
            for hl in range(2):
                F_ap = pp_t[3][:, 512:1024]
                for kc in range(8):
                    P.add("pe", lambda e, kc=kc, hl=hl, F_ap=F_ap: e.matmul(
                        F_ap, lhsT=wq[:, kc, hl * 128:(hl + 1) * 128], rhs=ht[:, kc, :],
                        start=(kc == 0), stop=(kc == 7)), hb + [wq], [bk[3][1]])
                P.add("dve", lambda e, hl=hl, qt=qt, F_ap=F_ap: e.tensor_scalar(
                    out=qt[:, hl, :], in0=F_ap, scalar1=0.125, scalar2=None, op0=ALU.mult),
                    [bk[3][1]], [qt])
                for c in range(2):
                    P.add("dve", lambda e, hl=hl, c=c, F_ap=F_ap: e.tensor_scalar(
                        out=Qpad[c][c * 64:(c + 1) * 64, hl, :], in0=F_ap[c * 64:(c + 1) * 64, :],
                        scalar1=0.125, scalar2=None, op0=ALU.mult), [bk[3][1]], [Qpad[c]])

            for hl in range(2):
                h = 2 * hg + hl
                O1_b, O2_b, d1_b, d2_b = bk[2][0], bk[2][1], bk[3][0], bk[3][1]
                O1_ap, O2_ap = pp_t[2][:, 0:512], pp_t[2][:, 512:1024]
                d1_ap, d2_ap = pp_t[3][:, 0:512], pp_t[3][:, 512:1024]
                F_b, F_ap = d2_b, d2_ap

                def qk(kt, hl=hl, h=h, qt=qt, nkt=nkt):
                    Sb = bk[kt % 2]
                    St = pp_t[kt % 2]
                    tadd = kt >= nkt - 8
                    for c in range(2):
                        P.add("pe", lambda e, c=c: e.matmul(
                            St[:, c * 512:(c + 1) * 512], lhsT=KT[hl][:, kt * 128:(kt + 1) * 128],
                            rhs=Qpad[c][:, hl, :], start=True, stop=not tadd),
                            [KT[hl], Qpad[c]], [Sb[c]])
                    if tadd:
                        tk = kt - (nkt - 8)
                        for c in range(2):
                            P.add("pe", lambda e, c=c: e.matmul(
                                St[:, c * 512:(c + 1) * 512], lhsT=diag[:, h, :], rhs=tmask[:, tk, :],
                                start=False, stop=True), [diag, tmask], [Sb[c]])

                def ex(kt, hl=hl, h=h, j=j, nkt=nkt):
                    Sb = bk[kt % 2]
                    St = pp_t[kt % 2]
                    pb = Pp[kt % 3]
                    ci = cb_base[j] + h * nkt + kt
                    P.add("act", lambda e: e.activation(out=pb[:], in_=St[:], func=AF.Exp,
                                                        bias=cbias[:, ci:ci + 1], scale=1.0),
                          [Sb[0], Sb[1], cbias], [pb])

                def av(kt, hl=hl, nkt=nkt):
                    pb = Pp[kt % 3]
                    first, last = kt == 0, kt == nkt - 1
                    P.add("pe", lambda e: e.matmul(O1_ap, lhsT=V[:, kt, hl * 128:(hl + 1) * 128],
                                                   rhs=pb[:, 0:512], start=first, stop=last),
                          [V, pb], [O1_b])
                    P.add("pe", lambda e: e.matmul(O2_ap, lhsT=V[:, kt, hl * 128:(hl + 1) * 128],
                                                   rhs=pb[:, 512:1024], start=first, stop=last),
                          [V, pb], [O2_b])
                    P.add("pe", lambda e: e.matmul(d1_ap, lhsT=ones_bf[:], rhs=pb[:, 0:512],
                                                   start=first, stop=last), [ones_bf, pb], [d1_b])
                    P.add("pe", lambda e: e.matmul(d2_ap, lhsT=ones_bf[:], rhs=pb[:, 512:1024],
                                                   start=first, stop=last), [ones_bf, pb], [d2_b])

                qk(0)
                qk(1)
                for kt in range(nkt):
                    ex(kt)
                    av(kt)
                    if kt + 2 < nkt:
                        qk(kt + 2)

                P.add("dve", lambda e: e.tensor_copy(out=O1s[:], in_=O1_ap), [O1_b], [O1s])
                P.add("dve", lambda e: e.tensor_copy(out=O2s[:], in_=O2_ap), [O2_b], [O2s])
                P.add("dve", lambda e: e.tensor_copy(out=rx[:], in_=d1_ap), [d1_b], [rx])
                P.add("dve", lambda e: e.tensor_copy(out=rt[:], in_=d2_ap), [d2_b], [rt])
                P.add("dve", lambda e: e.reciprocal(out=rx[:], in_=rx[:]), [rx], [rx])
                P.add("dve", lambda e: e.reciprocal(out=rt[:], in_=rt[:]), [rt], [rt])
                P.add("dve", lambda e: e.tensor_tensor(out=O1s[:], in0=rx[:], in1=O1s[:], op=ALU.mult),
                      [rx, O1s], [O1s])
                P.add("dve", lambda e: e.tensor_tensor(out=O2s[:], in0=rt[:], in1=O2s[:], op=ALU.mult),
                      [rt, O2s], [O2s])
                P.add("dve", lambda e: e.scalar_tensor_tensor(out=O1s[:], in0=O2s[:], scalar=nlam[:, 0:1],
                                                              in1=O1s[:], op0=ALU.mult, op1=ALU.add),
                      [O2s, nlam, O1s], [O1s])
                P.add("dve", lambda e: e.tensor_tensor(out=O2s[:], in0=O1s[:], in1=O1s[:], op=ALU.mult),
                      [O1s], [O2s])
                P.add("pe", lambda e: e.matmul(F_ap, lhsT=ones_f[:], rhs=O2s[:], start=True, stop=True),
                      [ones_f, O2s], [F_b])
                P.add("dve", lambda e: e.tensor_scalar(out=rx[:], in0=F_ap, scalar1=1.0 / 128, scalar2=EPS,
                                                       op0=ALU.mult, op1=ALU.add), [F_b], [rx])
                rsqrt("dve", ry, ry[:], rx, rx[:], rt, rt[:])
                P.add("dve", lambda e, h=h, j=j: e.scalar_tensor_tensor(
                    out=on_t[:, j, h, :], in0=O1s[:], scalar=subg[:, 0:1], in1=ry[:],
                    op0=ALU.mult, op1=ALU.mult), [O1s, subg, ry], [on[j]])

    if debug == "p2":
        o1 = dout("d_on", [128, NSLOT * 4 * 512], BF16)
        o2 = dout("d_KT", [128, S], BF16)
        o3 = dout("d_V", [128, (S // 128) * 256], BF16)
        ds = P.dsem("dbgs")
        fin = [dma("sp", o1.ap(), on_t[:].rearrange("p a b c -> p (a b c)"), reads=on, dsem=ds),
               dma("sp", o2.ap(), KT[0][:], reads=[KT[0]], dsem=ds),
               dma("sp", o3.ap(), V[:].rearrange("p a b -> p (a b)"), reads=[V], dsem=ds)]
        op = P.add("sp", None)
        op.deps = fin
        with nc.Block() as block:
            P.emit(block)
        esA.close()
        es.close()
        return nc, dbg

    P.barrier()
    esA.close()

    es3 = ExitStack()
    NR = 5
    ring = [P.sb("ring%d" % i, [128, 4096], BF16, es3) for i in range(NR)]
    rsem = [P.dsem("rsem%d" % i, es3) for i in range(NR)]
    ring_n = [0]

    def wload(src_ap, shape_str, **kw):
        i = ring_n[0] % NR
        ring_n[0] += 1
        rb = ring[i]
        n = 1
        for v_ in src_ap.shape[1:]:
            n *= v_
        dst = rb[:, 0:n].rearrange(shape_str, **kw)
        dma("sp", dst, src_ap, reads=list(wbufs.values()), writes=[rb], dsem=rsem[i])
        return rb, dst

    x3 = [P.sb("x3_%d" % i, [128, D], F32, es3) for i in range(4)]
    x3sem = [P.dsem("x3sem%d" % i, es3) for i in range(4)]
    xhb = P.sb("xhb", [128, D], F32, es3)
    xhsem = P.dsem("xhsem", es3)
    sqj3 = P.sb("sqj3", [128, D], BF16, es3)
    xsb3 = [P.sb("xsb3_%d" % i, [128, D], BF16, es3) for i in range(5)]
    stats3 = [tuple(P.sb("st3_%d_%d" % (i, k_), [128, 5], F32, es3) for k_ in range(4)) for i in range(3)]
    h3_t = es3.enter_context(nc.sbuf_tensor("s_h3", [128, 8, 514], BF16))
    h3 = [Buf("h3_%d" % r, h3_t) for r in range(5)]
    usb = P.sb("usb", [128, 514], F32, es3)
    vsb = P.sb("vsb", [128, 514], F32, es3)
    zsb = P.sb("zsb", [128, 512], F32, es3)
    aT = P.sb("aT", [128, 4, 512], BF16, es3)
    tha = [P.sb("tha%d" % i, [128, 512], F32, es3) for i in range(2)]
    m1 = [P.sb("m1_%d" % i, [128, 512], F32, es3) for i in range(2)]
    mT = P.sb("mT", [128, 8, 512], BF16, es3)
    AT = P.sb("AT", [128, 22, 512], BF16, es3)
    tmpr = [P.sb("tmpr%d" % i, [128, 512], F32, es3) for i in range(2)]
    osem = [P.dsem("osem%d" % i, es3) for i in range(4)]
    pb_t = [es3.enter_context(nc.psum_tensor("p_b%d" % i, [128, 512], F32)) for i in range(8)]
    pb = [Buf("pb%d" % i, pb_t[i]) for i in range(8)]
    bank_n = [0]

    def nbank():
        i = bank_n[0] % 8
        bank_n[0] += 1
        return pb[i], pb_t[i]

    win_b = wb_in_d.ap().rearrange("(kc p) n -> p kc n", p=128)
    woa_b = wb_oa_d.ap().rearrange("(cc p) n -> p cc n", p=128)
    wob_b = wb_ob_d.ap().rearrange("(cc p) n -> p cc n", p=128)
    wo_b = wb_o_d.ap().rearrange("(kc p) n -> p kc n", p=128)
    wg_b = wb_g_d.ap().rearrange("(kc p) n -> p kc n", p=128)
    wu_b = wb_u_d.ap().rearrange("(kc p) n -> p kc n", p=128)
    wd_b = wb_d_d.ap().rearrange("(fc p) n -> p fc n", p=128)
    y_v = y_d.ap()
    out_ops = []

    for j in range(NSLOT):
        xts = []
        for r in range(5):
            if r < 4:
                x_b = x3[r]
                dma("pool", x_b[:], xo_v[(j * 4 + r) * 128:(j * 4 + r + 1) * 128, :], writes=[x_b],
                    dsem=x3sem[r])
            else:
                x_b = xhb
                dma("pool", x_b[:], xh_d.ap()[j], writes=[x_b], dsem=xhsem)
            xts.append(x_b)
        norm_a1(xts, stats3[0], sqj3)
        norm_a2(xts, stats3[0], xsb3)
        for r in range(5):
            pbk, pbt = nbank()
            if r < 4:
                norm_b(xsb3[r], A1, modc, [pbk], pbt[:].bitcast(BF16), h3[r],
                       lambda kc, r=r: h3_t[:, kc, 2 + r * 128:2 + (r + 1) * 128])
            else:
                norm_b(xsb3[4], A1, modc, [pbk], pbt[:].bitcast(BF16), h3[4],
                       lambda kc: h3_t[:, kc, 0:2], ncols=2)

        wu_, wu_v = wload(win_b[:, :, 0:512], "p (k n) -> p k n", k=8)
        wgb_, wgb_v = wload(win_b[:, :, 512:1024], "p (k n) -> p k n", k=8)
        wgc_, wgc_v = wload(win_b[:, :, 1024:1536], "p (k n) -> p k n", k=8)
        for cc in range(4):
            bu, tu = nbank()
            bgc, tgc = nbank()
            bgb, tgb = nbank()
            bh, th = nbank()
            for (wb_, wv_, bb, tt, hcol) in ((wu_, wu_v, bu, tu, 0), (wgc_, wgc_v, bgc, tgc, 2),
                                            (wgb_, wgb_v, bgb, tgb, None)):
                for kc in range(8):
                    P.add("pe", lambda e, kc=kc, wv_=wv_, tt=tt, cc=cc: e.matmul(
                        tt[:], lhsT=wv_[:, kc, cc * 128:(cc + 1) * 128], rhs=h3_t[:, kc, 2:514],
                        start=(kc == 0), stop=(kc == 7)), [wb_] + h3[0:4], [bb])
                if hcol is not None:
                    for kc in range(8):
                        P.add("pe", lambda e, kc=kc, wv_=wv_, th=th, cc=cc, hcol=hcol: e.matmul(
                            th[:, hcol:hcol + 2], lhsT=wv_[:, kc, cc * 128:(cc + 1) * 128],
                            rhs=h3_t[:, kc, 0:2], start=(kc == 0), stop=(kc == 7)), [wb_, h3[4]], [bh])
            P.add("act", lambda e, tu=tu: e.activation(out=usb[:, 2:514], in_=tu[:], func=AF.Copy),
                  [bu], [usb])
            P.add("act", lambda e, th=th: e.activation(out=usb[:, 0:2], in_=th[:, 0:2], func=AF.Copy),
                  [bh], [usb])
            P.add("dve", lambda e, tgc=tgc: e.tensor_tensor(out=vsb[:, 2:514], in0=tgc[:], in1=usb[:, 2:514],
                                                            op=ALU.mult), [bgc, usb], [vsb])
            P.add("dve", lambda e, th=th, j=j: e.scalar_tensor_tensor(
                out=vsb[:, 0:2], in0=th[:, 2:4], scalar=hm[:, j:j + 1], in1=usb[:, 0:2],
                op0=ALU.mult, op1=ALU.mult), [bh, hm, usb], [vsb])
            P.add("dve", lambda e, cc=cc: e.tensor_scalar(out=zsb[:], in0=vsb[:, 0:512],
                                                          scalar1=cw[:, cc * 3:cc * 3 + 1], scalar2=None,
                                                          op0=ALU.mult), [vsb, cw], [zsb])
            for i_ in (1, 2):
                P.add("dve", lambda e, cc=cc, i_=i_: e.scalar_tensor_tensor(
                    out=zsb[:], in0=vsb[:, i_:i_ + 512], scalar=cw[:, cc * 3 + i_:cc * 3 + i_ + 1],
                    in1=zsb[:], op0=ALU.mult, op1=ALU.add), [vsb, cw, zsb], [zsb])
            P.add("dve", lambda e, cc=cc, tgb=tgb: e.tensor_tensor(out=aT[:, cc, :], in0=tgb[:], in1=zsb[:],
                                                                   op=ALU.mult), [bgb, zsb], [aT])

        for dh_ in range(2):
            i_ab = ring_n[0] % NR
            ring_n[0] += 1
            wab_ = ring[i_ab]
            wab_v = wab_[:, 0:4096].rearrange("p (a c n) -> p a c n", a=2, c=4)
            oa1 = dma("sp", wab_v[:, 0], woa_b[:, :, dh_ * 512:(dh_ + 1) * 512], reads=list(wbufs.values()),
                      writes=[wab_], dsem=rsem[i_ab])
            oa2 = dma("sp", wab_v[:, 1], wob_b[:, :, dh_ * 512:(dh_ + 1) * 512], reads=list(wbufs.values()),
                      writes=[wab_], dsem=rsem[i_ab], exclude=[oa1])
            oa1.dcount = oa2.dcount
            woa_ = wob_ = wab_
            wga_, wga_v = wload(win_b[:, :, 3072 + dh_ * 512:3072 + (dh_ + 1) * 512], "p (k n) -> p k n", k=8)
            wgb2_, wgb2_v = wload(win_b[:, :, 4096 + dh_ * 512:4096 + (dh_ + 1) * 512], "p (k n) -> p k n", k=8)
            for dc4 in range(4):
                dc = dh_ * 4 + dc4
                bga, tga = nbank()
                bya, tya = nbank()
                bgb_, tgb_ = nbank()
                byb, tyb = nbank()
                for (wb_, wv_, bb, tt) in ((wga_, wga_v, bga, tga), (wgb2_, wgb2_v, bgb_, tgb_)):
                    for kc in range(8):
                        P.add("pe", lambda e, kc=kc, wv_=wv_, tt=tt, dc4=dc4: e.matmul(
                            tt[:], lhsT=wv_[:, kc, dc4 * 128:(dc4 + 1) * 128], rhs=h3_t[:, kc, 2:514],
                            start=(kc == 0), stop=(kc == 7)), [wb_] + h3[0:4], [bb])
                for c4 in range(4):
                    P.add("pe", lambda e, c4=c4, dc4=dc4, tya=tya, wab_v=wab_v: e.matmul(
                        tya[:], lhsT=wab_v[:, 0, c4, dc4 * 128:(dc4 + 1) * 128], rhs=aT[:, c4, :],
                        start=(c4 == 0), stop=(c4 == 3)), [woa_, aT], [bya])
                for c4 in range(4):
                    P.add("pe", lambda e, c4=c4, dc4=dc4, tyb=tyb, j=j, wab_v=wab_v: e.matmul(
                        tyb[:], lhsT=wab_v[:, 1, c4, dc4 * 128:(dc4 + 1) * 128], rhs=on_t[:, j, c4, :],
                        start=(c4 == 0), stop=(c4 == 3)), [wob_, on[j]], [byb])
                P.add("act", lambda e, tga=tga: e.activation(out=tha[0][:], in_=tga[:], func=AF.Tanh, scale=0.5),
                      [bga], [tha[0]])
                P.add("act", lambda e, tgb_=tgb_: e.activation(out=tha[1][:], in_=tgb_[:], func=AF.Tanh, scale=0.5),
                      [bgb_], [tha[1]])
                P.add("dve", lambda e, tya=tya: e.scalar_tensor_tensor(
                    out=m1[0][:], in0=tha[0][:], scalar=1.0, in1=tya[:], op0=ALU.add, op1=ALU.mult),
                    [tha[0], bya], [m1[0]])
                P.add("dve", lambda e, tyb=tyb: e.scalar_tensor_tensor(
                    out=m1[1][:], in0=tha[1][:], scalar=1.0, in1=tyb[:], op0=ALU.add, op1=ALU.mult),
                    [tha[1], byb], [m1[1]])
                P.add("dve", lambda e, dc=dc: e.tensor_tensor(out=mT[:, dc, :], in0=m1[0][:], in1=m1[1][:],
                                                              op=ALU.add), [m1[0], m1[1]], [mT])

        for dh_ in range(2):
            wo_, wo_v = wload(wo_b[:, :, dh_ * 512:(dh_ + 1) * 512], "p (k n) -> p k n", k=8)
            for ts in range(4):
                bo, to = nbank()
                for kc in range(8):
                    P.add("pe", lambda e, kc=kc, ts=ts, to=to, wo_v=wo_v: e.matmul(
                        to[:], lhsT=mT[:, kc, ts * 128:(ts + 1) * 128], rhs=wo_v[:, kc, :],
                        start=(kc == 0), stop=(kc == 7)), [mT, wo_], [bo])
                tr = tmpr[ts % 2]
                P.add("dve", lambda e, to=to, tr=tr, dh_=dh_: e.tensor_tensor(
                    out=tr[:], in0=to[:], in1=gtm[:, dh_ * 512:(dh_ + 1) * 512], op=ALU.mult),
                    [bo, gtm], [tr])
                P.add("dve", lambda e, ts=ts, tr=tr, dh_=dh_: e.tensor_tensor(
                    out=x3[ts][:, dh_ * 512:(dh_ + 1) * 512], in0=x3[ts][:, dh_ * 512:(dh_ + 1) * 512],
                    in1=tr[:], op=ALU.add), [x3[ts], tr], [x3[ts]])

        norm_a1(x3, stats3[1], sqj3)
        norm_a2(x3, stats3[1], xsb3[0:4])
        for r in range(4):
            pbk, pbt = nbank()
            norm_b(xsb3[r], A2, modc[:, 16:24], [pbk], pbt[:].bitcast(BF16), h3[r],
                   lambda kc, r=r: h3_t[:, kc, 2 + r * 128:2 + (r + 1) * 128], bcol_buf=modc)

        for t6 in range(6):
            nf = 4 if t6 < 5 else 2
            wg_, wg_v = wload(wg_b[:, :, t6 * 512:t6 * 512 + nf * 128], "p (k n) -> p k n", k=8)
            wu2_, wu2_v = wload(wu_b[:, :, t6 * 512:t6 * 512 + nf * 128], "p (k n) -> p k n", k=8)
            for f4 in range(nf):
                fc = t6 * 4 + f4
                bg_, tg_ = nbank()
                bu_, tu_ = nbank()
                for (wb_, wv_, bb, tt) in ((wg_, wg_v, bg_, tg_), (wu2_, wu2_v, bu_, tu_)):
                    for kc in range(8):
                        P.add("pe", lambda e, kc=kc, wv_=wv_, tt=tt, f4=f4: e.matmul(
                            tt[:], lhsT=wv_[:, kc, f4 * 128:(f4 + 1) * 128], rhs=h3_t[:, kc, 2:514],
                            start=(kc == 0), stop=(kc == 7)), [wb_] + h3[0:4], [bb])
                th_ = tha[fc % 2]
                s1_ = m1[fc % 2]
                P.add("act", lambda e, tg_=tg_, th_=th_: e.activation(out=th_[:], in_=tg_[:], func=AF.Tanh,
                                                                     scale=0.5), [bg_], [th_])
                P.add("dve", lambda e, tg_=tg_, th_=th_, s1_=s1_: e.scalar_tensor_tensor(
                    out=s1_[:], in0=th_[:], scalar=1.0, in1=tg_[:], op0=ALU.add, op1=ALU.mult),
                    [th_, bg_], [s1_])
                P.add("dve", lambda e, tu_=tu_, s1_=s1_, fc=fc: e.tensor_tensor(
                    out=AT[:, fc, :], in0=s1_[:], in1=tu_[:], op=ALU.mult), [s1_, bu_], [AT])

        for dh_ in range(2):
            banks = [nbank() for _ in range(4)]
            for t3 in range(3):
                f0 = t3 * 8
                nf = 8 if t3 < 2 else 6
                wd_, wd_v = wload(wd_b[:, f0:f0 + nf, dh_ * 512:(dh_ + 1) * 512], "p (f n) -> p f n", f=nf)
                for ts in range(4):
                    bo, to = banks[ts]
                    for f_ in range(nf):
                        fc = f0 + f_
                        P.add("pe", lambda e, f_=f_, fc=fc, ts=ts, to=to, wd_v=wd_v: e.matmul(
                            to[:], lhsT=AT[:, fc, ts * 128:(ts + 1) * 128], rhs=wd_v[:, f_, :],
                            start=(fc == 0), stop=(fc == 21)), [AT, wd_], [bo])
            for ts in range(4):
                bo, to = banks[ts]
                tr = tmpr[ts % 2]
                P.add("dve", lambda e, to=to, tr=tr, dh_=dh_: e.tensor_tensor(
                    out=tr[:], in0=to[:], in1=gtf[:, dh_ * 512:(dh_ + 1) * 512], op=ALU.mult),
                    [bo, gtf], [tr])
                P.add("dve", lambda e, ts=ts, tr=tr, dh_=dh_: e.tensor_tensor(
                    out=x3[ts][:, dh_ * 512:(dh_ + 1) * 512], in0=x3[ts][:, dh_ * 512:(dh_ + 1) * 512],
                    in1=tr[:], op=ALU.add), [x3[ts], tr], [x3[ts]])
        ssq, sx, sy, stt_ = stats3[2]
        for ts in range(4):
            P.add("act", lambda e, ts=ts: e.activation(out=sqj3[:], in_=x3[ts][:], func=AF.Square,
                                                       accum_out=ssq[:, ts:ts + 1]), [x3[ts]], [sqj3, ssq])
        P.add("dve", lambda e: e.tensor_scalar(out=sx[:, 0:4], in0=ssq[:, 0:4], scalar1=1.0 / D, scalar2=EPS,
                                               op0=ALU.mult, op1=ALU.add), [ssq], [sx])
        rsqrt("dve", sy, sy[:, 0:4], sx, sx[:, 0:4], stt_, stt_[:, 0:4])
        for ts in range(4):
            P.add("dve", lambda e, ts=ts: e.scalar_tensor_tensor(
                out=x3[ts][:], in0=x3[ts][:], scalar=sy[:, ts:ts + 1], in1=gfin[:], op0=ALU.mult, op1=ALU.mult),
                [x3[ts], sy, gfin], [x3[ts]])
            out_ops.append(dma("pool", y_v[(j * 4 + ts) * 128:(j * 4 + ts + 1) * 128, :], x3[ts][:],
                               reads=[x3[ts]], dsem=x3sem[ts]))

    opf = P.add("sp", None)
    opf.deps = list(out_ops)
    opf2 = P.add("pool", None)
    opf2.deps = list(out_ops)
    with nc.Block() as block:
        P.emit(block)
    es3.close()
    es.close()
    return nc, dbg


def _prep_inputs(inp):
    f32 = np.float32
    x = np.asarray(inp["x"], f32)
    c = np.asarray(inp["c"], f32)
    w_ada = np.ascontiguousarray(np.asarray(inp["w_ada"], f32)[0])
    b_ada = np.asarray(inp["b_ada"], f32)[0]
    shared = {
        "w_ada": w_ada,
        "bcol": np.ascontiguousarray(b_ada.reshape(48, 128).T),
        "brow": np.ascontiguousarray(np.stack([b_ada[2048:3072], b_ada[5120:6144]])),
        "gmix": np.ascontiguousarray(np.asarray(inp["g_mix"], f32)[0].reshape(8, 128).T),
        "gffn": np.ascontiguousarray(np.asarray(inp["g_ffn"], f32)[0].reshape(8, 128).T),
        "gfin": np.ascontiguousarray(np.asarray(inp["g_final"], f32).reshape(1, D)),
        "w_in": np.ascontiguousarray(np.asarray(inp["w_in"], f32)[0]),
        "cw": np.ascontiguousarray(
            np.asarray(inp["conv_w"], f32)[0].reshape(3, 4, 128).transpose(2, 1, 0).reshape(128, 12)),
        "w_out_a": np.ascontiguousarray(np.asarray(inp["w_out_a"], f32)[0]),
        "w_out_b": np.ascontiguousarray(np.asarray(inp["w_out_b"], f32)[0]),
        "w_out": np.ascontiguousarray(np.asarray(inp["w_out"], f32)[0]),
        "w_gate": np.ascontiguousarray(np.asarray(inp["w_gate"], f32)[0]),
        "w_up": np.ascontiguousarray(np.asarray(inp["w_up"], f32)[0]),
        "w_down": np.ascontiguousarray(np.asarray(inp["w_down"], f32)[0]),
        "lam": np.ascontiguousarray(np.stack([np.asarray(inp[k], f32)[0] for k in
                                              ("lambda_q1", "lambda_k1", "lambda_q2", "lambda_k2")])),
        "subg": np.ascontiguousarray(np.asarray(inp["subln_g"], f32)[0].reshape(128, 1)),
        "ident": np.eye(128, dtype=f32).astype(ml_dtypes.bfloat16),
        "diag": np.stack([np.eye(128, dtype=f32) * s for s in SLOPES]).astype(ml_dtypes.bfloat16),
    }
    maps = []
    for core in range(8):
        b, par = core // 2, core % 2
        tiles = [2 * j + 1 for j in range(NSLOT)] if par == 0 else [2 * j for j in range(NSLOT)]
        xb = x[b]
        xo = np.concatenate([xb[t * TQ:(t + 1) * TQ] for t in tiles], axis=0)
        xh = np.zeros((NSLOT, 128, D), f32)
        hmask = np.zeros((128, NSLOT), f32)
        for j, t in enumerate(tiles):
            if t > 0:
                xh[j, 0:2] = xb[t * TQ - 2:t * TQ]
                hmask[:, j] = 1.0
        qoff = 512 if par == 0 else 0
        kk = np.arange(1024)[:, None]
        qq = (np.arange(512) + qoff)[None, :]
        allowed = (kk // 64) <= (qq // 64)
        tm = np.where(allowed, np.where(kk > qq, -2.0 * (kk - qq), 0.0), MASKV).astype(f32)
        tm = tm.reshape(8, 128, 512)
        cb = np.zeros((128, sum(16 * (2 * jj + 2) for jj in range(NSLOT))), f32)
        idx = 0
        for j, t in enumerate(tiles):
            ref = t * TQ + 255
            nkt = 4 * (2 * j + 2)
            for h in range(4):
                for kt in range(nkt):
                    kpos = kt * 128 + np.arange(128)
                    cb[:, idx] = SLOPES[h] * (kpos - ref)
                    idx += 1
        assert idx == cb.shape[1]
        m = dict(shared)
        m.update({
            "xs": np.ascontiguousarray(xb[:S]), "xo": np.ascontiguousarray(xo), "xh": xh, "hm": hmask,
            "ccol": np.ascontiguousarray(c[b].reshape(8, 128).T),
            "tmask": tm.astype(ml_dtypes.bfloat16), "cbias": cb,
        })
        maps.append((m, tiles))
    return maps


def kernel(**inputs):
    maps = _prep_inputs(inputs)
    nc, _ = _build()
    res = run_bass_kernel_spmd(nc, [m for m, _ in maps], core_ids=list(range(8)))
    out = np.zeros((NB, S, D), np.float32)
    for core in range(8):
        y = res.results[core]["y"]
        b = core // 2
        for j, t in enumerate(maps[core][1]):
            out[b, t * TQ:(t + 1) * TQ] = y[j * TQ:(j + 1) * TQ]
    return out
```
